# Optimizing a Trainium2 kernel written in Bass

```python
import math
import jax, jax.numpy as jnp
from jax import lax
import numpy as np

D_MODEL = 1024
BATCH = 4
SEQ = 4096
DEPTH = 2

N_MIXERS = 2
EXPAND = 2
BRANCH_WIDTH = EXPAND * D_MODEL
MEM_HEADS = 4
MEM_HEAD_DIM = 128
MEM_WIDTH = MEM_HEADS * MEM_HEAD_DIM
MIX_WIDTH = BRANCH_WIDTH - MEM_WIDTH
MEM_TOKENS = 256
DA_QK_DIM = 64
DA_V_DIM = 2 * DA_QK_DIM
DA_HEADS = MIX_WIDTH // DA_V_DIM
Q_BLOCK = 128
ROPE_THETA = 10000.0
POOL_WINDOWS = (2, 4, 8, 16)
N_POOL_GROUPS = len(POOL_WINDOWS)
POOL_GROUP_WIDTH = MIX_WIDTH // N_POOL_GROUPS
N_ATTN_LAYERS = (DEPTH + 1) // 2
N_POOL_LAYERS = DEPTH // 2
ATTN_IN_WIDTH = 3 * MIX_WIDTH + MEM_WIDTH + BRANCH_WIDTH
POOL_IN_WIDTH = MIX_WIDTH + MEM_WIDTH + BRANCH_WIDTH
POS_OFFSET_MAX = 1024
EPS = 1e-6

kernel_name = 'hybrid_diffattn_multiscale_pool_memcross'


def rms_norm(x, g):
    xf = x.astype(jnp.float32)
    y = xf * lax.rsqrt(jnp.mean(xf * xf, axis=-1, keepdims=True) + EPS)
    return (y * g.astype(jnp.float32)).astype(x.dtype)


def rope_tables(positions, dim):
    inv = ROPE_THETA ** (-jnp.arange(0, dim, 2, dtype=jnp.float32) / dim)
    ang = positions.astype(jnp.float32)[..., None] * inv
    ang = jnp.concatenate([ang, ang], axis=-1)
    return jnp.cos(ang), jnp.sin(ang)


def apply_rotary(x, cos, sin):
    x1, x2 = jnp.split(x, 2, axis=-1)
    rot = jnp.concatenate([-x2, x1], axis=-1)
    return (x.astype(jnp.float32) * cos + rot.astype(jnp.float32) * sin).astype(x.dtype)


def lambda_init_fn(layer_idx):
    return 0.8 - 0.6 * math.exp(-0.3 * layer_idx)


def diff_attention(u, positions, lam, subln_g, lambda_init):
    B, S, _ = u.shape
    q, k, v = jnp.split(u, 3, axis=-1)
    q = q.reshape(B, S, DA_HEADS, 2, DA_QK_DIM)
    k = k.reshape(B, S, DA_HEADS, 2, DA_QK_DIM)
    v = v.reshape(B, S, DA_HEADS, DA_V_DIM)
    cos, sin = rope_tables(positions, DA_QK_DIM)
    cos = cos[:, :, None, None, :]
    sin = sin[:, :, None, None, :]
    q = apply_rotary(q, cos, sin)
    k = apply_rotary(k, cos, sin)
    lf = lam.astype(jnp.float32)
    lam_full = jnp.exp(jnp.sum(lf[0] * lf[1])) - jnp.exp(jnp.sum(lf[2] * lf[3])) + lambda_init
    nb = S // Q_BLOCK
    qb = q.reshape(B, nb, Q_BLOCK, DA_HEADS, 2, DA_QK_DIM).transpose(1, 0, 2, 3, 4, 5)
    kpos = jnp.arange(S)
    scale = DA_QK_DIM ** -0.5

    def block(args):
        q_blk, start = args
        s = jnp.einsum('bqhmd,bkhmd->bhmqk', q_blk, k).astype(jnp.float32) * scale
        qpos = start + jnp.arange(Q_BLOCK)
        mask = kpos[None, :] <= qpos[:, None]
        s = jnp.where(mask, s, -jnp.inf)
        p = jax.nn.softmax(s, axis=-1)
        w = p[:, :, 0] - lam_full * p[:, :, 1]
        return jnp.einsum('bhqk,bkhe->bqhe', w.astype(v.dtype), v)

    starts = jnp.arange(nb) * Q_BLOCK
    o = lax.map(block, (qb, starts))
    o = o.transpose(1, 0, 2, 3, 4).reshape(B, S, DA_HEADS, DA_V_DIM)
    o = rms_norm(o, subln_g) * (1.0 - lambda_init)
    return o.reshape(B, S, MIX_WIDTH)


def multiscale_pool(u, w_group, scale):
    B, S, _ = u.shape
    uf = u.astype(jnp.float32)
    groups = jnp.split(uf, N_POOL_GROUPS, axis=-1)
    t_count = jnp.arange(1, S + 1)
    outs = []
    for ug, win in zip(groups, POOL_WINDOWS):
        c = jnp.cumsum(ug, axis=1)
        lag = jnp.concatenate([jnp.zeros((B, win, ug.shape[-1]), jnp.float32), c[:, :S - win]], axis=1)
        cnt = jnp.minimum(t_count, win).astype(jnp.float32)[None, :, None]
        outs.append((c - lag) / cnt - ug)
    pooled = jnp.stack(outs, axis=2).astype(u.dtype)
    mixed = jnp.einsum('bsgc,gcd->bsgd', pooled, w_group)
    return mixed.reshape(B, S, MIX_WIDTH) * scale


def memory_attention(q_mem, mem_n, w_kv):
    B, S, _ = q_mem.shape
    kv = jnp.einsum('bmd,de->bme', mem_n, w_kv)
    k, v = jnp.split(kv, 2, axis=-1)
    q = q_mem.reshape(B, S, MEM_HEADS, MEM_HEAD_DIM)
    k = k.reshape(B, MEM_TOKENS, MEM_HEADS, MEM_HEAD_DIM)
    v = v.reshape(B, MEM_TOKENS, MEM_HEADS, MEM_HEAD_DIM)
    s = jnp.einsum('bshd,bmhd->bhsm', q, k).astype(jnp.float32) * (MEM_HEAD_DIM ** -0.5)
    p = jax.nn.softmax(s, axis=-1).astype(v.dtype)
    o = jnp.einsum('bhsm,bmhd->bshd', p, v)
    return o.reshape(B, S, MEM_WIDTH)


def setup_inputs(seed: int = 0) -> dict:
    key = jax.random.key(seed)
    ks = jax.random.split(key, 16)
    f32 = jnp.float32
    x = jax.random.normal(ks[0], (BATCH, SEQ, D_MODEL), f32)
    mem = jax.random.normal(ks[1], (BATCH, MEM_TOKENS, D_MODEL), f32)
    offsets = jax.random.randint(ks[2], (BATCH, 1), 0, POS_OFFSET_MAX, dtype=jnp.int32)
    positions = (offsets + jnp.arange(SEQ, dtype=jnp.int32)[None, :]).astype(jnp.int32)
    ln_g = 1.0 + 0.02 * jax.random.normal(ks[3], (DEPTH, D_MODEL), f32)
    attn_w_in = jax.random.normal(ks[4], (N_ATTN_LAYERS, D_MODEL, ATTN_IN_WIDTH), f32) * D_MODEL ** -0.5
    attn_lambda = 0.1 * jax.random.normal(ks[5], (N_ATTN_LAYERS, 4, DA_QK_DIM), f32)
    attn_subln_g = 1.0 + 0.02 * jax.random.normal(ks[6], (N_ATTN_LAYERS, DA_V_DIM), f32)
    pool_w_in = jax.random.normal(ks[7], (N_POOL_LAYERS, D_MODEL, POOL_IN_WIDTH), f32) * D_MODEL ** -0.5
    pool_w_group = jax.random.normal(ks[8], (N_POOL_LAYERS, N_POOL_GROUPS, POOL_GROUP_WIDTH, POOL_GROUP_WIDTH), f32) * POOL_GROUP_WIDTH ** -0.5
    pool_scale = 1.0 + 0.02 * jax.random.normal(ks[9], (N_POOL_LAYERS, MIX_WIDTH), f32)
    mem_norm_g = 1.0 + 0.02 * jax.random.normal(ks[10], (D_MODEL,), f32)
    mem_w_kv = jax.random.normal(ks[11], (DEPTH, D_MODEL, 2 * MEM_WIDTH), f32) * D_MODEL ** -0.5
    w_out = jax.random.normal(ks[12], (DEPTH, BRANCH_WIDTH, D_MODEL), f32) * BRANCH_WIDTH ** -0.5
    final_g = 1.0 + 0.02 * jax.random.normal(ks[13], (D_MODEL,), f32)
    return {'x': x, 'mem': mem, 'positions': positions, 'ln_g': ln_g,
            'attn_w_in': attn_w_in, 'attn_lambda': attn_lambda, 'attn_subln_g': attn_subln_g,
            'pool_w_in': pool_w_in, 'pool_w_group': pool_w_group, 'pool_scale': pool_scale,
            'mem_norm_g': mem_norm_g, 'mem_w_kv': mem_w_kv, 'w_out': w_out, 'final_g': final_g}


def reference(x, mem, positions, ln_g, attn_w_in, attn_lambda, attn_subln_g,
              pool_w_in, pool_w_group, pool_scale, mem_norm_g, mem_w_kv, w_out, final_g):
    h = x
    mem_n = rms_norm(mem, mem_norm_g)
    for i in range(DEPTH):
        hn = rms_norm(h, ln_g[i])
        j = i // N_MIXERS
        if i % N_MIXERS == 0:
            proj = jnp.einsum('bsd,de->bse', hn, attn_w_in[j])
            u = proj[..., :3 * MIX_WIDTH]
            q_mem = proj[..., 3 * MIX_WIDTH:3 * MIX_WIDTH + MEM_WIDTH]
            gate = proj[..., 3 * MIX_WIDTH + MEM_WIDTH:]
            y = diff_attention(u, positions, attn_lambda[j], attn_subln_g[j], lambda_init_fn(i))
        else:
            proj = jnp.einsum('bsd,de->bse', hn, pool_w_in[j])
            u = proj[..., :MIX_WIDTH]
            q_mem = proj[..., MIX_WIDTH:MIX_WIDTH + MEM_WIDTH]
            gate = proj[..., MIX_WIDTH + MEM_WIDTH:]
            y = multiscale_pool(u, pool_w_group[j], pool_scale[j])
        m = memory_attention(q_mem, mem_n, mem_w_kv[i])
        z = jnp.concatenate([y, m], axis=-1) * jax.nn.silu(gate)
        h = h + jnp.einsum('bse,ed->bsd', z, w_out[i])
    return rms_norm(h, final_g)
```

```python
import math
from contextlib import ExitStack

import numpy as np
import concourse.bass as bass
import concourse.mybir as mybir
from concourse.bass_utils import run_bass_kernel_spmd

F32 = mybir.dt.float32
BF16 = mybir.dt.bfloat16
I32 = mybir.dt.int32
AF = mybir.ActivationFunctionType
ALU = mybir.AluOpType

NDMASEM = 14
DMA_SLOTS = {"sp": list(range(0, 8)), "pool": list(range(8, 14)), "act": list(range(0, 8))}

D = 1024
NB_OWN = 17
NB_CTX = 15
TOK_OWN = NB_OWN * 128
TOK_CTX = NB_CTX * 128
TOK_ALL = 4096
NH = 12
EPS = 1e-6
MIX = 1536
NEG = -30000.0
TWO_PI = 2.0 * math.pi
CW1 = 6.28125
CW2 = TWO_PI - CW1
OWN_GROUPS = [(0, 512), (512, 512), (1024, 512), (1536, 512), (2048, 128)]
CTX_GROUPS = [(2176, 512), (2688, 512), (3200, 512), (3712, 384)]
QGROUPS = [(0, 4, 7, 0), (4, 4, 7, 0), (8, 4, 15, 1), (12, 4, 15, 1), (16, 1, 15, 1)]
POOL_WINDOWS = (2, 4, 8, 16)
POOL_CHUNKS = [(0, 1024), (1024, 1152)]


class T:
    __slots__ = ("name", "w", "r", "excl")

    def __init__(self, name="", excl=False):
        self.name = name
        self.w = None
        self.r = {}
        self.excl = excl


class Op:
    __slots__ = ("eng", "fn", "deps", "veng", "idx", "signal", "count", "is_dma", "dval")


class Prog:
    ENGS = ("pe", "act", "dve", "pool", "sp")

    def __init__(self):
        self.streams = {e: [] for e in self.ENGS}
        self.ndma = {"sp": 0, "pool": 0, "act": 0}
        self.dma_cnt = [0] * NDMASEM
        self.guard = None

    def add(self, eng, fn, reads=(), writes=(), dma=False):
        o = Op()
        o.eng = eng
        o.fn = fn
        o.is_dma = dma
        o.signal = False
        o.count = 0
        o.dval = 0
        reads = list(reads)
        if self.guard is not None:
            reads.append(self.guard)
        deps = {}
        for t in reads:
            if t.w is not None:
                deps[id(t.w)] = t.w
            if t.excl:
                for p in t.r.values():
                    if p.eng != eng:
                        deps[id(p)] = p
        for t in writes:
            if t.w is not None:
                deps[id(t.w)] = t.w
            for p in t.r.values():
                deps[id(p)] = p
        o.deps = list(deps.values())
        if dma:
            slots = DMA_SLOTS[eng]
            slot = slots[self.ndma[eng] % len(slots)]
            self.ndma[eng] += 1
            self.dma_cnt[slot] += 1
            o.veng = ("dma", slot)
            o.dval = 16 * self.dma_cnt[slot]
        else:
            o.veng = eng
        for t in reads:
            t.r[o.veng] = o
        for t in writes:
            t.w = o
            t.r = {}
        o.idx = len(self.streams[eng])
        self.streams[eng].append(o)
        return o

    def emit(self, nc):
        for st in self.streams.values():
            for o in st:
                for d in o.deps:
                    d.signal = True
        for e, st in self.streams.items():
            c = 0
            for o in st:
                if o.is_dma:
                    continue
                if o.signal:
                    c += 1
                o.count = c
        with ExitStack() as es:
            sems = {e: es.enter_context(nc.semaphore("s_" + e)) for e in self.ENGS}
            dsems = [es.enter_context(nc.semaphore("d%d" % i)) for i in range(NDMASEM)]
            block = es.enter_context(nc.Block())
            streams = self.streams
            dma_cnt = self.dma_cnt

            def run(e, eng):
                waited = {}
                for o in streams[e]:
                    need = {}
                    for d in o.deps:
                        if d.is_dma:
                            key = d.veng
                            val = d.dval
                        else:
                            key = d.eng
                            val = d.count
                            if d.eng == e:
                                if e == "pe":
                                    continue
                                if o.idx - d.idx > 3:
                                    continue
                        if need.get(key, 0) < val:
                            need[key] = val
                    if o.is_dma and o.dval > 16:
                        if need.get(o.veng, 0) < o.dval - 16:
                            need[o.veng] = o.dval - 16
                    for key, val in need.items():
                        if waited.get(key, 0) >= val:
                            continue
                        sem = dsems[key[1]] if isinstance(key, tuple) else sems[key]
                        eng.wait_ge(sem, val)
                        waited[key] = val
                    ins = o.fn(eng)
                    if o.is_dma:
                        ins.then_inc(dsems[o.veng[1]], 16)
                    elif o.signal:
                        ins.then_inc(sems[e], 1)
                if e == "sp":
                    for slot in range(NDMASEM):
                        if dma_cnt[slot] > 0:
                            eng.wait_ge(dsems[slot], 16 * dma_cnt[slot])

            @block.tensor
            def _(eng):
                run("pe", eng)

            @block.scalar
            def _(eng):
                run("act", eng)

            @block.vector
            def _(eng):
                run("dve", eng)

            @block.gpsimd
            def _(eng):
                run("pool", eng)

            @block.sync
            def _(eng):
                run("sp", eng)


class Builder:
    def __init__(self, do_l0=True, do_l1=True, lambda_init0=0.2):
        self.do_l0 = do_l0
        self.do_l1 = do_l1
        self.lambda_init0 = lambda_init0
        self.p = Prog()
        self.epi_loaded = set()
        self.nc = bass.Bass("TRN2", target_bir_lowering=False)

    def mm(self, out, lhsT, rhs, start, stop, reads, writes, skip=False):
        self.p.add("pe", lambda e: e.matmul(out, lhsT=lhsT, rhs=rhs, start=start, stop=stop,
                                            skip_group_check=skip), reads, writes)

    def tr(self, out, in_, reads, writes):
        ident = self.identb
        self.p.add("pe", lambda e: e.transpose(out=out, in_=in_, identity=ident[:]), reads, writes)

    def act(self, out, in_, func, reads, writes, bias=None, scale=None, accum=None):
        kw = {}
        if bias is not None:
            kw["bias"] = bias
        if scale is not None:
            kw["scale"] = scale
        if accum is not None:
            kw["accum_out"] = accum
        self.p.add("act", lambda e: e.activation(out=out, in_=in_, func=func, **kw), reads, writes)

    def ts(self, eng, out, in0, s1, op0, reads, writes, s2=None, op1=None):
        if op1 is None:
            self.p.add(eng, lambda e: e.tensor_scalar(out=out, in0=in0, scalar1=s1, scalar2=None, op0=op0),
                       reads, writes)
        else:
            self.p.add(eng, lambda e: e.tensor_scalar(out=out, in0=in0, scalar1=s1, scalar2=s2, op0=op0, op1=op1),
                       reads, writes)

    def tt(self, eng, out, in0, in1, op, reads, writes):
        self.p.add(eng, lambda e: e.tensor_tensor(out=out, in0=in0, in1=in1, op=op), reads, writes)

    def stt(self, out, in0, scalar, in1, op0, op1, reads, writes):
        self.p.add("dve", lambda e: e.scalar_tensor_tensor(out=out, in0=in0, scalar=scalar, in1=in1,
                                                           op0=op0, op1=op1), reads, writes)

    def cp(self, eng, out, in_, reads, writes):
        if eng == "act":
            self.p.add("act", lambda e: e.copy(out=out, in_=in_), reads, writes)
        else:
            self.p.add(eng, lambda e: e.tensor_copy(out=out, in_=in_), reads, writes)

    def memset(self, eng, ap, val, writes):
        self.p.add(eng, lambda e: e.memset(ap, val), (), writes)

    def recip(self, out, in_, reads, writes):
        self.p.add("dve", lambda e: e.reciprocal(out=out, in_=in_), reads, writes)

    def dma(self, out, in_, reads, writes, q="sp"):
        self.p.add(q, lambda e: e.dma_start(out=out, in_=in_), reads, writes, dma=True)

    def bc(self, ap):
        b = ap.partition_broadcast(128)
        return b.rearrange("p o n -> p (o n)")

    def load_epi_weights(self, L, which):
        g = self.p.guard
        self.p.guard = None
        for w in which:
            if (L, w) in self.epi_loaded:
                continue
            self.epi_loaded.add((L, w))
            if w == "gt":
                ctxT = [self.T_hn[b] for b in range(17, 32)]
                self.dma(self.regB[:, :].rearrange("p (c n) -> p c n", c=16),
                         self.w_gt[L].rearrange("p (c n) -> p c n", c=16), ctxT, [self.T_regB_w] + ctxT, q="pool")
            elif w == "qm":
                self.dma(self.wqm[:], self.w_qm[L].rearrange("p (c n) -> p c n", c=8), [], [self.T_wqm], q="pool")
            else:
                self.dma(self.wo, self.w_o[L].rearrange("p (e d) -> p e d", e=16), [], [self.T_cs], q="pool")
        self.p.guard = g

    def fence(self):
        g = self.p.guard
        self.p.guard = None
        fz = self.fz
        self.p.add("pool", lambda e: e.memset(fz[:], 0.0), (), [g])
        self.p.guard = g

    def build(self):
        nc = self.nc
        with ExitStack() as es:
            self.es = es
            self._declare(es)
            self._setup()
            self.fence()
            if self.do_l0:
                self.load_epi_weights(0, ("qm",))
                self._l0_mixer()
                self.fence()
                self._epilogue(0)
                if self.do_l1:
                    self.load_epi_weights(1, ("gt", "qm", "wo"))
                self.fence()
            if self.do_l1:
                self._l1_mixer()
                self.fence()
                self._epilogue(1)
            self.p.emit(nc)
        return nc

    def _declare(self, es):
        nc = self.nc
        di = lambda n, s, d: nc.dram_tensor(n, s, d, kind="ExternalInput").ap()
        self.xl = di("xl", [TOK_ALL if self.do_l0 else TOK_OWN, D], F32)
        self.posl = di("posl", [1, TOK_ALL], I32)
        self.meml = di("meml", [256, D], F32)
        self.c_ident = di("c_ident", [128, 128], F32)
        self.c_rmat = di("c_rmat", [128, 128], F32)
        self.c_tri = di("c_tri", [128, 128], F32)
        self.c_invf = di("c_invf", [128, 1], F32)
        self.c_bias = di("c_bias", [128, 22], F32)
        self.v_lng = di("v_lng", [2, D], F32)
        self.v_memg = di("v_memg", [1, D], F32)
        self.v_fing = di("v_fing", [1, D], F32)
        self.v_subg = di("v_subg", [1, 128], F32)
        self.v_pscale = di("v_pscale", [1, MIX], F32)
        self.v_lam = di("v_lam", [1, 256], F32)
        self.w_h0 = di("w_h0", [NH, 128, 8 * 384], F32)
        self.w_qm = di("w_qm", [2, 128, 8 * 512], F32)
        self.w_gt = di("w_gt", [2, 128, 8 * 2048], F32)
        self.w_o = di("w_o", [2, 128, 16 * 1024], F32)
        self.w_kv = di("w_kv", [2, 128, 8 * 1024], F32)
        self.w_u = di("w_u", [4, 128, 8 * 384], F32)
        self.w_gp = di("w_gp", [4, 128, 3 * 384], F32)
        self.out = nc.dram_tensor("out", [TOK_OWN, D], F32, kind="ExternalOutput").ap()
        self.yscr = nc.dram_tensor("yscr", [TOK_OWN, MIX], BF16).ap()
        if self.do_l0 and self.do_l1:
            self.h1scr = nc.dram_tensor("h1scr", [TOK_OWN, D], F32).ap()
        else:
            self.h1scr = None

        sb = lambda n, s, d: es.enter_context(nc.sbuf_tensor(n, s, d))
        self.hnT = sb("hnT", [128, 8, TOK_OWN], BF16)
        self.regB = sb("regB", [128, 16384], BF16)
        self.regC = sb("regC", [128, 8192], F32)
        self.regD = sb("regD", [128, 13312], F32)
        self.wslot = [sb("wslot%d" % i, [128, 8, 384], BF16) for i in range(2)]
        self.wqm = sb("wqm", [128, 8, 512], BF16)
        self.kmT = [sb("kmT%d" % i, [128, 4, 256], BF16) for i in range(2)]
        self.vm = [sb("vm%d" % i, [128, 2, 4, 132], BF16) for i in range(2)]
        self.gbuf = sb("gbuf", [128, D], F32)
        self.subg = sb("subg", [128, 128], F32)
        self.identb = sb("identb", [128, 128], BF16)
        self.rmatb = sb("rmatb", [128, 128], BF16)
        self.trib = sb("trib", [128, 128], BF16)
        self.cst = sb("cst", [128, 32], F32)
        self.fz = sb("fz", [128, 2], F32)
        self.ps = [es.enter_context(nc.psum_tensor("ps%d" % i, [128, 512], F32)) for i in range(8)]

        self.hnT_ctx = self.regB[:, 0:8 * TOK_CTX].rearrange("p (c t) -> p c t", c=8)
        self.wgt = self.regB[:, :].rearrange("p (c n) -> p c n", c=8)
        self.cos = self.regC[:, 0:4096]
        self.sin = self.regC[:, 4096:8192]
        self.wo = self.regC[:, :].bitcast(BF16).rearrange("p (e d) -> p e d", e=16)

        self.T_hn = [T("hn%d" % i) for i in range(32)]
        self.T_regB_w = T("wgt")
        self.T_cs = T("cossin/wo")
        self.T_ws = [T("ws0"), T("ws1")]
        self.T_wqm = T("wqm")
        self.T_km = [T(), T()]
        self.T_vm = [T(), T()]
        self.T_g = T("gbuf")
        self.T_const = T("const")
        self.T_ps = [T("ps%d" % i, excl=True) for i in range(8)]
        self.T_y = [T("yscr%d" % i) for i in range(NB_OWN)]
        self.T_h1 = [T("h1scr%d" % i) for i in range(NB_OWN)]
        self.T_out = T("out")
        self.p.guard = T("guard")

    def carve(self, off_bytes, shape, dtype):
        free = list(shape[1:])
        n = int(np.prod(free))
        nbytes = n * (4 if dtype in (F32, I32) else 2)
        assert off_bytes % 4 == 0 and off_bytes + nbytes <= 13312 * 4, (off_bytes, nbytes)
        a = self.regD[:, off_bytes // 4:(off_bytes + nbytes + 3) // 4]
        if dtype != F32:
            a = a.bitcast(dtype)
        a = a[:, 0:n]
        if len(free) == 2:
            return a.rearrange("p (a b) -> p a b", a=free[0])
        if len(free) == 3:
            return a.rearrange("p (a b c) -> p a b c", a=free[0], b=free[1])
        return a

    def hn_ap(self, c, t0, n):
        if t0 < TOK_OWN:
            assert t0 + n <= TOK_OWN
            return self.hnT[:, c, t0:t0 + n]
        return self.hnT_ctx[:, c, t0 - TOK_OWN:t0 - TOK_OWN + n]

    def hn_T(self, t0, n):
        return [self.T_hn[b] for b in range(t0 // 128, (t0 + n) // 128)]

    def norm_block(self, src, Tsrc, junk, Tjunk, st, Tst, col, g, Tg, dst_f32=None, Tdst=None,
                   hnb=None, Thnb=None):
        eps_ap = self.cst[:, 1:2]
        ss = st[:, col:col + 1]
        rs = st[:, col + 1:col + 2]
        self.act(junk, src, AF.Square, [Tsrc], [Tjunk, Tst], accum=ss)
        self.act(rs, ss, AF.Ln, [Tst, self.T_const], [Tst], bias=eps_ap, scale=1.0 / D)
        self.act(rs, rs, AF.Exp, [Tst], [Tst], scale=-0.5)
        if hnb is not None:
            self.stt(hnb, src, rs, g, ALU.mult, ALU.mult, [Tsrc, Tst, Tg], [Thnb])
        if dst_f32 is not None:
            self.stt(dst_f32, src, rs, g, ALU.mult, ALU.mult, [Tsrc, Tst, Tg], [Tdst])

    def transpose_block(self, hnb, Thnb, blk, psi):
        psb = self.ps[psi][:].bitcast(BF16)
        for c in range(8):
            self.tr(psb[:, c * 128:(c + 1) * 128], hnb[:, c * 128:(c + 1) * 128],
                    [Thnb, self.T_const], [self.T_ps[psi]])
        if blk < NB_OWN:
            dst = self.hnT[:, :, blk * 128:(blk + 1) * 128]
        else:
            b = blk - NB_OWN
            dst = self.hnT_ctx[:, :, b * 128:(b + 1) * 128]
        src = psb.rearrange("p (c t) -> p c t", c=8)
        eng = "dve" if blk % 2 == 0 else "act"
        self.cp(eng, dst, src, [self.T_ps[psi]], [self.T_hn[blk]])

    def _setup(self):
        p = self.p
        Tc = self.T_const
        tmpf = self.carve(0, [128, 128], F32)
        Ttmp = T()
        for src, dst in ((self.c_ident, self.identb), (self.c_rmat, self.rmatb), (self.c_tri, self.trib)):
            self.dma(tmpf[:, :], src, [], [Ttmp])
            self.cp("dve", dst[:], tmpf[:, :], [Ttmp], [Tc])
        self.memset("dve", self.cst[:, :], 0.0, [Tc])
        self.memset("dve", self.cst[:, 1:2], EPS, [Tc])
        self.dma(self.cst[:, 0:1], self.c_invf, [], [Tc])
        self.dma(self.cst[:, 8:30], self.c_bias, [], [Tc])
        lamt = self.carve(1024, [128, 256], F32)
        lt2 = self.carve(2048, [128, 128], F32)
        Tl = T()
        self.dma(lamt[:, :], self.bc(self.v_lam), [], [Tl])
        self.tt("dve", lt2[:, 0:64], lamt[:, 0:64], lamt[:, 64:128], ALU.mult, [Tl], [Tl])
        self.tt("dve", lt2[:, 64:128], lamt[:, 128:192], lamt[:, 192:256], ALU.mult, [Tl], [Tl])
        st0 = self.carve(3072, [128, 16], F32)
        Tst0 = T()
        self.act(lamt[:, 0:64], lt2[:, 0:64], AF.Copy, [Tl], [Tl, Tst0], accum=st0[:, 0:1])
        self.act(lamt[:, 64:128], lt2[:, 64:128], AF.Copy, [Tl], [Tl, Tst0], accum=st0[:, 1:2])
        self.act(st0[:, 2:4], st0[:, 0:2], AF.Exp, [Tst0], [Tst0])
        self.tt("dve", st0[:, 4:5], st0[:, 3:4], st0[:, 2:3], ALU.subtract, [Tst0], [Tst0])
        self.ts("dve", self.cst[:, 2:3], st0[:, 4:5], -self.lambda_init0, ALU.add, [Tst0], [Tc])
        self.dma(self.subg[:], self.bc(self.v_subg), [], [Tc])
        self.ts("dve", self.subg[:], self.subg[:], 1.0 - self.lambda_init0, ALU.mult, [Tc], [Tc])

        xst = [self.carve(4096 + i * 4096, [128, D], F32) for i in range(2)]
        Txst = [T(), T()]
        hnb = [self.carve(12288 + i * 2048, [128, D], BF16) for i in range(2)]
        Thnb = [T(), T()]
        junk = self.carve(16384, [128, D], F32)
        Tjunk = T()
        st = self.carve(20480, [128, 80], F32)
        Tst = T()
        memT = self.carve(20992, [128, 8, 256], BF16)
        Tmem = T()
        wkv = self.carve(25088, [128, 8, 1024], BF16)
        Twkv = T()
        self.dma(self.gbuf[:], self.bc(self.v_memg), [], [self.T_g])
        for mb in range(2):
            i = mb % 2
            self.dma(xst[i][:, :], self.meml[mb * 128:(mb + 1) * 128, :], [], [Txst[i]])
            self.norm_block(xst[i][:, :], Txst[i], junk[:, :], Tjunk, st, Tst, 2 * mb, self.gbuf[:], self.T_g,
                            hnb=hnb[i][:, :], Thnb=Thnb[i])
            psb = self.ps[mb][:].bitcast(BF16)
            for c in range(8):
                self.tr(psb[:, c * 128:(c + 1) * 128], hnb[i][:, c * 128:(c + 1) * 128], [Thnb[i], Tc],
                        [self.T_ps[mb]])
            self.cp("dve", memT[:, :, mb * 128:(mb + 1) * 128], psb.rearrange("p (c t) -> p c t", c=8),
                    [self.T_ps[mb]], [Tmem])
        for L in range(2):
            if (L == 0 and not self.do_l0) or (L == 1 and not self.do_l1):
                continue
            self.dma(wkv, self.w_kv[L].rearrange("p (c n) -> p c n", c=8), [], [Twkv], q="pool")
            for hd in range(4):
                pi = 2 + (hd % 2)
                for c in range(8):
                    self.mm(self.ps[pi][:, 0:256], wkv[:, c, hd * 128:(hd + 1) * 128], memT[:, c, :],
                            c == 0, c == 7, [Twkv, Tmem], [self.T_ps[pi]])
                self.cp("act", self.kmT[L][:, hd, :], self.ps[pi][:, 0:256], [self.T_ps[pi]], [self.T_km[L]])
            self.memset("pool", self.vm[L][:, :, :, 128:129], 1.0, [self.T_vm[L]])
            for mc in range(2):
                pi = 4 + mc
                for c in range(8):
                    self.mm(self.ps[pi][:, :], memT[:, c, mc * 128:(mc + 1) * 128], wkv[:, c, 512:1024],
                            c == 0, c == 7, [Twkv, Tmem], [self.T_ps[pi]])
                self.cp("dve", self.vm[L][:, mc, :, 0:128], self.ps[pi][:, :].rearrange("p (h e) -> p h e", h=4),
                        [self.T_ps[pi]], [self.T_vm[L]])

        if not self.do_l0:
            self.dma(self.gbuf[:], self.bc(self.v_lng[1:2, :]), [], [self.T_g])
            for blk in range(NB_OWN):
                i = blk % 2
                self.dma(xst[i][:, :], self.xl[blk * 128:(blk + 1) * 128, :], [], [Txst[i]])
                self.norm_block(xst[i][:, :], Txst[i], junk[:, :], Tjunk, st, Tst, 4 + 2 * (blk % 8),
                                self.gbuf[:], self.T_g, hnb=hnb[i][:, :], Thnb=Thnb[i])
                self.transpose_block(hnb[i], Thnb[i], blk, 6 + i)
            return

        posi = self.carve(41472, [128, 512], I32)
        Tpi = T()
        posf = self.carve(43520, [128, 512], F32)
        ang = self.carve(45568, [128, 512], F32)
        kf = self.carve(47616, [128, 512], F32)
        ki = self.carve(49664, [128, 384], I32)
        Trope = T()
        invf = self.cst[:, 0:1]
        self.memset("dve", self.cst[:, 3:4], math.pi / 2, [Tc])
        def rope_chunk(t0):
            n = min(384, TOK_ALL - t0)
            self.dma(posi[:, 0:n], self.bc(self.posl[:, t0:t0 + n]), [], [Tpi])
            self.cp("dve", posf[:, 0:n], posi[:, 0:n], [Tpi], [Trope])
            for which, dst in ((0, self.sin), (1, self.cos)):
                if which == 0:
                    self.ts("dve", ang[:, 0:n], posf[:, 0:n], invf, ALU.mult, [Trope, Tc], [Trope])
                else:
                    self.ts("dve", ang[:, 0:n], posf[:, 0:n], invf, ALU.mult, [Trope, Tc], [Trope],
                            s2=self.cst[:, 3:4], op1=ALU.add)
                self.ts("dve", ki[:, 0:n], ang[:, 0:n], 1.0 / TWO_PI, ALU.mult, [Trope], [Trope])
                self.cp("dve", kf[:, 0:n], ki[:, 0:n], [Trope], [Trope])
                self.stt(ang[:, 0:n], kf[:, 0:n], -CW1, ang[:, 0:n], ALU.mult, ALU.add, [Trope], [Trope])
                self.stt(ang[:, 0:n], kf[:, 0:n], -CW2, ang[:, 0:n], ALU.mult, ALU.add, [Trope], [Trope])
                self.ts("dve", ang[:, 0:n], ang[:, 0:n], -3.1415925, ALU.max, [Trope], [Trope],
                        s2=3.1415925, op1=ALU.min)
                self.act(dst[:, t0:t0 + n], ang[:, 0:n], AF.Sin, [Trope], [self.T_cs])

        self.dma(self.gbuf[:], self.bc(self.v_lng[0:1, :]), [Tmem], [self.T_g])
        chunks = list(range(0, TOK_ALL, 384))
        nci = 0
        for blk in range(32):
            if blk % 3 == 0 and nci < len(chunks):
                rope_chunk(chunks[nci])
                nci += 1
            i = blk % 2
            self.dma(xst[i][:, :], self.xl[blk * 128:(blk + 1) * 128, :], [], [Txst[i]])
            self.norm_block(xst[i][:, :], Txst[i], junk[:, :], Tjunk, st, Tst, 4 + 2 * (blk % 8),
                            self.gbuf[:], self.T_g, hnb=hnb[i][:, :], Thnb=Thnb[i])
            self.transpose_block(hnb[i], Thnb[i], blk, 6 + i)
        while nci < len(chunks):
            rope_chunk(chunks[nci])
            nci += 1

    def _l0_mixer(self):
        Tc = self.T_const
        kTz = self.carve(0, [128, 2, TOK_ALL], BF16)
        qT = self.carve(16384, [128, TOK_OWN], BF16)
        vh = self.carve(20736, [128, 32, 132], BF16)
        xb = [self.carve(29184 + i * 1024, [128, 512], BF16) for i in range(2)]
        t1 = [self.carve(31232 + i * 2048, [128, 512], F32) for i in range(2)]
        t2 = [self.carve(35328 + i * 2048, [128, 512], F32) for i in range(2)]
        NPT = 6
        pt = [self.carve(39424 + i * 1024, [128, 512], BF16) for i in range(NPT)]
        osb = self.carve(45568, [128, 3, 396], F32)
        yst = [self.carve(50320 + i * 1024, [128, 4, 128], BF16) for i in range(2)]
        stt_ = self.carve(52368, [128, 64], F32)
        yq = self.carve(52624, [128, 128], F32)
        T_kT = [T() for _ in range(9)]
        T_qT = [T() for _ in range(5)]
        T_vh = [T() for _ in range(8)]
        T_xb = [T(), T()]
        T_t1 = [T(), T()]
        T_t2 = [T(), T()]
        T_pt = [T() for _ in range(NPT)]
        T_osb = T()
        T_yst = [T(), T()]
        T_st = T()
        T_yq = T()
        T_vone = T()

        self.memset("pool", kTz[64:128, 0, :], 0.0, [T_vone])
        self.memset("pool", kTz[0:64, 1, :], 0.0, [T_vone])
        self.memset("pool", vh[:, :, 128:129], 1.0, [T_vone])

        def load_w(h):
            s = h % 2
            self.dma(self.wslot[s][:], self.w_h0[h].rearrange("p (c n) -> p c n", c=8), [], [self.T_ws[s]], q="pool")

        load_w(0)
        allg = OWN_GROUPS + CTX_GROUPS
        LA = 4
        NS = 5
        ptr = 0
        obank = lambda a: 5 + a // 3
        oacc = lambda a: self.ps[obank(a)][:, (a % 3) * 132:(a % 3) * 132 + 129]

        def kblock_T(kb):
            t0 = kb * 128
            for gi_, (g0, gn) in enumerate(allg):
                if g0 <= t0 < g0 + gn:
                    return [T_kT[gi_], T_vh[kb // 4]]
            raise AssertionError

        for h in range(NH):
            s = h % 2
            W = self.wslot[s]
            Tw = self.T_ws[s]
            if h + 1 < NH:
                load_w(h + 1)
            jobs = [("q", g) for g in range(len(OWN_GROUPS))] + [("k", g) for g in range(9)]

            def front(j):
                kind, g = jobs[j]
                t0, n = allg[g]
                col0 = 0 if kind == "q" else 128
                i = j % 2
                pa = i
                for c in range(8):
                    self.mm(self.ps[pa][:, 0:n], W[:, c, col0:col0 + 128], self.hn_ap(c, t0, n),
                            c == 0, c == 7, [Tw] + self.hn_T(t0, n), [self.T_ps[pa]])
                self.cp("act", xb[i][:, 0:n], self.ps[pa][:, 0:n], [self.T_ps[pa]], [T_xb[i]])

            def back(j):
                kind, g = jobs[j]
                t0, n = allg[g]
                i = j % 2
                pa, pb = i, 2 + i
                self.mm(self.ps[pb][:, 0:n], self.rmatb[:], xb[i][:, 0:n], True, True, [Tc, T_xb[i]],
                        [self.T_ps[pb]])
                self.tt("dve", t1[i][:, 0:n], self.ps[pa][:, 0:n], self.cos[:, t0:t0 + n], ALU.mult,
                        [self.T_ps[pa], self.T_cs], [T_t1[i]])
                self.tt("dve", t2[i][:, 0:n], self.ps[pb][:, 0:n], self.sin[:, t0:t0 + n], ALU.mult,
                        [self.T_ps[pb], self.T_cs], [T_t2[i]])
                if kind == "q":
                    self.tt("pool", qT[:, t0:t0 + n], t1[i][:, 0:n], t2[i][:, 0:n], ALU.add,
                            [T_t1[i], T_t2[i]], [T_qT[g]])
                else:
                    self.tt("pool", kTz[0:64, 0, t0:t0 + n], t1[i][0:64, 0:n], t2[i][0:64, 0:n], ALU.add,
                            [T_t1[i], T_t2[i], T_vone], [T_kT[g]])
                    self.tt("pool", kTz[64:128, 1, t0:t0 + n], t1[i][64:128, 0:n], t2[i][64:128, 0:n], ALU.add,
                            [T_t1[i], T_t2[i], T_vone], [T_kT[g]])

            front(0)
            for j in range(len(jobs)):
                if j + 1 < len(jobs):
                    front(j + 1)
                back(j)
            for vb in range(8):
                pi = 4 + (vb % 2)
                for j in range(4):
                    blk = vb * 4 + j
                    for c in range(8):
                        self.mm(self.ps[pi][:, j * 128:(j + 1) * 128], self.hn_ap(c, blk * 128, 128),
                                W[:, c, 256:384], c == 0, c == 7, [Tw, self.T_hn[blk]], [self.T_ps[pi]], skip=True)
                eng = "act" if vb % 2 == 0 else "dve"
                self.cp(eng, vh[:, vb * 4:(vb + 1) * 4, 0:128],
                        self.ps[pi][:, :].rearrange("p (j e) -> p j e", j=4), [self.T_ps[pi], T_vone], [T_vh[vb]])
            if h == NH - 1:
                self.load_epi_weights(0, ("gt", "wo"))

            steps = []
            for gi, (q0, nq, nctx, btab) in enumerate(QGROUPS):
                klist = []
                for sl in range(nctx):
                    bcol = 8 + (sl if btab == 0 else 7 + sl)
                    klist.append((NB_OWN + sl, self.cst[:, bcol:bcol + 1], None, 0))
                for m in range(q0 + nq):
                    if m < q0:
                        klist.append((m, None, None, 0))
                    else:
                        klist.append((m, None, (m - q0) * 128, (m - q0) * 128))
                nk = len(klist)
                for ki_, (kb, bias_ap, dcol, c0) in enumerate(klist):
                    for mp in range(2):
                        steps.append(dict(gi=gi, ki=ki_, nk=nk, kb=kb, bias=bias_ap, dcol=dcol, c0=c0, mp=mp,
                                          last=(ki_ == nk - 1 and mp == 1)))
            started = {}

            def s_front(idx):
                nonlocal ptr
                st_ = steps[idx]
                q0, nq, nctx, btab = QGROUPS[st_["gi"]]
                ncols = nq * 128
                c0, kb, mp = st_["c0"], st_["kb"], st_["mp"]
                qg_T = []
                for (g0, gn), tq in zip(OWN_GROUPS, T_qT):
                    if g0 < (q0 + nq) * 128 and q0 * 128 < g0 + gn:
                        qg_T.append(tq)
                kT_T = kblock_T(kb)
                si = idx % NS
                pti = ptr % NPT
                ptr += 1
                st_["pti"] = pti
                self.mm(self.ps[si][:, c0:ncols], kTz[:, mp, kb * 128:(kb + 1) * 128],
                        qT[:, q0 * 128 + c0:q0 * 128 + ncols], True, True,
                        [kT_T[0], T_vone] + qg_T, [self.T_ps[si]])
                if st_["bias"] is not None:
                    self.act(pt[pti][:, c0:ncols], self.ps[si][:, c0:ncols], AF.Exp,
                             [self.T_ps[si], Tc], [T_pt[pti]], bias=st_["bias"], scale=0.125)
                else:
                    self.act(pt[pti][:, c0:ncols], self.ps[si][:, c0:ncols], AF.Exp,
                             [self.T_ps[si]], [T_pt[pti]], scale=0.125)
                dcol = st_["dcol"]
                if dcol is not None:
                    self.tt("pool", pt[pti][:, dcol:dcol + 128], pt[pti][:, dcol:dcol + 128],
                            self.trib[:], ALU.mult, [T_pt[pti], Tc], [T_pt[pti]])

            def s_back(idx):
                st_ = steps[idx]
                gi = st_["gi"]
                q0, nq, nctx, btab = QGROUPS[gi]
                c0, kb, mp, pti = st_["c0"], st_["kb"], st_["mp"], st_["pti"]
                kT_T = kblock_T(kb)
                stt_set = started.setdefault(gi, set())
                for qb in range(c0 // 128, nq):
                    a = mp * 4 + qb
                    bk = obank(a)
                    first = bk not in stt_set
                    stt_set.add(bk)
                    self.mm(oacc(a), pt[pti][:, qb * 128:(qb + 1) * 128], vh[:, kb, 0:129],
                            first, st_["ki"] == st_["nk"] - 1, [T_pt[pti], kT_T[1], T_vone], [self.T_ps[bk]],
                            skip=True)
                if st_["last"]:
                    finalize(gi)

            def finalize(gi):
                q0, nq, nctx, btab = QGROUPS[gi]
                for bk in range(3):
                    used = 396 if (bk + 1) * 3 <= 8 else 264
                    eng = "act" if bk != 1 else "dve"
                    self.cp(eng, osb[:, bk, 0:used], self.ps[5 + bk][:, 0:used], [self.T_ps[5 + bk]], [T_osb])
                osf = osb.rearrange("p b n -> p (b n)")
                rl = stt_[:, 0:8]
                self.recip(rl.rearrange("p (a o) -> p a o", o=1),
                           osf.rearrange("p (a n) -> p a n", n=132)[:, 0:8, 128:129], [T_osb], [T_st])
                self.ts("dve", stt_[:, 4:8], stt_[:, 4:8], self.cst[:, 2:3], ALU.mult, [T_st, Tc], [T_st])
                ysl = yst[gi % 2]
                Tys = T_yst[gi % 2]
                for qb in range(nq):
                    o0 = osf[:, qb * 132:qb * 132 + 128]
                    o1 = osf[:, (4 + qb) * 132:(4 + qb) * 132 + 128]
                    self.ts("pool", yq[:, :], o0, stt_[:, qb:qb + 1], ALU.mult, [T_osb, T_st], [T_yq])
                    self.stt(yq[:, :], o1, stt_[:, 4 + qb:5 + qb], yq[:, :], ALU.mult, ALU.add,
                             [T_osb, T_st, T_yq], [T_yq])
                    sc = 16 + 2 * qb
                    self.act(o0, yq[:, :], AF.Square, [T_yq], [T_osb, T_st], accum=stt_[:, sc:sc + 1])
                    self.act(stt_[:, sc + 1:sc + 2], stt_[:, sc:sc + 1], AF.Ln, [T_st, Tc], [T_st],
                             bias=self.cst[:, 1:2], scale=1.0 / 128)
                    self.act(stt_[:, sc + 1:sc + 2], stt_[:, sc + 1:sc + 2], AF.Exp, [T_st], [T_st], scale=-0.5)
                    self.stt(ysl[:, qb, :], yq[:, :], stt_[:, sc + 1:sc + 2], self.subg[:], ALU.mult, ALU.mult,
                             [T_yq, T_st, Tc], [Tys])
                dst = self.yscr[q0 * 128:(q0 + nq) * 128, h * 128:(h + 1) * 128].rearrange(
                    "(q p) e -> p q e", p=128)
                self.dma(dst, ysl[:, 0:nq, :], [Tys], [self.T_y[b] for b in range(q0, q0 + nq)])

            nst = len(steps)
            for idx in range(nst + LA):
                if idx < nst:
                    s_front(idx)
                if idx - LA >= 0:
                    s_back(idx - LA)

    def _l1_mixer(self):
        PADW = 16
        U0 = [self.carve(i * 4160, [128, PADW + 1024], F32) for i in range(2)]
        U1 = [self.carve(8320 + i * 4672, [128, PADW + 1152], F32) for i in range(2)]
        A = self.carve(17664, [128, PADW + 1152], F32)
        Bf = self.carve(22336, [128, PADW + 1152], F32)
        pooledT = self.carve(27008, [128, 3, TOK_OWN], BF16)
        wgp = [self.carve(40064 + i * 2304, [128, 3, 384], BF16) for i in range(2)]
        pscale = self.carve(44672, [128, MIX], F32)
        ysb = [self.carve(50816 + i * 768, [128, 384], BF16) for i in range(2)]
        invc = self.carve(52352, [128, 4, 16], F32)
        Ubuf = [U0, U1]
        T_U = [[T(), T()], [T(), T()]]
        T_A = T()
        T_B = T()
        T_pl = [[T() for _ in range(2)] for _ in range(3)]
        T_wgp = [T(), T()]
        T_psc = T()
        T_ysb = [T(), T()]
        T_inv = T()

        self.dma(pscale[:, :], self.bc(self.v_pscale), [], [T_psc])
        for g, w in enumerate(POOL_WINDOWS):
            for t in range(16):
                self.memset("pool", invc[:, g, t:t + 1], 1.0 / min(t + 1, w), [T_inv])
        for ci in range(2):
            for i in range(2):
                self.memset("pool", Ubuf[ci][i][:, 0:PADW], 0.0, [T_U[ci][i]])
        self.memset("pool", A[:, 0:PADW], 0.0, [T_A])
        self.memset("pool", Bf[:, 0:PADW], 0.0, [T_B])

        def load_wu(g):
            s = g % 2
            self.dma(self.wslot[s][:], self.w_u[g].rearrange("p (c n) -> p c n", c=8), [], [self.T_ws[s]], q="pool")

        def u_stage(g, ci):
            w = POOL_WINDOWS[g]
            c0, cn = POOL_CHUNKS[ci]
            s = g % 2
            W = self.wslot[s]
            for cc in range(3):
                u = Ubuf[ci][cc % 2]
                Tu = T_U[ci][cc % 2]
                halo = 0 if c0 == 0 else PADW
                t = c0 - halo
                pieces = []
                while t < c0 + cn:
                    n = min(512, c0 + cn - t)
                    if t < c0:
                        n = halo
                    pieces.append((t, n))
                    t += n
                for pi_, (t0, n) in enumerate(pieces):
                    pb = pi_ % 2
                    for c in range(8):
                        self.mm(self.ps[pb][:, 0:n], W[:, c, cc * 128:(cc + 1) * 128], self.hn_ap(c, t0, n),
                                c == 0, c == 7,
                                [self.T_ws[s]] + self.hn_T(t0 - (t0 % 128), 128 * ((t0 % 128 + n + 127) // 128)),
                                [self.T_ps[pb]])
                    dcol = PADW + (t0 - c0)
                    eng = "act" if pi_ % 2 == 0 else "dve"
                    self.cp(eng, u[:, dcol:dcol + n], self.ps[pb][:, 0:n], [self.T_ps[pb]], [Tu])
                src, Tsrc = u, Tu
                bufs = [(A, T_A), (Bf, T_B)]
                for stp in range(g + 1):
                    sh = 1 << stp
                    dst, Tdst = bufs[stp % 2]
                    eng = "dve" if stp % 2 == 0 else "pool"
                    self.tt(eng, dst[:, sh:PADW + cn], src[:, sh:PADW + cn], src[:, 0:PADW + cn - sh], ALU.add,
                            [Tsrc], [Tdst])
                    src, Tsrc = dst, Tdst
                self.stt(pooledT[:, cc, c0:c0 + cn], src[:, PADW:PADW + cn], 1.0 / w, u[:, PADW:PADW + cn],
                         ALU.mult, ALU.subtract, [Tsrc, Tu], [T_pl[cc][ci]])
                if c0 == 0:
                    tmp = src[:, 0:16]
                    self.tt("dve", tmp, src[:, PADW:PADW + 16], invc[:, g, :], ALU.mult, [Tsrc, T_inv], [Tsrc])
                    self.tt("dve", pooledT[:, cc, 0:16], tmp, u[:, PADW:PADW + 16], ALU.subtract,
                            [Tsrc, Tu], [T_pl[cc][ci]])

        def m_stage(g, ci):
            blks = range(0, 8) if ci == 0 else range(8, NB_OWN)
            for blk in blks:
                pb = 2 + blk % 2
                for cc in range(3):
                    self.mm(self.ps[pb][:, 0:384], pooledT[:, cc, blk * 128:(blk + 1) * 128], wgp[g % 2][:, cc, :],
                            cc == 0, cc == 2, [T_pl[cc][ci], T_wgp[g % 2]], [self.T_ps[pb]])
                yb = ysb[blk % 2]
                self.tt("dve", yb[:, :], self.ps[pb][:, 0:384], pscale[:, g * 384:(g + 1) * 384], ALU.mult,
                        [self.T_ps[pb], T_psc], [T_ysb[blk % 2]])
                self.dma(self.yscr[blk * 128:(blk + 1) * 128, g * 384:(g + 1) * 384], yb[:, :], [T_ysb[blk % 2]],
                         [self.T_y[blk]])

        load_wu(0)
        for g in range(4):
            if g + 1 < 4:
                load_wu(g + 1)
            self.dma(wgp[g % 2], self.w_gp[g].rearrange("p (c n) -> p c n", c=3), [], [T_wgp[g % 2]], q="pool")
            u_stage(g, 0)
            if g > 0:
                m_stage(g - 1, 1)
            u_stage(g, 1)
            m_stage(g, 0)
        m_stage(3, 1)

    def _epilogue(self, L):
        Tc = self.T_const
        last = (L == 1)
        xst = [self.carve(i * 4096, [128, D], F32) for i in range(2)]
        hsb = [self.carve(8192 + i * 4096, [128, D], F32) for i in range(2)]
        hnb = self.carve(16384, [128, D], BF16)
        sg = [self.carve(18432 + i * 4096, [128, 2048], BF16) for i in range(2)]
        z = [self.carve(26624 + i * 4096, [128, 2048], BF16) for i in range(2)]
        zT = self.carve(34816, [128, 16, 128], BF16)
        ysb = [self.carve(38912 + i * 3072, [128, MIX], BF16) for i in range(2)]
        qmT = [self.carve(45056 + i * 1024, [128, 4, 128], BF16) for i in range(2)]
        pm = self.carve(47104, [128, 8, 128], BF16)
        om = self.carve(49152, [128, 4, 132], F32)
        stB = self.carve(51264, [128, 16], F32)
        stC = self.carve(51328, [128, 32], F32)
        junk = zT.rearrange("p e t -> p (e t)")[:, 0:D]
        T_xst = [T(), T()]
        T_hsb = [T(), T()]
        T_hnb = T()
        T_sg = [T(), T()]
        T_z = [T(), T()]
        T_zT = T()
        T_ysb = [T(), T()]
        T_qm = [T(), T()]
        T_pm = T()
        T_stB = T()
        T_stC = T()
        T_om = T()

        self.load_epi_weights(L, ("gt", "qm", "wo"))
        gsrc = self.v_fing[0:1, :] if last else self.v_lng[1:2, :]
        self.dma(self.gbuf[:], self.bc(gsrc), [], [self.T_g])

        def stage_a(blk):
            i = blk % 2
            Thn = self.T_hn[blk]
            rows = slice(blk * 128, (blk + 1) * 128)
            self.dma(ysb[i][:, :], self.yscr[rows, :], [self.T_y[blk]], [T_ysb[i]])
            for hd in range(4):
                for c in range(8):
                    self.mm(self.ps[0][:, hd * 128:(hd + 1) * 128], self.wqm[:, c, hd * 128:(hd + 1) * 128],
                            self.hnT[:, c, rows], c == 0, c == 7, [self.T_wqm, Thn], [self.T_ps[0]], skip=True)
            self.cp("dve", qmT[i][:, :, :], self.ps[0][:, :].rearrange("p (h t) -> p h t", h=4), [self.T_ps[0]],
                    [T_qm[i]])
            for n4 in range(4):
                pi = 1 + n4 % 2
                for c in range(8):
                    self.mm(self.ps[pi][:, :], self.hnT[:, c, rows], self.wgt[:, c, n4 * 512:(n4 + 1) * 512],
                            c == 0, c == 7, [self.T_regB_w, Thn], [self.T_ps[pi]])
                self.act(sg[i][:, n4 * 512:(n4 + 1) * 512], self.ps[pi][:, :], AF.Silu, [self.T_ps[pi]], [T_sg[i]])

        def stage_b(blk):
            i = blk % 2
            rows = slice(blk * 128, (blk + 1) * 128)
            if L == 0 or not self.do_l0:
                self.dma(xst[i][:, :], self.xl[rows, :], [], [T_xst[i]])
            else:
                self.dma(xst[i][:, :], self.h1scr[rows, :], [self.T_h1[blk]], [T_xst[i]])
            for hd in range(4):
                for mc in range(2):
                    pi = 3 + hd // 2
                    col = ((hd % 2) * 2 + mc) * 128
                    self.mm(self.ps[pi][:, col:col + 128], self.kmT[L][:, hd, mc * 128:(mc + 1) * 128],
                            qmT[i][:, hd, :], True, True, [self.T_km[L], T_qm[i]], [self.T_ps[pi]], skip=True)
            for half in range(2):
                pi = 3 + half
                self.act(pm[:, half * 4:(half + 1) * 4, :], self.ps[pi][:, :].rearrange("p (a t) -> p a t", a=4),
                         AF.Exp, [self.T_ps[pi]], [T_pm], scale=128 ** -0.5)
            for hd in range(4):
                pi = 5 + hd // 2
                col = (hd % 2) * 132
                for mc in range(2):
                    self.mm(self.ps[pi][:, col:col + 129], pm[:, hd * 2 + mc, :], self.vm[L][:, mc, hd, 0:129],
                            mc == 0 and hd % 2 == 0, mc == 1, [T_pm, self.T_vm[L]], [self.T_ps[pi]], skip=True)
            for half in range(2):
                self.cp("dve", om[:, half * 2:(half + 1) * 2, :],
                        self.ps[5 + half][:, 0:264].rearrange("p (a n) -> p a n", a=2), [self.T_ps[5 + half]], [T_om])
            self.recip(stB[:, 0:4].rearrange("p (a o) -> p a o", o=1), om[:, :, 128:129], [T_om], [T_stB])
            self.tt("pool", z[i][:, 0:MIX], ysb[i][:, :], sg[i][:, 0:MIX], ALU.mult, [T_ysb[i], T_sg[i]], [T_z[i]])
            for hd in range(4):
                self.stt(z[i][:, MIX + hd * 128:MIX + (hd + 1) * 128], om[:, hd, 0:128], stB[:, hd:hd + 1],
                         sg[i][:, MIX + hd * 128:MIX + (hd + 1) * 128], ALU.mult, ALU.mult,
                         [T_om, T_stB, T_sg[i]], [T_z[i]])

        def stage_c(blk):
            i = blk % 2
            rows = slice(blk * 128, (blk + 1) * 128)
            for half in range(2):
                pi = 7
                psb = self.ps[pi][:].bitcast(BF16)
                for e in range(8):
                    ee = half * 8 + e
                    self.tr(psb[:, e * 128:(e + 1) * 128], z[i][:, ee * 128:(ee + 1) * 128], [T_z[i], Tc],
                            [self.T_ps[pi]])
                eng = "act" if half == 0 else "dve"
                self.cp(eng, zT[:, half * 8:(half + 1) * 8, :], psb.rearrange("p (e t) -> p e t", e=8),
                        [self.T_ps[pi]], [T_zT])
            for n2 in range(2):
                pi = 5 + n2
                for e in range(16):
                    self.mm(self.ps[pi][:, :], zT[:, e, :], self.wo[:, e, n2 * 512:(n2 + 1) * 512], e == 0, e == 15,
                            [T_zT, self.T_cs], [self.T_ps[pi]])
                self.tt("dve", hsb[i][:, n2 * 512:(n2 + 1) * 512], self.ps[pi][:, :],
                        xst[i][:, n2 * 512:(n2 + 1) * 512], ALU.add, [self.T_ps[pi], T_xst[i]], [T_hsb[i]])
            if last:
                self.norm_block(hsb[i][:, :], T_hsb[i], junk, T_zT, stC, T_stC, 2 * (blk % 8), self.gbuf[:],
                                self.T_g, dst_f32=xst[i][:, :], Tdst=T_xst[i])
                self.dma(self.out[rows, :], xst[i][:, :], [T_xst[i]], [self.T_out])
            else:
                if self.h1scr is not None:
                    self.dma(self.h1scr[rows, :], hsb[i][:, :], [T_hsb[i]], [self.T_h1[blk]])
                else:
                    self.dma(self.out[rows, :], hsb[i][:, :], [T_hsb[i]], [self.T_out])
                    return
                self.norm_block(hsb[i][:, :], T_hsb[i], junk, T_zT, stC, T_stC, 2 * (blk % 8), self.gbuf[:],
                                self.T_g, hnb=hnb[:, :], Thnb=T_hnb)
                self.transpose_block(hnb, T_hnb, blk, 0)

        for it in range(NB_OWN + 2):
            if it < NB_OWN:
                stage_a(it)
            if 0 <= it - 1 < NB_OWN:
                stage_b(it - 1)
            if 0 <= it - 2 < NB_OWN:
                stage_c(it - 2)


def _own_ctx_blocks(j):
    if j == 0:
        own = list(range(0, 8)) + list(range(23, 32))
        ctx = list(range(8, 23))
    else:
        own = list(range(7, 24))
        ctx = list(range(0, 7)) + list(range(24, 32))
    return own, ctx


def _blocks_to_rows(blks):
    return np.concatenate([np.arange(b * 128, (b + 1) * 128) for b in blks])


def _consts(j):
    ident = np.eye(128, dtype=np.float32)
    rmat = np.zeros((128, 128), np.float32)
    for f in range(128):
        if (f % 64) < 32:
            rmat[f + 32, f] = -1.0
        else:
            rmat[f - 32, f] = 1.0
    pp, cc = np.meshgrid(np.arange(128), np.arange(128), indexing="ij")
    tri = (pp <= cc).astype(np.float32)
    inv = (np.float32(10000.0) ** (-np.arange(0, 64, 2, dtype=np.float32) / np.float32(64))).astype(np.float32)
    invf = inv[np.arange(128) % 32].reshape(128, 1).astype(np.float32)
    bias = np.zeros((128, 22), np.float32)
    if j == 0:
        bias[:, 0:7] = NEG
    else:
        bias[:, 7 + 7:22] = NEG
    return ident, rmat, tri, invf, bias


def _prep_weights(inp):
    f = np.float32
    aw = np.asarray(inp["attn_w_in"], f)[0]
    wq, wk, wv = aw[:, 0:1536], aw[:, 1536:3072], aw[:, 3072:4608]
    w_h0 = np.empty((NH, 128, 8, 384), f)
    for h in range(NH):
        cat = np.concatenate([wq[:, h * 128:(h + 1) * 128], wk[:, h * 128:(h + 1) * 128],
                              wv[:, h * 128:(h + 1) * 128]], axis=1)
        w_h0[h] = cat.reshape(8, 128, 384).transpose(1, 0, 2)
    pw = np.asarray(inp["pool_w_in"], f)[0]

    def pk(w, k):
        n = w.shape[1]
        return np.ascontiguousarray(w.reshape(k, 128, n).transpose(1, 0, 2)).reshape(128, k * n)

    w_qm = np.stack([pk(aw[:, 4608:5120], 8), pk(pw[:, 1536:2048], 8)])
    w_gt = np.stack([pk(aw[:, 5120:7168], 8), pk(pw[:, 2048:4096], 8)])
    wo = np.asarray(inp["w_out"], f)
    w_o = np.stack([pk(wo[0], 16), pk(wo[1], 16)])
    kv = np.asarray(inp["mem_w_kv"], f)
    w_kv = np.stack([pk(kv[0], 8), pk(kv[1], 8)])
    w_u = np.stack([pk(pw[:, g * 384:(g + 1) * 384], 8) for g in range(4)])
    gp = np.asarray(inp["pool_w_group"], f)[0]
    w_gp = np.stack([pk(gp[g], 3) for g in range(4)])
    return dict(w_h0=np.ascontiguousarray(w_h0).reshape(NH, 128, 8 * 384), w_qm=w_qm, w_gt=w_gt, w_o=w_o,
                w_kv=w_kv, w_u=w_u, w_gp=w_gp)


_NC_CACHE = {}


def _get_nc(do_l0, do_l1):
    key = (do_l0, do_l1)
    if key not in _NC_CACHE:
        lam0 = 0.8 - 0.6 * math.exp(-0.3 * 0)
        _NC_CACHE[key] = Builder(do_l0, do_l1, lam0).build()
    return _NC_CACHE[key]


def _in_maps(inp, xsrc, full_tokens):
    f = np.float32
    wts = _prep_weights(inp)
    x = np.asarray(xsrc, f)
    mem = np.asarray(inp["mem"], f)
    pos = np.asarray(inp["positions"], np.int32)
    maps = []
    for core in range(8):
        b, j = core // 2, core % 2
        own, ctx = _own_ctx_blocks(j)
        rows_own = _blocks_to_rows(own)
        rows_all = np.concatenate([rows_own, _blocks_to_rows(ctx)])
        ident, rmat, tri, invf, bias = _consts(j)
        m = dict(wts)
        if full_tokens:
            m["xl"] = np.ascontiguousarray(x[b][rows_all])
        else:
            m["xl"] = np.ascontiguousarray(x[core])
        m["posl"] = np.ascontiguousarray(pos[b][rows_all]).reshape(1, TOK_ALL)
        m["meml"] = np.ascontiguousarray(mem[b])
        m["c_ident"] = ident
        m["c_rmat"] = rmat
        m["c_tri"] = tri
        m["c_invf"] = invf
        m["c_bias"] = bias
        m["v_lng"] = np.asarray(inp["ln_g"], f)
        m["v_memg"] = np.asarray(inp["mem_norm_g"], f).reshape(1, D)
        m["v_fing"] = np.asarray(inp["final_g"], f).reshape(1, D)
        m["v_subg"] = np.asarray(inp["attn_subln_g"], f).reshape(1, 128)
        m["v_pscale"] = np.asarray(inp["pool_scale"], f).reshape(1, MIX)
        m["v_lam"] = np.asarray(inp["attn_lambda"], f).reshape(1, 256)
        maps.append(m)
    return maps


def _assemble(res):
    out = np.empty((4, TOK_ALL, D), np.float32)
    for core in range(8):
        b, j = core // 2, core % 2
        own, _ = _own_ctx_blocks(j)
        o = res[core]["out"]
        for li, gb in enumerate(own):
            if j == 0 and li == 8:
                continue
            if j == 1 and li == 0:
                continue
            out[b, gb * 128:(gb + 1) * 128] = o[li * 128:(li + 1) * 128]
    return out


FUSED = True


def kernel(**inputs):
    if FUSED:
        nc = _get_nc(True, True)
        res = run_bass_kernel_spmd(nc, _in_maps(inputs, inputs["x"], True), core_ids=list(range(8)))
        return _assemble(res.results)
    nc0 = _get_nc(True, False)
    r0 = run_bass_kernel_spmd(nc0, _in_maps(inputs, inputs["x"], True), core_ids=list(range(8)))
    h1 = [r0.results[c]["out"] for c in range(8)]
    nc1 = _get_nc(False, True)
    r1 = run_bass_kernel_spmd(nc1, _in_maps(inputs, h1, False), core_ids=list(range(8)))
    return _assemble(r1.results)
```

```python
import math
from contextlib import ExitStack

import numpy as np
import concourse.bass as bass
import concourse.mybir as mybir
from concourse.bass_utils import run_bass_kernel_spmd

F32 = mybir.dt.float32
BF16 = mybir.dt.bfloat16
I32 = mybir.dt.int32
AF = mybir.ActivationFunctionType
ALU = mybir.AluOpType

NDMASEM = 14
DMA_SLOTS = {"sp": list(range(0, 8)), "pool": list(range(8, 14)), "act": list(range(0, 8))}

D = 1024
NB_OWN = 17
NB_CTX = 15
TOK_OWN = NB_OWN * 128
TOK_CTX = NB_CTX * 128
TOK_ALL = 4096
NH = 12
EPS = 1e-6
MIX = 1536
NEG = -30000.0
TWO_PI = 2.0 * math.pi
CW1 = 6.28125
CW2 = TWO_PI - CW1
OWN_GROUPS = [(0, 512), (512, 512), (1024, 512), (1536, 512), (2048, 128)]
CTX_GROUPS = [(2176, 512), (2688, 512), (3200, 512), (3712, 384)]
QGROUPS = [(0, 4, 7, 0), (4, 4, 7, 0), (8, 4, 15, 1), (12, 4, 15, 1), (16, 1, 15, 1)]
POOL_WINDOWS = (2, 4, 8, 16)
POOL_CHUNKS = [(0, 1024), (1024, 1152)]


class T:
    __slots__ = ("name", "w", "r", "excl")

    def __init__(self, name="", excl=False):
        self.name = name
        self.w = None
        self.r = {}
        self.excl = excl


class Op:
    __slots__ = ("eng", "fn", "deps", "veng", "idx", "signal", "count", "is_dma", "dval")


class Prog:
    ENGS = ("pe", "act", "dve", "pool", "sp")

    def __init__(self):
        self.streams = {e: [] for e in self.ENGS}
        self.ndma = {"sp": 0, "pool": 0, "act": 0}
        self.dma_cnt = [0] * NDMASEM
        self.guard = None

    def add(self, eng, fn, reads=(), writes=(), dma=False):
        o = Op()
        o.eng = eng
        o.fn = fn
        o.is_dma = dma
        o.signal = False
        o.count = 0
        o.dval = 0
        reads = list(reads)
        if self.guard is not None:
            reads.append(self.guard)
        deps = {}
        for t in reads:
            if t.w is not None:
                deps[id(t.w)] = t.w
            if t.excl:
                for p in t.r.values():
                    if p.eng != eng:
                        deps[id(p)] = p
        for t in writes:
            if t.w is not None:
                deps[id(t.w)] = t.w
            for p in t.r.values():
                deps[id(p)] = p
        o.deps = list(deps.values())
        if dma:
            slots = DMA_SLOTS[eng]
            slot = slots[self.ndma[eng] % len(slots)]
            self.ndma[eng] += 1
            self.dma_cnt[slot] += 1
            o.veng = ("dma", slot)
            o.dval = 16 * self.dma_cnt[slot]
        else:
            o.veng = eng
        for t in reads:
            t.r[o.veng] = o
        for t in writes:
            t.w = o
            t.r = {}
        o.idx = len(self.streams[eng])
        self.streams[eng].append(o)
        return o

    def emit(self, nc):
        for st in self.streams.values():
            for o in st:
                for d in o.deps:
                    d.signal = True
        for e, st in self.streams.items():
            c = 0
            for o in st:
                if o.is_dma:
                    continue
                if o.signal:
                    c += 1
                o.count = c
        with ExitStack() as es:
            sems = {e: es.enter_context(nc.semaphore("s_" + e)) for e in self.ENGS}
            dsems = [es.enter_context(nc.semaphore("d%d" % i)) for i in range(NDMASEM)]
            block = es.enter_context(nc.Block())
            streams = self.streams
            dma_cnt = self.dma_cnt

            def run(e, eng):
                waited = {}
                for o in streams[e]:
                    need = {}
                    for d in o.deps:
                        if d.is_dma:
                            key = d.veng
                            val = d.dval
                        else:
                            key = d.eng
                            val = d.count
                            if d.eng == e:
                                if e == "pe":
                                    continue
                                if o.idx - d.idx > 3:
                                    continue
                        if need.get(key, 0) < val:
                            need[key] = val
                    if o.is_dma and o.dval > 16:
                        if need.get(o.veng, 0) < o.dval - 16:
                            need[o.veng] = o.dval - 16
                    for key, val in need.items():
                        if waited.get(key, 0) >= val:
                            continue
                        sem = dsems[key[1]] if isinstance(key, tuple) else sems[key]
                        eng.wait_ge(sem, val)
                        waited[key] = val
                    ins = o.fn(eng)
                    if o.is_dma:
                        ins.then_inc(dsems[o.veng[1]], 16)
                    elif o.signal:
                        ins.then_inc(sems[e], 1)
                if e == "sp":
                    for slot in range(NDMASEM):
                        if dma_cnt[slot] > 0:
                            eng.wait_ge(dsems[slot], 16 * dma_cnt[slot])

            @block.tensor
            def _(eng):
                run("pe", eng)

            @block.scalar
            def _(eng):
                run("act", eng)

            @block.vector
            def _(eng):
                run("dve", eng)

            @block.gpsimd
            def _(eng):
                run("pool", eng)

            @block.sync
            def _(eng):
                run("sp", eng)


class Builder:
    def __init__(self, do_l0=True, do_l1=True, lambda_init0=0.2):
        self.do_l0 = do_l0
        self.do_l1 = do_l1
        self.lambda_init0 = lambda_init0
        self.p = Prog()
        self.epi_loaded = set()
        self.nc = bass.Bass("TRN2", target_bir_lowering=False)

    def mm(self, out, lhsT, rhs, start, stop, reads, writes, skip=False):
        self.p.add("pe", lambda e: e.matmul(out, lhsT=lhsT, rhs=rhs, start=start, stop=stop,
                                            skip_group_check=skip), reads, writes)

    def tr(self, out, in_, reads, writes):
        ident = self.identb
        self.p.add("pe", lambda e: e.transpose(out=out, in_=in_, identity=ident[:]), reads, writes)

    def act(self, out, in_, func, reads, writes, bias=None, scale=None, accum=None):
        kw = {}
        if bias is not None:
            kw["bias"] = bias
        if scale is not None:
            kw["scale"] = scale
        if accum is not None:
            kw["accum_out"] = accum
        self.p.add("act", lambda e: e.activation(out=out, in_=in_, func=func, **kw), reads, writes)

    def ts(self, eng, out, in0, s1, op0, reads, writes, s2=None, op1=None):
        if op1 is None:
            self.p.add(eng, lambda e: e.tensor_scalar(out=out, in0=in0, scalar1=s1, scalar2=None, op0=op0),
                       reads, writes)
        else:
            self.p.add(eng, lambda e: e.tensor_scalar(out=out, in0=in0, scalar1=s1, scalar2=s2, op0=op0, op1=op1),
                       reads, writes)

    def tt(self, eng, out, in0, in1, op, reads, writes):
        self.p.add(eng, lambda e: e.tensor_tensor(out=out, in0=in0, in1=in1, op=op), reads, writes)

    def stt(self, out, in0, scalar, in1, op0, op1, reads, writes):
        self.p.add("dve", lambda e: e.scalar_tensor_tensor(out=out, in0=in0, scalar=scalar, in1=in1,
                                                           op0=op0, op1=op1), reads, writes)

    def cp(self, eng, out, in_, reads, writes):
        if eng == "act":
            self.p.add("act", lambda e: e.copy(out=out, in_=in_), reads, writes)
        else:
            self.p.add(eng, lambda e: e.tensor_copy(out=out, in_=in_), reads, writes)

    def memset(self, eng, ap, val, writes):
        self.p.add(eng, lambda e: e.memset(ap, val), (), writes)

    def recip(self, out, in_, reads, writes):
        self.p.add("dve", lambda e: e.reciprocal(out=out, in_=in_), reads, writes)

    def dma(self, out, in_, reads, writes, q="sp"):
        self.p.add(q, lambda e: e.dma_start(out=out, in_=in_), reads, writes, dma=True)

    def bc(self, ap):
        b = ap.partition_broadcast(128)
        return b.rearrange("p o n -> p (o n)")

    def dbg(self, name, ap, reads):
        import os
        if not os.environ.get("KDBG"):
            return
        shape = [int(x) for x in ap.shape]
        d = self.nc.dram_tensor("dbg_" + name, shape, ap.dtype, kind="ExternalOutput").ap()
        self.dma(d, ap, reads, [T()])

    def load_epi_weights(self, L, which):
        g = self.p.guard
        self.p.guard = None
        for w in which:
            if (L, w) in self.epi_loaded:
                continue
            self.epi_loaded.add((L, w))
            if w == "gt":
                ctxT = [self.T_hn[b] for b in range(17, 32)]
                self.dma(self.regB[:, :].rearrange("p (c n) -> p c n", c=16),
                         self.w_gt[L].rearrange("p (c n) -> p c n", c=16), ctxT, [self.T_regB_w] + ctxT, q="pool")
            elif w == "qm":
                self.dma(self.wqm[:], self.w_qm[L].rearrange("p (c n) -> p c n", c=8), [], [self.T_wqm], q="pool")
            else:
                self.dma(self.wo, self.w_o[L].rearrange("p (e d) -> p e d", e=16), [], [self.T_cs], q="pool")
        self.p.guard = g

    def fence(self):
        g = self.p.guard
        self.p.guard = None
        fz = self.fz
        self.p.add("pool", lambda e: e.memset(fz[:], 0.0), (), [g])
        self.p.guard = g

    def build(self):
        nc = self.nc
        with ExitStack() as es:
            self.es = es
            self._declare(es)
            self._setup()
            self.fence()
            import os
            ph = int(os.environ.get("KPHASE", "9"))
            if self.do_l0 and ph >= 1:
                self.load_epi_weights(0, ("qm",))
                self._l0_mixer()
                self.fence()
                if ph >= 2:
                    self._epilogue(0)
                    if self.do_l1:
                        self.load_epi_weights(1, ("gt", "qm", "wo"))
                    self.fence()
            if self.do_l1:
                self._l1_mixer()
                self.fence()
                self._epilogue(1)
            self.p.emit(nc)
        return nc

    def _declare(self, es):
        nc = self.nc
        di = lambda n, s, d: nc.dram_tensor(n, s, d, kind="ExternalInput").ap()
        self.xl = di("xl", [TOK_ALL if self.do_l0 else TOK_OWN, D], F32)
        self.posl = di("posl", [1, TOK_ALL], I32)
        self.meml = di("meml", [256, D], F32)
        self.c_ident = di("c_ident", [128, 128], F32)
        self.c_rmat = di("c_rmat", [128, 128], F32)
        self.c_tri = di("c_tri", [128, 128], F32)
        self.c_invf = di("c_invf", [128, 1], F32)
        self.c_bias = di("c_bias", [128, 22], F32)
        self.v_lng = di("v_lng", [2, D], F32)
        self.v_memg = di("v_memg", [1, D], F32)
        self.v_fing = di("v_fing", [1, D], F32)
        self.v_subg = di("v_subg", [1, 128], F32)
        self.v_pscale = di("v_pscale", [1, MIX], F32)
        self.v_lam = di("v_lam", [1, 256], F32)
        self.w_h0 = di("w_h0", [NH, 128, 8 * 384], F32)
        self.w_qm = di("w_qm", [2, 128, 8 * 512], F32)
        self.w_gt = di("w_gt", [2, 128, 8 * 2048], F32)
        self.w_o = di("w_o", [2, 128, 16 * 1024], F32)
        self.w_kv = di("w_kv", [2, 128, 8 * 1024], F32)
        self.w_u = di("w_u", [12, 128, 8 * 128], F32)
        self.w_gp = di("w_gp", [4, 128, 3 * 384], F32)
        self.out = nc.dram_tensor("out", [TOK_OWN, D], F32, kind="ExternalOutput").ap()
        import os
        if os.environ.get("KDUMP"):
            self.yscr = nc.dram_tensor("yscr", [TOK_OWN, MIX], BF16, kind="ExternalOutput").ap()
        else:
            self.yscr = nc.dram_tensor("yscr", [TOK_OWN, MIX], BF16).ap()
        if self.do_l0 and self.do_l1:
            self.h1scr = nc.dram_tensor("h1scr", [TOK_OWN, D], F32).ap()
        else:
            self.h1scr = None

        sb = lambda n, s, d: es.enter_context(nc.sbuf_tensor(n, s, d))
        self.hnT = sb("hnT", [128, 8, TOK_OWN], BF16)
        self.regB = sb("regB", [128, 16384], BF16)
        self.regC = sb("regC", [128, 8192], F32)
        self.regD = sb("regD", [128, 13312], F32)
        self.wslot = [sb("wslot%d" % i, [128, 8, 384], BF16) for i in range(2)]
        self.wqm = sb("wqm", [128, 8, 512], BF16)
        self.kmT = [sb("kmT%d" % i, [128, 4, 256], BF16) for i in range(2)]
        self.vm = [sb("vm%d" % i, [128, 2, 4, 132], BF16) for i in range(2)]
        self.gbuf = sb("gbuf", [128, D], F32)
        self.subg = sb("subg", [128, 128], F32)
        self.identb = sb("identb", [128, 128], BF16)
        self.rmatb = sb("rmatb", [128, 128], BF16)
        self.trib = sb("trib", [128, 128], BF16)
        self.cst = sb("cst", [128, 32], F32)
        self.fz = sb("fz", [128, 2], F32)
        self.ps = [es.enter_context(nc.psum_tensor("ps%d" % i, [128, 512], F32)) for i in range(8)]

        self.hnT_ctx = self.regB[:, 0:8 * TOK_CTX].rearrange("p (c t) -> p c t", c=8)
        self.wgt = self.regB[:, :].rearrange("p (c n) -> p c n", c=8)
        self.cos = self.regC[:, 0:4096]
        self.sin = self.regC[:, 4096:8192]
        self.wo = self.regC[:, :].bitcast(BF16).rearrange("p (e d) -> p e d", e=16)

        self.T_hn = [T("hn%d" % i) for i in range(32)]
        self.T_regB_w = T("wgt")
        self.T_cs = T("cossin/wo")
        self.T_ws = [T("ws0"), T("ws1")]
        self.T_wqm = T("wqm")
        self.T_km = [T(), T()]
        self.T_vm = [T(), T()]
        self.T_g = T("gbuf")
        self.T_const = T("const")
        self.T_ps = [T("ps%d" % i, excl=True) for i in range(8)]
        self.T_y = [T("yscr%d" % i) for i in range(NB_OWN)]
        self.T_h1 = [T("h1scr%d" % i) for i in range(NB_OWN)]
        self.T_out = T("out")
        self.p.guard = T("guard")

    def carve(self, off_bytes, shape, dtype):
        free = list(shape[1:])
        n = int(np.prod(free))
        nbytes = n * (4 if dtype in (F32, I32) else 2)
        assert off_bytes % 4 == 0 and off_bytes + nbytes <= 13312 * 4, (off_bytes, nbytes)
        a = self.regD[:, off_bytes // 4:(off_bytes + nbytes + 3) // 4]
        if dtype != F32:
            a = a.bitcast(dtype)
        a = a[:, 0:n]
        if len(free) == 2:
            return a.rearrange("p (a b) -> p a b", a=free[0])
        if len(free) == 3:
            return a.rearrange("p (a b c) -> p a b c", a=free[0], b=free[1])
        return a

    def hn_ap(self, c, t0, n):
        if t0 < TOK_OWN:
            assert t0 + n <= TOK_OWN
            return self.hnT[:, c, t0:t0 + n]
        return self.hnT_ctx[:, c, t0 - TOK_OWN:t0 - TOK_OWN + n]

    def hn_T(self, t0, n):
        return [self.T_hn[b] for b in range(t0 // 128, (t0 + n) // 128)]

    def norm_block(self, src, Tsrc, junk, Tjunk, st, Tst, col, g, Tg, dst_f32=None, Tdst=None,
                   hnb=None, Thnb=None):
        eps_ap = self.cst[:, 1:2]
        ss = st[:, col:col + 1]
        rs = st[:, col + 1:col + 2]
        self.act(junk, src, AF.Square, [Tsrc], [Tjunk, Tst], accum=ss)
        self.act(rs, ss, AF.Ln, [Tst, self.T_const], [Tst], bias=eps_ap, scale=1.0 / D)
        self.act(rs, rs, AF.Exp, [Tst], [Tst], scale=-0.5)
        if hnb is not None:
            self.stt(hnb, src, rs, g, ALU.mult, ALU.mult, [Tsrc, Tst, Tg], [Thnb])
        if dst_f32 is not None:
            self.stt(dst_f32, src, rs, g, ALU.mult, ALU.mult, [Tsrc, Tst, Tg], [Tdst])

    def transpose_block(self, hnb, Thnb, blk, psi):
        psb = self.ps[psi][:].bitcast(BF16)
        for c in range(8):
            self.tr(psb[:, c * 128:(c + 1) * 128], hnb[:, c * 128:(c + 1) * 128],
                    [Thnb, self.T_const], [self.T_ps[psi]])
        if blk < NB_OWN:
            dst = self.hnT[:, :, blk * 128:(blk + 1) * 128]
        else:
            b = blk - NB_OWN
            dst = self.hnT_ctx[:, :, b * 128:(b + 1) * 128]
        src = psb.rearrange("p (c t) -> p c t", c=8)
        eng = "dve" if blk % 2 == 0 else "act"
        self.cp(eng, dst, src, [self.T_ps[psi]], [self.T_hn[blk]])

    def _setup(self):
        p = self.p
        Tc = self.T_const
        tmpf = self.carve(0, [128, 128], F32)
        Ttmp = T()
        for src, dst in ((self.c_ident, self.identb), (self.c_rmat, self.rmatb), (self.c_tri, self.trib)):
            self.dma(tmpf[:, :], src, [], [Ttmp])
            self.cp("dve", dst[:], tmpf[:, :], [Ttmp], [Tc])
        self.memset("dve", self.cst[:, :], 0.0, [Tc])
        self.memset("dve", self.cst[:, 1:2], EPS, [Tc])
        self.memset("dve", self.cst[:, 4:8], -0.5, [Tc])
        self.dma(self.cst[:, 0:1], self.c_invf, [], [Tc])
        self.dma(self.cst[:, 8:30], self.c_bias, [], [Tc])
        lamt = self.carve(1024, [128, 256], F32)
        lt2 = self.carve(2048, [128, 128], F32)
        Tl = T()
        self.dma(lamt[:, :], self.bc(self.v_lam), [], [Tl])
        self.tt("dve", lt2[:, 0:64], lamt[:, 0:64], lamt[:, 64:128], ALU.mult, [Tl], [Tl])
        self.tt("dve", lt2[:, 64:128], lamt[:, 128:192], lamt[:, 192:256], ALU.mult, [Tl], [Tl])
        st0 = self.carve(3072, [128, 16], F32)
        Tst0 = T()
        self.act(lamt[:, 0:64], lt2[:, 0:64], AF.Copy, [Tl], [Tl, Tst0], accum=st0[:, 0:1])
        self.act(lamt[:, 64:128], lt2[:, 64:128], AF.Copy, [Tl], [Tl, Tst0], accum=st0[:, 1:2])
        self.act(st0[:, 2:4], st0[:, 0:2], AF.Exp, [Tst0], [Tst0])
        self.tt("dve", st0[:, 4:5], st0[:, 3:4], st0[:, 2:3], ALU.subtract, [Tst0], [Tst0])
        self.ts("dve", self.cst[:, 2:3], st0[:, 4:5], -self.lambda_init0, ALU.add, [Tst0], [Tc])
        self.dma(self.subg[:], self.bc(self.v_subg), [], [Tc])
        self.ts("dve", self.subg[:], self.subg[:], 1.0 - self.lambda_init0, ALU.mult, [Tc], [Tc])

        xst = [self.carve(4096 + i * 4096, [128, D], F32) for i in range(2)]
        Txst = [T(), T()]
        hnb = [self.carve(12288 + i * 2048, [128, D], BF16) for i in range(2)]
        Thnb = [T(), T()]
        junk = self.carve(16384, [128, D], F32)
        Tjunk = T()
        st = self.carve(20480, [128, 80], F32)
        Tst = T()
        memT = self.carve(20992, [128, 8, 256], BF16)
        Tmem = T()
        wkv = self.carve(25088, [128, 8, 1024], BF16)
        Twkv = T()
        self.dma(self.gbuf[:], self.bc(self.v_memg), [], [self.T_g])
        for mb in range(2):
            i = mb % 2
            self.dma(xst[i][:, :], self.meml[mb * 128:(mb + 1) * 128, :], [], [Txst[i]])
            self.norm_block(xst[i][:, :], Txst[i], junk[:, :], Tjunk, st, Tst, 2 * mb, self.gbuf[:], self.T_g,
                            hnb=hnb[i][:, :], Thnb=Thnb[i])
            psb = self.ps[mb][:].bitcast(BF16)
            for c in range(8):
                self.tr(psb[:, c * 128:(c + 1) * 128], hnb[i][:, c * 128:(c + 1) * 128], [Thnb[i], Tc],
                        [self.T_ps[mb]])
            self.cp("dve", memT[:, :, mb * 128:(mb + 1) * 128], psb.rearrange("p (c t) -> p c t", c=8),
                    [self.T_ps[mb]], [Tmem])
        for L in range(2):
            if (L == 0 and not self.do_l0) or (L == 1 and not self.do_l1):
                continue
            self.dma(wkv, self.w_kv[L].rearrange("p (c n) -> p c n", c=8), [], [Twkv], q="pool")
            for hd in range(4):
                pi = 2 + (hd % 2)
                for c in range(8):
                    self.mm(self.ps[pi][:, 0:256], wkv[:, c, hd * 128:(hd + 1) * 128], memT[:, c, :],
                            c == 0, c == 7, [Twkv, Tmem], [self.T_ps[pi]])
                self.cp("act", self.kmT[L][:, hd, :], self.ps[pi][:, 0:256], [self.T_ps[pi]], [self.T_km[L]])
            self.memset("pool", self.vm[L][:, :, :, 128:129], 1.0, [self.T_vm[L]])
            for mc in range(2):
                pi = 4 + mc
                for c in range(8):
                    self.mm(self.ps[pi][:, :], memT[:, c, mc * 128:(mc + 1) * 128], wkv[:, c, 512:1024],
                            c == 0, c == 7, [Twkv, Tmem], [self.T_ps[pi]])
                self.cp("dve", self.vm[L][:, mc, :, 0:128], self.ps[pi][:, :].rearrange("p (h e) -> p h e", h=4),
                        [self.T_ps[pi]], [self.T_vm[L]])

        if not self.do_l0:
            self.dma(self.gbuf[:], self.bc(self.v_lng[1:2, :]), [], [self.T_g])
            for blk in range(NB_OWN):
                i = blk % 2
                self.dma(xst[i][:, :], self.xl[blk * 128:(blk + 1) * 128, :], [], [Txst[i]])
                self.norm_block(xst[i][:, :], Txst[i], junk[:, :], Tjunk, st, Tst, 4 + 2 * (blk % 8),
                                self.gbuf[:], self.T_g, hnb=hnb[i][:, :], Thnb=Thnb[i])
                self.transpose_block(hnb[i], Thnb[i], blk, 6 + i)
            return

        posi = self.carve(41472, [128, 512], I32)
        Tpi = T()
        posf = self.carve(43520, [128, 512], F32)
        ang = self.carve(45568, [128, 512], F32)
        kf = self.carve(47616, [128, 512], F32)
        ki = self.carve(49664, [128, 384], I32)
        Trope = T()
        invf = self.cst[:, 0:1]
        self.memset("dve", self.cst[:, 3:4], math.pi / 2, [Tc])
        for t0 in range(0, TOK_ALL, 384):
            n = min(384, TOK_ALL - t0)
            self.dma(posi[:, 0:n], self.bc(self.posl[:, t0:t0 + n]), [], [Tpi])
            self.cp("dve", posf[:, 0:n], posi[:, 0:n], [Tpi], [Trope])
            for which, dst in ((0, self.sin), (1, self.cos)):
                if which == 0:
                    self.ts("dve", ang[:, 0:n], posf[:, 0:n], invf, ALU.mult, [Trope, Tc], [Trope])
                else:
                    self.ts("dve", ang[:, 0:n], posf[:, 0:n], invf, ALU.mult, [Trope, Tc], [Trope],
                            s2=self.cst[:, 3:4], op1=ALU.add)
                self.ts("dve", ki[:, 0:n], ang[:, 0:n], 1.0 / TWO_PI, ALU.mult, [Trope], [Trope])
                self.cp("dve", kf[:, 0:n], ki[:, 0:n], [Trope], [Trope])
                self.stt(ang[:, 0:n], kf[:, 0:n], -CW1, ang[:, 0:n], ALU.mult, ALU.add, [Trope], [Trope])
                self.stt(ang[:, 0:n], kf[:, 0:n], -CW2, ang[:, 0:n], ALU.mult, ALU.add, [Trope], [Trope])
                self.ts("dve", ang[:, 0:n], ang[:, 0:n], -3.1415925, ALU.max, [Trope], [Trope],
                        s2=3.1415925, op1=ALU.min)
                self.act(dst[:, t0:t0 + n], ang[:, 0:n], AF.Sin, [Trope], [self.T_cs])

        self.dbg("cst", self.cst[:, :], [Tc])
        self.dbg("cos", self.cos[:, 0:512], [self.T_cs])
        self.dbg("sin", self.sin[:, 0:512], [self.T_cs])
        self.dma(self.gbuf[:], self.bc(self.v_lng[0:1, :]), [Tmem], [self.T_g])
        for blk in range(32):
            i = blk % 2
            self.dma(xst[i][:, :], self.xl[blk * 128:(blk + 1) * 128, :], [], [Txst[i]])
            self.norm_block(xst[i][:, :], Txst[i], junk[:, :], Tjunk, st, Tst, 4 + 2 * (blk % 8),
                            self.gbuf[:], self.T_g, hnb=hnb[i][:, :], Thnb=Thnb[i])
            self.transpose_block(hnb[i], Thnb[i], blk, 6 + i)

    def _l0_mixer(self):
        Tc = self.T_const
        kTz = self.carve(0, [128, 2, TOK_ALL], BF16)
        qT = self.carve(16384, [128, TOK_OWN], BF16)
        vh = self.carve(20736, [128, 32, 132], BF16)
        xb = [self.carve(29184 + i * 512, [128, 256], BF16) for i in range(4)]
        t1 = [self.carve(31232 + i * 1024, [128, 256], F32) for i in range(4)]
        t2 = [self.carve(35328 + i * 1024, [128, 256], F32) for i in range(4)]
        NPT = 6
        pt = [self.carve(39424 + i * 1024, [128, 512], BF16) for i in range(NPT)]
        osb = self.carve(45568, [128, 3, 396], F32)
        yst = [self.carve(50320 + i * 1024, [128, 4, 128], BF16) for i in range(2)]
        stt_ = self.carve(52368, [128, 64], F32)
        yq = self.carve(52624, [128, 128], F32)
        T_kT = [T() for _ in range(9)]
        T_qT = [T() for _ in range(5)]
        T_vh = [T() for _ in range(8)]
        T_xb = [T() for _ in range(4)]
        T_t1 = [T() for _ in range(4)]
        T_t2 = [T() for _ in range(4)]
        T_pt = [T() for _ in range(NPT)]
        T_osb = T()
        T_yst = [T(), T()]
        T_st = T()
        T_yq = T()
        T_vone = T()

        self.memset("pool", kTz[64:128, 0, :], 0.0, [T_vone])
        self.memset("pool", kTz[0:64, 1, :], 0.0, [T_vone])
        self.memset("pool", vh[:, :, 128:129], 1.0, [T_vone])

        def load_w(h):
            s = h % 2
            self.dma(self.wslot[s][:], self.w_h0[h].rearrange("p (c n) -> p c n", c=8), [], [self.T_ws[s]], q="pool")

        load_w(0)
        allg = OWN_GROUPS + CTX_GROUPS
        LA = 4
        NS = 5
        ptr = 0
        obank = lambda a: 5 + a // 3
        oacc = lambda a: self.ps[obank(a)][:, (a % 3) * 132:(a % 3) * 132 + 129]

        def kblock_T(kb):
            t0 = kb * 128
            for gi_, (g0, gn) in enumerate(allg):
                if g0 <= t0 < g0 + gn:
                    return [T_kT[gi_], T_vh[kb // 4]]
            raise AssertionError

        for h in range(NH):
            s = h % 2
            W = self.wslot[s]
            Tw = self.T_ws[s]
            if h + 1 < NH:
                load_w(h + 1)
            jobs = []
            for kind, glist in (("q", list(range(len(OWN_GROUPS)))), ("k", list(range(9)))):
                for g in glist:
                    g0, gn = allg[g]
                    for off in range(0, gn, 256):
                        jobs.append((kind, g, g0 + off, min(256, gn - off)))

            def front(j):
                kind, g, t0, n = jobs[j]
                col0 = 0 if kind == "q" else 128
                i = j % 4
                pa = i
                for c in range(8):
                    self.mm(self.ps[pa][:, 0:n], W[:, c, col0:col0 + 128], self.hn_ap(c, t0, n),
                            c == 0, c == 7, [Tw] + self.hn_T(t0, n), [self.T_ps[pa]])
                self.cp("act", xb[i][:, 0:n], self.ps[pa][:, 0:n], [self.T_ps[pa]], [T_xb[i]])

            def back(j):
                kind, g, t0, n = jobs[j]
                i = j % 4
                pa, pb = i, 4 + i
                self.mm(self.ps[pb][:, 0:n], self.rmatb[:], xb[i][:, 0:n], True, True, [Tc, T_xb[i]],
                        [self.T_ps[pb]])
                self.tt("dve", t1[i][:, 0:n], self.ps[pa][:, 0:n], self.cos[:, t0:t0 + n], ALU.mult,
                        [self.T_ps[pa], self.T_cs], [T_t1[i]])
                self.tt("dve", t2[i][:, 0:n], self.ps[pb][:, 0:n], self.sin[:, t0:t0 + n], ALU.mult,
                        [self.T_ps[pb], self.T_cs], [T_t2[i]])
                if kind == "q":
                    self.tt("pool", qT[:, t0:t0 + n], t1[i][:, 0:n], t2[i][:, 0:n], ALU.add,
                            [T_t1[i], T_t2[i]], [T_qT[g]])
                else:
                    self.tt("pool", kTz[0:64, 0, t0:t0 + n], t1[i][0:64, 0:n], t2[i][0:64, 0:n], ALU.add,
                            [T_t1[i], T_t2[i], T_vone], [T_kT[g]])
                    self.tt("pool", kTz[64:128, 1, t0:t0 + n], t1[i][64:128, 0:n], t2[i][64:128, 0:n], ALU.add,
                            [T_t1[i], T_t2[i], T_vone], [T_kT[g]])

            front(0)
            for j in range(len(jobs)):
                if j + 1 < len(jobs):
                    front(j + 1)
                back(j)
            for vb in range(8):
                pi = 4 + (vb % 2)
                for j in range(4):
                    blk = vb * 4 + j
                    for c in range(8):
                        self.mm(self.ps[pi][:, j * 128:(j + 1) * 128], self.hn_ap(c, blk * 128, 128),
                                W[:, c, 256:384], c == 0, c == 7, [Tw, self.T_hn[blk]], [self.T_ps[pi]], skip=True)
                eng = "act" if vb % 2 == 0 else "dve"
                self.cp(eng, vh[:, vb * 4:(vb + 1) * 4, 0:128],
                        self.ps[pi][:, :].rearrange("p (j e) -> p j e", j=4), [self.T_ps[pi], T_vone], [T_vh[vb]])
            if h == NH - 1:
                self.load_epi_weights(0, ("gt", "wo"))
            if h == 0:
                self.dbg("qT", qT[:, :], T_qT)
                self.dbg("kTz", kTz[:, :, 0:512], T_kT)
                self.dbg("vh", vh[:, 0:4, :], T_vh)

            steps = []
            for gi, (q0, nq, nctx, btab) in enumerate(QGROUPS):
                klist = []
                for sl in range(nctx):
                    bcol = 8 + (sl if btab == 0 else 7 + sl)
                    klist.append((NB_OWN + sl, self.cst[:, bcol:bcol + 1], None, 0))
                for m in range(q0 + nq):
                    if m < q0:
                        klist.append((m, None, None, 0))
                    else:
                        klist.append((m, None, (m - q0) * 128, (m - q0) * 128))
                nk = len(klist)
                for ki_, (kb, bias_ap, dcol, c0) in enumerate(klist):
                    for mp in range(2):
                        steps.append(dict(gi=gi, ki=ki_, nk=nk, kb=kb, bias=bias_ap, dcol=dcol, c0=c0, mp=mp,
                                          last=(ki_ == nk - 1 and mp == 1)))
            started = {}

            def s_front(idx):
                nonlocal ptr
                st_ = steps[idx]
                q0, nq, nctx, btab = QGROUPS[st_["gi"]]
                ncols = nq * 128
                c0, kb, mp = st_["c0"], st_["kb"], st_["mp"]
                qg_T = []
                for (g0, gn), tq in zip(OWN_GROUPS, T_qT):
                    if g0 < (q0 + nq) * 128 and q0 * 128 < g0 + gn:
                        qg_T.append(tq)
                kT_T = kblock_T(kb)
                si = idx % NS
                pti = ptr % NPT
                ptr += 1
                st_["pti"] = pti
                self.mm(self.ps[si][:, c0:ncols], kTz[:, mp, kb * 128:(kb + 1) * 128],
                        qT[:, q0 * 128 + c0:q0 * 128 + ncols], True, True,
                        [kT_T[0], T_vone] + qg_T, [self.T_ps[si]])
                if st_["bias"] is not None:
                    self.act(pt[pti][:, c0:ncols], self.ps[si][:, c0:ncols], AF.Exp,
                             [self.T_ps[si], Tc], [T_pt[pti]], bias=st_["bias"], scale=0.125)
                else:
                    self.act(pt[pti][:, c0:ncols], self.ps[si][:, c0:ncols], AF.Exp,
                             [self.T_ps[si]], [T_pt[pti]], scale=0.125)
                dcol = st_["dcol"]
                if dcol is not None:
                    self.tt("pool", pt[pti][:, dcol:dcol + 128], pt[pti][:, dcol:dcol + 128],
                            self.trib[:], ALU.mult, [T_pt[pti], Tc], [T_pt[pti]])

            def s_back(idx):
                st_ = steps[idx]
                gi = st_["gi"]
                q0, nq, nctx, btab = QGROUPS[gi]
                c0, kb, mp, pti = st_["c0"], st_["kb"], st_["mp"], st_["pti"]
                kT_T = kblock_T(kb)
                stt_set = started.setdefault(gi, set())
                for qb in range(c0 // 128, nq):
                    a = mp * 4 + qb
                    bk = obank(a)
                    first = bk not in stt_set
                    stt_set.add(bk)
                    self.mm(oacc(a), pt[pti][:, qb * 128:(qb + 1) * 128], vh[:, kb, 0:129],
                            first, st_["ki"] == st_["nk"] - 1, [T_pt[pti], kT_T[1], T_vone], [self.T_ps[bk]],
                            skip=True)
                if st_["last"]:
                    finalize(gi)

            def finalize(gi):
                q0, nq, nctx, btab = QGROUPS[gi]
                for bk in range(3):
                    used = 396 if (bk + 1) * 3 <= 8 else 264
                    self.cp("dve", osb[:, bk, 0:used], self.ps[5 + bk][:, 0:used], [self.T_ps[5 + bk]], [T_osb])
                osf = osb.rearrange("p b n -> p (b n)")
                acc3 = osf[:, 0:8 * 132].rearrange("p (a n) -> p a n", n=132)
                rl = stt_[:, 0:8]
                self.recip(rl.rearrange("p (a o) -> p a o", o=1), acc3[:, 0:8, 128:129], [T_osb], [T_st])
                self.ts("dve", stt_[:, 4:8], stt_[:, 4:8], self.cst[:, 2:3], ALU.mult, [T_st, Tc], [T_st])
                ysl = yst[gi % 2]
                Tys = T_yst[gi % 2]
                for qb in range(nq):
                    o0 = osf[:, qb * 132:qb * 132 + 128]
                    o1 = osf[:, (4 + qb) * 132:(4 + qb) * 132 + 128]
                    self.ts("dve", o0, o0, stt_[:, qb:qb + 1], ALU.mult, [T_osb, T_st], [T_osb])
                    self.stt(o0, o1, stt_[:, 4 + qb:5 + qb], o0, ALU.mult, ALU.add, [T_osb, T_st], [T_osb])
                yv = acc3[:, 0:nq, 0:128]
                sqv = acc3[:, 4:4 + nq, 0:128]
                self.tt("dve", sqv, yv, yv, ALU.mult, [T_osb], [T_osb])
                self.p.add("dve", lambda e, o=stt_[:, 16:16 + nq], i_=sqv: e.tensor_reduce(
                    out=o, in_=i_, axis=mybir.AxisListType.X, op=ALU.add), [T_osb], [T_st])
                self.ts("dve", stt_[:, 16:16 + nq], stt_[:, 16:16 + nq], 1.0 / 128, ALU.mult, [T_st], [T_st],
                        s2=EPS, op1=ALU.add)
                self.tt("pool", stt_[:, 24:24 + nq], stt_[:, 16:16 + nq], self.cst[:, 4:4 + nq], ALU.pow,
                        [T_st, Tc], [T_st])
                for qb in range(nq):
                    o0 = osf[:, qb * 132:qb * 132 + 128]
                    self.stt(ysl[:, qb, :], o0, stt_[:, 24 + qb:25 + qb], self.subg[:], ALU.mult, ALU.mult,
                             [T_osb, T_st, Tc], [Tys])
                dst = self.yscr[q0 * 128:(q0 + nq) * 128, h * 128:(h + 1) * 128].rearrange(
                    "(q p) e -> p q e", p=128)
                self.dma(dst, ysl[:, 0:nq, :], [Tys], [self.T_y[b] for b in range(q0, q0 + nq)])

            nst = len(steps)
            for idx in range(nst + LA):
                if idx < nst:
                    s_front(idx)
                if idx - LA >= 0:
                    s_back(idx - LA)

    def _l1_mixer(self):
        Tc = self.T_const
        PADW = 16
        CH = 1152
        U = [self.carve(i * (PADW + CH) * 4, [128, PADW + CH], F32) for i in range(2)]
        A = self.carve(9344, [128, PADW + CH], F32)
        Bf = self.carve(14016, [128, PADW + CH], F32)
        pooledT = self.carve(18688, [128, 3, TOK_OWN], BF16)
        wgp = [self.carve(31744 + i * 2304, [128, 3, 384], BF16) for i in range(2)]
        pscale = self.carve(36352, [128, MIX], F32)
        ysb = [self.carve(42496 + i * 768, [128, 384], BF16) for i in range(2)]
        invc = self.carve(44032, [128, 4, 16], F32)
        T_U = [T(), T()]
        T_A = T()
        T_B = T()
        T_pl = [[T() for _ in range(2)] for _ in range(3)]
        T_wgp = [T(), T()]
        T_psc = T()
        T_ysb = [T(), T()]
        T_inv = T()

        self.dma(pscale[:, :], self.bc(self.v_pscale), [], [T_psc])
        for g, w in enumerate(POOL_WINDOWS):
            for t in range(16):
                self.memset("pool", invc[:, g, t:t + 1], 1.0 / min(t + 1, w), [T_inv])
        for i in range(2):
            self.memset("pool", U[i][:, 0:PADW], 0.0, [T_U[i]])
        self.memset("pool", A[:, 0:PADW], 0.0, [T_A])
        self.memset("pool", Bf[:, 0:PADW], 0.0, [T_B])

        def load_wu(ft):
            s = ft % 2
            self.dma(self.wslot[s][:, :, 0:128], self.w_u[ft].rearrange("p (c n) -> p c n", c=8), [],
                     [self.T_ws[s]], q="pool")

        load_wu(0)
        un = 0
        for g, w in enumerate(POOL_WINDOWS):
            self.dma(wgp[g % 2], self.w_gp[g].rearrange("p (c n) -> p c n", c=3), [], [T_wgp[g % 2]], q="pool")
            for cc in range(3):
                ft = g * 3 + cc
                s = ft % 2
                W = self.wslot[s]
                if ft + 1 < 12:
                    load_wu(ft + 1)
                for ci, (c0, cn) in enumerate(POOL_CHUNKS):
                    u = U[un % 2]
                    Tu = T_U[un % 2]
                    un += 1
                    halo = 0 if c0 == 0 else PADW
                    t = c0 - halo
                    pieces = []
                    while t < c0 + cn:
                        n = min(512, c0 + cn - t)
                        if t < c0:
                            n = halo
                        pieces.append((t, n))
                        t += n
                    for pi_, (t0, n) in enumerate(pieces):
                        pb = pi_ % 2
                        for c in range(8):
                            self.mm(self.ps[pb][:, 0:n], W[:, c, 0:128], self.hn_ap(c, t0, n), c == 0, c == 7,
                                    [self.T_ws[s]] + self.hn_T(t0 - (t0 % 128), 128 * ((t0 % 128 + n + 127) // 128)),
                                    [self.T_ps[pb]])
                        dcol = PADW + (t0 - c0)
                        eng = "act" if pi_ % 2 == 0 else "dve"
                        self.cp(eng, u[:, dcol:dcol + n], self.ps[pb][:, 0:n], [self.T_ps[pb]], [Tu])
                    src, Tsrc = u, Tu
                    bufs = [(A, T_A), (Bf, T_B)]
                    nsteps = g + 1
                    for stp in range(nsteps):
                        sh = 1 << stp
                        dst, Tdst = bufs[stp % 2]
                        eng = "dve" if stp % 2 == 0 else "pool"
                        lo = PADW - (PADW - sh) if False else sh
                        self.tt(eng, dst[:, sh:PADW + cn], src[:, sh:PADW + cn], src[:, 0:PADW + cn - sh], ALU.add,
                                [Tsrc], [Tdst])
                        src, Tsrc = dst, Tdst
                    dstp = pooledT[:, cc, c0:c0 + cn]
                    self.stt(dstp, src[:, PADW:PADW + cn], 1.0 / w, u[:, PADW:PADW + cn], ALU.mult, ALU.subtract,
                             [Tsrc, Tu], [T_pl[cc][ci]])
                    if c0 == 0:
                        tmp = src[:, 0:16]
                        self.tt("dve", tmp, src[:, PADW:PADW + 16], invc[:, g, :], ALU.mult, [Tsrc, T_inv], [Tsrc])
                        self.tt("dve", pooledT[:, cc, 0:16], tmp, u[:, PADW:PADW + 16], ALU.subtract,
                                [Tsrc, Tu], [T_pl[cc][ci]])
            for blk in range(NB_OWN):
                pb = 2 + blk % 2
                ci = 0 if blk < 8 else 1
                for cc in range(3):
                    self.mm(self.ps[pb][:, 0:384], pooledT[:, cc, blk * 128:(blk + 1) * 128], wgp[g % 2][:, cc, :],
                            cc == 0, cc == 2, [T_pl[cc][ci], T_wgp[g % 2]], [self.T_ps[pb]])
                yb = ysb[blk % 2]
                self.tt("dve", yb[:, :], self.ps[pb][:, 0:384], pscale[:, g * 384:(g + 1) * 384], ALU.mult,
                        [self.T_ps[pb], T_psc], [T_ysb[blk % 2]])
                self.dma(self.yscr[blk * 128:(blk + 1) * 128, g * 384:(g + 1) * 384], yb[:, :], [T_ysb[blk % 2]],
                         [self.T_y[blk]])

    def _epilogue(self, L):
        Tc = self.T_const
        last = (L == 1)
        xst = [self.carve(i * 4096, [128, D], F32) for i in range(2)]
        hsb = [self.carve(8192 + i * 4096, [128, D], F32) for i in range(2)]
        hnb = self.carve(16384, [128, D], BF16)
        sg = [self.carve(18432 + i * 4096, [128, 2048], BF16) for i in range(2)]
        z = [self.carve(26624 + i * 4096, [128, 2048], BF16) for i in range(2)]
        zT = self.carve(34816, [128, 16, 128], BF16)
        ysb = [self.carve(38912 + i * 3072, [128, MIX], BF16) for i in range(2)]
        qmT = [self.carve(45056 + i * 1024, [128, 4, 128], BF16) for i in range(2)]
        pm = self.carve(47104, [128, 8, 128], BF16)
        om = self.carve(49152, [128, 4, 132], F32)
        stB = self.carve(51264, [128, 16], F32)
        stC = self.carve(51328, [128, 32], F32)
        junk = zT.rearrange("p e t -> p (e t)")[:, 0:D]
        T_xst = [T(), T()]
        T_hsb = [T(), T()]
        T_hnb = T()
        T_sg = [T(), T()]
        T_z = [T(), T()]
        T_zT = T()
        T_ysb = [T(), T()]
        T_qm = [T(), T()]
        T_pm = T()
        T_stB = T()
        T_stC = T()
        T_om = T()

        self.load_epi_weights(L, ("gt", "qm", "wo"))
        gsrc = self.v_fing[0:1, :] if last else self.v_lng[1:2, :]
        self.dma(self.gbuf[:], self.bc(gsrc), [], [self.T_g])

        def stage_a(blk):
            i = blk % 2
            Thn = self.T_hn[blk]
            rows = slice(blk * 128, (blk + 1) * 128)
            self.dma(ysb[i][:, :], self.yscr[rows, :], [self.T_y[blk]], [T_ysb[i]])
            for hd in range(4):
                for c in range(8):
                    self.mm(self.ps[0][:, hd * 128:(hd + 1) * 128], self.wqm[:, c, hd * 128:(hd + 1) * 128],
                            self.hnT[:, c, rows], c == 0, c == 7, [self.T_wqm, Thn], [self.T_ps[0]], skip=True)
            self.cp("dve", qmT[i][:, :, :], self.ps[0][:, :].rearrange("p (h t) -> p h t", h=4), [self.T_ps[0]],
                    [T_qm[i]])
            for n4 in range(4):
                pi = 1 + n4 % 2
                for c in range(8):
                    self.mm(self.ps[pi][:, :], self.hnT[:, c, rows], self.wgt[:, c, n4 * 512:(n4 + 1) * 512],
                            c == 0, c == 7, [self.T_regB_w, Thn], [self.T_ps[pi]])
                self.act(sg[i][:, n4 * 512:(n4 + 1) * 512], self.ps[pi][:, :], AF.Silu, [self.T_ps[pi]], [T_sg[i]])

        def stage_b(blk):
            i = blk % 2
            rows = slice(blk * 128, (blk + 1) * 128)
            if L == 0 or not self.do_l0:
                self.dma(xst[i][:, :], self.xl[rows, :], [], [T_xst[i]])
            else:
                self.dma(xst[i][:, :], self.h1scr[rows, :], [self.T_h1[blk]], [T_xst[i]])
            for hd in range(4):
                for mc in range(2):
                    pi = 3 + hd // 2
                    col = ((hd % 2) * 2 + mc) * 128
                    self.mm(self.ps[pi][:, col:col + 128], self.kmT[L][:, hd, mc * 128:(mc + 1) * 128],
                            qmT[i][:, hd, :], True, True, [self.T_km[L], T_qm[i]], [self.T_ps[pi]], skip=True)
            for half in range(2):
                pi = 3 + half
                self.act(pm[:, half * 4:(half + 1) * 4, :], self.ps[pi][:, :].rearrange("p (a t) -> p a t", a=4),
                         AF.Exp, [self.T_ps[pi]], [T_pm], scale=128 ** -0.5)
            for hd in range(4):
                pi = 5 + hd // 2
                col = (hd % 2) * 132
                for mc in range(2):
                    self.mm(self.ps[pi][:, col:col + 129], pm[:, hd * 2 + mc, :], self.vm[L][:, mc, hd, 0:129],
                            mc == 0 and hd % 2 == 0, mc == 1, [T_pm, self.T_vm[L]], [self.T_ps[pi]], skip=True)
            for half in range(2):
                self.cp("dve", om[:, half * 2:(half + 1) * 2, :],
                        self.ps[5 + half][:, 0:264].rearrange("p (a n) -> p a n", a=2), [self.T_ps[5 + half]], [T_om])
            self.recip(stB[:, 0:4].rearrange("p (a o) -> p a o", o=1), om[:, :, 128:129], [T_om], [T_stB])
            self.tt("pool", z[i][:, 0:MIX], ysb[i][:, :], sg[i][:, 0:MIX], ALU.mult, [T_ysb[i], T_sg[i]], [T_z[i]])
            for hd in range(4):
                self.stt(z[i][:, MIX + hd * 128:MIX + (hd + 1) * 128], om[:, hd, 0:128], stB[:, hd:hd + 1],
                         sg[i][:, MIX + hd * 128:MIX + (hd + 1) * 128], ALU.mult, ALU.mult,
                         [T_om, T_stB, T_sg[i]], [T_z[i]])

        def stage_c(blk):
            i = blk % 2
            rows = slice(blk * 128, (blk + 1) * 128)
            for half in range(2):
                pi = 7
                psb = self.ps[pi][:].bitcast(BF16)
                for e in range(8):
                    ee = half * 8 + e
                    self.tr(psb[:, e * 128:(e + 1) * 128], z[i][:, ee * 128:(ee + 1) * 128], [T_z[i], Tc],
                            [self.T_ps[pi]])
                eng = "act" if half == 0 else "dve"
                self.cp(eng, zT[:, half * 8:(half + 1) * 8, :], psb.rearrange("p (e t) -> p e t", e=8),
                        [self.T_ps[pi]], [T_zT])
            for n2 in range(2):
                pi = 5 + n2
                for e in range(16):
                    self.mm(self.ps[pi][:, :], zT[:, e, :], self.wo[:, e, n2 * 512:(n2 + 1) * 512], e == 0, e == 15,
                            [T_zT, self.T_cs], [self.T_ps[pi]])
                self.tt("dve", hsb[i][:, n2 * 512:(n2 + 1) * 512], self.ps[pi][:, :],
                        xst[i][:, n2 * 512:(n2 + 1) * 512], ALU.add, [self.T_ps[pi], T_xst[i]], [T_hsb[i]])
            if last:
                self.norm_block(hsb[i][:, :], T_hsb[i], junk, T_zT, stC, T_stC, 2 * (blk % 8), self.gbuf[:],
                                self.T_g, dst_f32=xst[i][:, :], Tdst=T_xst[i])
                self.dma(self.out[rows, :], xst[i][:, :], [T_xst[i]], [self.T_out])
            else:
                if self.h1scr is not None:
                    self.dma(self.h1scr[rows, :], hsb[i][:, :], [T_hsb[i]], [self.T_h1[blk]])
                else:
                    self.dma(self.out[rows, :], hsb[i][:, :], [T_hsb[i]], [self.T_out])
                    return
                self.norm_block(hsb[i][:, :], T_hsb[i], junk, T_zT, stC, T_stC, 2 * (blk % 8), self.gbuf[:],
                                self.T_g, hnb=hnb[:, :], Thnb=T_hnb)
                self.transpose_block(hnb, T_hnb, blk, 0)

        for it in range(NB_OWN + 2):
            if it < NB_OWN:
                stage_a(it)
            if 0 <= it - 1 < NB_OWN:
                stage_b(it - 1)
            if 0 <= it - 2 < NB_OWN:
                stage_c(it - 2)


def _own_ctx_blocks(j):
    if j == 0:
        own = list(range(0, 8)) + list(range(23, 32))
        ctx = list(range(8, 23))
    else:
        own = list(range(7, 24))
        ctx = list(range(0, 7)) + list(range(24, 32))
    return own, ctx


def _blocks_to_rows(blks):
    return np.concatenate([np.arange(b * 128, (b + 1) * 128) for b in blks])


def _consts(j):
    ident = np.eye(128, dtype=np.float32)
    rmat = np.zeros((128, 128), np.float32)
    for f in range(128):
        if (f % 64) < 32:
            rmat[f + 32, f] = -1.0
        else:
            rmat[f - 32, f] = 1.0
    pp, cc = np.meshgrid(np.arange(128), np.arange(128), indexing="ij")
    tri = (pp <= cc).astype(np.float32)
    inv = (np.float32(10000.0) ** (-np.arange(0, 64, 2, dtype=np.float32) / np.float32(64))).astype(np.float32)
    invf = inv[np.arange(128) % 32].reshape(128, 1).astype(np.float32)
    bias = np.zeros((128, 22), np.float32)
    if j == 0:
        bias[:, 0:7] = NEG
    else:
        bias[:, 7 + 7:22] = NEG
    return ident, rmat, tri, invf, bias


def _prep_weights(inp):
    f = np.float32
    aw = np.asarray(inp["attn_w_in"], f)[0]
    wq, wk, wv = aw[:, 0:1536], aw[:, 1536:3072], aw[:, 3072:4608]
    w_h0 = np.empty((NH, 128, 8, 384), f)
    for h in range(NH):
        cat = np.concatenate([wq[:, h * 128:(h + 1) * 128], wk[:, h * 128:(h + 1) * 128],
                              wv[:, h * 128:(h + 1) * 128]], axis=1)
        w_h0[h] = cat.reshape(8, 128, 384).transpose(1, 0, 2)
    pw = np.asarray(inp["pool_w_in"], f)[0]

    def pk(w, k):
        n = w.shape[1]
        return np.ascontiguousarray(w.reshape(k, 128, n).transpose(1, 0, 2)).reshape(128, k * n)

    w_qm = np.stack([pk(aw[:, 4608:5120], 8), pk(pw[:, 1536:2048], 8)])
    w_gt = np.stack([pk(aw[:, 5120:7168], 8), pk(pw[:, 2048:4096], 8)])
    wo = np.asarray(inp["w_out"], f)
    w_o = np.stack([pk(wo[0], 16), pk(wo[1], 16)])
    kv = np.asarray(inp["mem_w_kv"], f)
    w_kv = np.stack([pk(kv[0], 8), pk(kv[1], 8)])
    w_u = np.stack([pk(pw[:, ft * 128:(ft + 1) * 128], 8) for ft in range(12)])
    gp = np.asarray(inp["pool_w_group"], f)[0]
    w_gp = np.stack([pk(gp[g], 3) for g in range(4)])
    return dict(w_h0=np.ascontiguousarray(w_h0).reshape(NH, 128, 8 * 384), w_qm=w_qm, w_gt=w_gt, w_o=w_o,
                w_kv=w_kv, w_u=w_u, w_gp=w_gp)


_NC_CACHE = {}


def _get_nc(do_l0, do_l1):
    key = (do_l0, do_l1)
    if key not in _NC_CACHE:
        lam0 = 0.8 - 0.6 * math.exp(-0.3 * 0)
        _NC_CACHE[key] = Builder(do_l0, do_l1, lam0).build()
    return _NC_CACHE[key]


def _in_maps(inp, xsrc, full_tokens):
    f = np.float32
    wts = _prep_weights(inp)
    x = np.asarray(xsrc, f)
    mem = np.asarray(inp["mem"], f)
    pos = np.asarray(inp["positions"], np.int32)
    maps = []
    for core in range(8):
        b, j = core // 2, core % 2
        own, ctx = _own_ctx_blocks(j)
        rows_own = _blocks_to_rows(own)
        rows_all = np.concatenate([rows_own, _blocks_to_rows(ctx)])
        ident, rmat, tri, invf, bias = _consts(j)
        m = dict(wts)
        if full_tokens:
            m["xl"] = np.ascontiguousarray(x[b][rows_all])
        else:
            m["xl"] = np.ascontiguousarray(x[core])
        m["posl"] = np.ascontiguousarray(pos[b][rows_all]).reshape(1, TOK_ALL)
        m["meml"] = np.ascontiguousarray(mem[b])
        m["c_ident"] = ident
        m["c_rmat"] = rmat
        m["c_tri"] = tri
        m["c_invf"] = invf
        m["c_bias"] = bias
        m["v_lng"] = np.asarray(inp["ln_g"], f)
        m["v_memg"] = np.asarray(inp["mem_norm_g"], f).reshape(1, D)
        m["v_fing"] = np.asarray(inp["final_g"], f).reshape(1, D)
        m["v_subg"] = np.asarray(inp["attn_subln_g"], f).reshape(1, 128)
        m["v_pscale"] = np.asarray(inp["pool_scale"], f).reshape(1, MIX)
        m["v_lam"] = np.asarray(inp["attn_lambda"], f).reshape(1, 256)
        maps.append(m)
    return maps


def _assemble(res):
    out = np.empty((4, TOK_ALL, D), np.float32)
    for core in range(8):
        b, j = core // 2, core % 2
        own, _ = _own_ctx_blocks(j)
        o = res[core]["out"]
        for li, gb in enumerate(own):
            if j == 0 and li == 8:
                continue
            if j == 1 and li == 0:
                continue
            out[b, gb * 128:(gb + 1) * 128] = o[li * 128:(li + 1) * 128]
    return out


FUSED = True


def kernel(**inputs):
    if FUSED:
        nc = _get_nc(True, True)
        res = run_bass_kernel_spmd(nc, _in_maps(inputs, inputs["x"], True), core_ids=list(range(8)))
        return _assemble(res.results)
    nc0 = _get_nc(True, False)
    r0 = run_bass_kernel_spmd(nc0, _in_maps(inputs, inputs["x"], True), core_ids=list(range(8)))
    h1 = [r0.results[c]["out"] for c in range(8)]
    nc1 = _get_nc(False, True)
    r1 = run_bass_kernel_spmd(nc1, _in_maps(inputs, h1, False), core_ids=list(range(8)))
    return _assemble(r1.results)
```

```python
import math
from contextlib import ExitStack

import numpy as np
import concourse.bass as bass
import concourse.mybir as mybir
from concourse.bass_utils import run_bass_kernel_spmd

F32 = mybir.dt.float32
BF16 = mybir.dt.bfloat16
I32 = mybir.dt.int32
AF = mybir.ActivationFunctionType
ALU = mybir.AluOpType

NDMASEM = 14
DMA_SLOTS = {"sp": list(range(0, 8)), "pool": list(range(8, 14)), "act": list(range(0, 8))}

D = 1024
NB_OWN = 17
NB_CTX = 15
TOK_OWN = NB_OWN * 128
TOK_CTX = NB_CTX * 128
TOK_ALL = 4096
NH = 12
EPS = 1e-6
MIX = 1536
NEG = -30000.0
TWO_PI = 2.0 * math.pi
CW1 = 6.28125
CW2 = TWO_PI - CW1
OWN_GROUPS = [(0, 512), (512, 512), (1024, 512), (1536, 512), (2048, 128)]
CTX_GROUPS = [(2176, 512), (2688, 512), (3200, 512), (3712, 384)]
QGROUPS = [(0, 4, 7, 0), (4, 4, 7, 0), (8, 4, 15, 1), (12, 4, 15, 1), (16, 1, 15, 1)]
POOL_WINDOWS = (2, 4, 8, 16)
POOL_CHUNKS = [(0, 1024), (1024, 1152)]


class T:
    __slots__ = ("name", "w", "r", "excl")

    def __init__(self, name="", excl=False):
        self.name = name
        self.w = None
        self.r = {}
        self.excl = excl


class Op:
    __slots__ = ("eng", "fn", "deps", "veng", "idx", "signal", "count", "is_dma", "dval")


class Prog:
    ENGS = ("pe", "act", "dve", "pool", "sp")

    def __init__(self):
        self.streams = {e: [] for e in self.ENGS}
        self.ndma = {"sp": 0, "pool": 0, "act": 0}
        self.dma_cnt = [0] * NDMASEM
        self.guard = None

    def add(self, eng, fn, reads=(), writes=(), dma=False):
        o = Op()
        o.eng = eng
        o.fn = fn
        o.is_dma = dma
        o.signal = False
        o.count = 0
        o.dval = 0
        reads = list(reads)
        if self.guard is not None:
            reads.append(self.guard)
        deps = {}
        for t in reads:
            if t.w is not None:
                deps[id(t.w)] = t.w
            if t.excl:
                for p in t.r.values():
                    if p.eng != eng:
                        deps[id(p)] = p
        for t in writes:
            if t.w is not None:
                deps[id(t.w)] = t.w
            for p in t.r.values():
                deps[id(p)] = p
        o.deps = list(deps.values())
        if dma:
            slots = DMA_SLOTS[eng]
            slot = slots[self.ndma[eng] % len(slots)]
            self.ndma[eng] += 1
            self.dma_cnt[slot] += 1
            o.veng = ("dma", slot)
            o.dval = 16 * self.dma_cnt[slot]
        else:
            o.veng = eng
        for t in reads:
            t.r[o.veng] = o
        for t in writes:
            t.w = o
            t.r = {}
        o.idx = len(self.streams[eng])
        self.streams[eng].append(o)
        return o

    def emit(self, nc):
        for st in self.streams.values():
            for o in st:
                for d in o.deps:
                    d.signal = True
        for e, st in self.streams.items():
            c = 0
            for o in st:
                if o.is_dma:
                    continue
                if o.signal:
                    c += 1
                o.count = c
        with ExitStack() as es:
            sems = {e: es.enter_context(nc.semaphore("s_" + e)) for e in self.ENGS}
            dsems = [es.enter_context(nc.semaphore("d%d" % i)) for i in range(NDMASEM)]
            block = es.enter_context(nc.Block())
            streams = self.streams
            dma_cnt = self.dma_cnt

            def run(e, eng):
                waited = {}
                for o in streams[e]:
                    need = {}
                    for d in o.deps:
                        if d.is_dma:
                            key = d.veng
                            val = d.dval
                        else:
                            key = d.eng
                            val = d.count
                            if d.eng == e:
                                if e == "pe":
                                    continue
                                if o.idx - d.idx > 3:
                                    continue
                        if need.get(key, 0) < val:
                            need[key] = val
                    if o.is_dma and o.dval > 16:
                        if need.get(o.veng, 0) < o.dval - 16:
                            need[o.veng] = o.dval - 16
                    for key, val in need.items():
                        if waited.get(key, 0) >= val:
                            continue
                        sem = dsems[key[1]] if isinstance(key, tuple) else sems[key]
                        eng.wait_ge(sem, val)
                        waited[key] = val
                    ins = o.fn(eng)
                    if o.is_dma:
                        ins.then_inc(dsems[o.veng[1]], 16)
                    elif o.signal:
                        ins.then_inc(sems[e], 1)
                if e == "sp":
                    for slot in range(NDMASEM):
                        if dma_cnt[slot] > 0:
                            eng.wait_ge(dsems[slot], 16 * dma_cnt[slot])

            @block.tensor
            def _(eng):
                run("pe", eng)

            @block.scalar
            def _(eng):
                run("act", eng)

            @block.vector
            def _(eng):
                run("dve", eng)

            @block.gpsimd
            def _(eng):
                run("pool", eng)

            @block.sync
            def _(eng):
                run("sp", eng)


class Builder:
    def __init__(self, do_l0=True, do_l1=True, lambda_init0=0.2):
        self.do_l0 = do_l0
        self.do_l1 = do_l1
        self.lambda_init0 = lambda_init0
        self.p = Prog()
        self.epi_loaded = set()
        self.nc = bass.Bass("TRN2", target_bir_lowering=False)

    def mm(self, out, lhsT, rhs, start, stop, reads, writes, skip=False):
        self.p.add("pe", lambda e: e.matmul(out, lhsT=lhsT, rhs=rhs, start=start, stop=stop,
                                            skip_group_check=skip), reads, writes)

    def tr(self, out, in_, reads, writes):
        ident = self.identb
        self.p.add("pe", lambda e: e.transpose(out=out, in_=in_, identity=ident[:]), reads, writes)

    def act(self, out, in_, func, reads, writes, bias=None, scale=None, accum=None):
        kw = {}
        if bias is not None:
            kw["bias"] = bias
        if scale is not None:
            kw["scale"] = scale
        if accum is not None:
            kw["accum_out"] = accum
        self.p.add("act", lambda e: e.activation(out=out, in_=in_, func=func, **kw), reads, writes)

    def ts(self, eng, out, in0, s1, op0, reads, writes, s2=None, op1=None):
        if op1 is None:
            self.p.add(eng, lambda e: e.tensor_scalar(out=out, in0=in0, scalar1=s1, scalar2=None, op0=op0),
                       reads, writes)
        else:
            self.p.add(eng, lambda e: e.tensor_scalar(out=out, in0=in0, scalar1=s1, scalar2=s2, op0=op0, op1=op1),
                       reads, writes)

    def tt(self, eng, out, in0, in1, op, reads, writes):
        self.p.add(eng, lambda e: e.tensor_tensor(out=out, in0=in0, in1=in1, op=op), reads, writes)

    def stt(self, out, in0, scalar, in1, op0, op1, reads, writes):
        self.p.add("dve", lambda e: e.scalar_tensor_tensor(out=out, in0=in0, scalar=scalar, in1=in1,
                                                           op0=op0, op1=op1), reads, writes)

    def cp(self, eng, out, in_, reads, writes):
        if eng == "act":
            self.p.add("act", lambda e: e.copy(out=out, in_=in_), reads, writes)
        else:
            self.p.add(eng, lambda e: e.tensor_copy(out=out, in_=in_), reads, writes)

    def memset(self, eng, ap, val, writes):
        self.p.add(eng, lambda e: e.memset(ap, val), (), writes)

    def recip(self, out, in_, reads, writes):
        self.p.add("dve", lambda e: e.reciprocal(out=out, in_=in_), reads, writes)

    def dma(self, out, in_, reads, writes, q="sp"):
        self.p.add(q, lambda e: e.dma_start(out=out, in_=in_), reads, writes, dma=True)

    def bc(self, ap):
        b = ap.partition_broadcast(128)
        return b.rearrange("p o n -> p (o n)")

    def dbg(self, name, ap, reads):
        import os
        if not os.environ.get("KDBG"):
            return
        shape = [int(x) for x in ap.shape]
        d = self.nc.dram_tensor("dbg_" + name, shape, ap.dtype, kind="ExternalOutput").ap()
        self.dma(d, ap, reads, [T()])

    def load_epi_weights(self, L, which):
        g = self.p.guard
        self.p.guard = None
        for w in which:
            if (L, w) in self.epi_loaded:
                continue
            self.epi_loaded.add((L, w))
            if w == "gt":
                ctxT = [self.T_hn[b] for b in range(17, 32)]
                self.dma(self.regB[:, :].rearrange("p (c n) -> p c n", c=16),
                         self.w_gt[L].rearrange("p (c n) -> p c n", c=16), ctxT, [self.T_regB_w] + ctxT, q="pool")
            elif w == "qm":
                self.dma(self.wqm[:], self.w_qm[L].rearrange("p (c n) -> p c n", c=8), [], [self.T_wqm], q="pool")
            else:
                self.dma(self.wo, self.w_o[L].rearrange("p (e d) -> p e d", e=16), [], [self.T_cs], q="pool")
        self.p.guard = g

    def fence(self):
        g = self.p.guard
        self.p.guard = None
        fz = self.fz
        self.p.add("pool", lambda e: e.memset(fz[:], 0.0), (), [g])
        self.p.guard = g

    def build(self):
        nc = self.nc
        with ExitStack() as es:
            self.es = es
            self._declare(es)
            self._setup()
            self.fence()
            import os
            ph = int(os.environ.get("KPHASE", "9"))
            if self.do_l0 and ph >= 1:
                self.load_epi_weights(0, ("qm",))
                self._l0_mixer()
                self.fence()
                if ph >= 2:
                    self._epilogue(0)
                    if self.do_l1:
                        self.load_epi_weights(1, ("gt", "qm", "wo"))
                    self.fence()
            if self.do_l1:
                self._l1_mixer()
                self.fence()
                self._epilogue(1)
            self.p.emit(nc)
        return nc

    def _declare(self, es):
        nc = self.nc
        di = lambda n, s, d: nc.dram_tensor(n, s, d, kind="ExternalInput").ap()
        self.xl = di("xl", [TOK_ALL if self.do_l0 else TOK_OWN, D], F32)
        self.posl = di("posl", [1, TOK_ALL], I32)
        self.meml = di("meml", [256, D], F32)
        self.c_ident = di("c_ident", [128, 128], F32)
        self.c_rmat = di("c_rmat", [128, 128], F32)
        self.c_tri = di("c_tri", [128, 128], F32)
        self.c_invf = di("c_invf", [128, 1], F32)
        self.c_bias = di("c_bias", [128, 22], F32)
        self.v_lng = di("v_lng", [2, D], F32)
        self.v_memg = di("v_memg", [1, D], F32)
        self.v_fing = di("v_fing", [1, D], F32)
        self.v_subg = di("v_subg", [1, 128], F32)
        self.v_pscale = di("v_pscale", [1, MIX], F32)
        self.v_lam = di("v_lam", [1, 256], F32)
        self.w_h0 = di("w_h0", [NH, 128, 8 * 384], F32)
        self.w_qm = di("w_qm", [2, 128, 8 * 512], F32)
        self.w_gt = di("w_gt", [2, 128, 8 * 2048], F32)
        self.w_o = di("w_o", [2, 128, 16 * 1024], F32)
        self.w_kv = di("w_kv", [2, 128, 8 * 1024], F32)
        self.w_u = di("w_u", [12, 128, 8 * 128], F32)
        self.w_gp = di("w_gp", [4, 128, 3 * 384], F32)
        self.out = nc.dram_tensor("out", [TOK_OWN, D], F32, kind="ExternalOutput").ap()
        import os
        if os.environ.get("KDUMP"):
            self.yscr = nc.dram_tensor("yscr", [TOK_OWN, MIX], BF16, kind="ExternalOutput").ap()
        else:
            self.yscr = nc.dram_tensor("yscr", [TOK_OWN, MIX], BF16).ap()
        if self.do_l0 and self.do_l1:
            self.h1scr = nc.dram_tensor("h1scr", [TOK_OWN, D], F32).ap()
        else:
            self.h1scr = None

        sb = lambda n, s, d: es.enter_context(nc.sbuf_tensor(n, s, d))
        self.hnT = sb("hnT", [128, 8, TOK_OWN], BF16)
        self.regB = sb("regB", [128, 16384], BF16)
        self.regC = sb("regC", [128, 8192], F32)
        self.regD = sb("regD", [128, 13440], F32)
        self.wslot = [sb("wslot%d" % i, [128, 8, 384], BF16) for i in range(2)]
        self.wqm = sb("wqm", [128, 8, 512], BF16)
        self.kmT = [sb("kmT%d" % i, [128, 4, 256], BF16) for i in range(2)]
        self.vm = [sb("vm%d" % i, [128, 2, 4, 132], BF16) for i in range(2)]
        self.gbuf = sb("gbuf", [128, D], F32)
        self.subg = sb("subg", [128, 128], F32)
        self.identb = sb("identb", [128, 128], BF16)
        self.rmatb = sb("rmatb", [128, 128], BF16)
        self.trib = sb("trib", [128, 128], BF16)
        self.cst = sb("cst", [128, 32], F32)
        self.fz = sb("fz", [128, 2], F32)
        self.ps = [es.enter_context(nc.psum_tensor("ps%d" % i, [128, 512], F32)) for i in range(8)]

        self.hnT_ctx = self.regB[:, 0:8 * TOK_CTX].rearrange("p (c t) -> p c t", c=8)
        self.wgt = self.regB[:, :].rearrange("p (c n) -> p c n", c=8)
        self.cos = self.regC[:, 0:4096]
        self.sin = self.regC[:, 4096:8192]
        self.wo = self.regC[:, :].bitcast(BF16).rearrange("p (e d) -> p e d", e=16)

        self.T_hn = [T("hn%d" % i) for i in range(32)]
        self.T_regB_w = T("wgt")
        self.T_cs = T("cossin/wo")
        self.T_ws = [T("ws0"), T("ws1")]
        self.T_wqm = T("wqm")
        self.T_km = [T(), T()]
        self.T_vm = [T(), T()]
        self.T_g = T("gbuf")
        self.T_const = T("const")
        self.T_ps = [T("ps%d" % i, excl=True) for i in range(8)]
        self.T_y = [T("yscr%d" % i) for i in range(NB_OWN)]
        self.T_h1 = [T("h1scr%d" % i) for i in range(NB_OWN)]
        self.T_out = T("out")
        self.p.guard = T("guard")

    def carve(self, off_bytes, shape, dtype):
        free = list(shape[1:])
        n = int(np.prod(free))
        nbytes = n * (4 if dtype in (F32, I32) else 2)
        assert off_bytes % 4 == 0 and off_bytes + nbytes <= 13440 * 4, (off_bytes, nbytes)
        a = self.regD[:, off_bytes // 4:(off_bytes + nbytes + 3) // 4]
        if dtype != F32:
            a = a.bitcast(dtype)
        a = a[:, 0:n]
        if len(free) == 2:
            return a.rearrange("p (a b) -> p a b", a=free[0])
        if len(free) == 3:
            return a.rearrange("p (a b c) -> p a b c", a=free[0], b=free[1])
        return a

    def hn_ap(self, c, t0, n):
        if t0 < TOK_OWN:
            assert t0 + n <= TOK_OWN
            return self.hnT[:, c, t0:t0 + n]
        return self.hnT_ctx[:, c, t0 - TOK_OWN:t0 - TOK_OWN + n]

    def hn_T(self, t0, n):
        return [self.T_hn[b] for b in range(t0 // 128, (t0 + n) // 128)]

    def norm_block(self, src, Tsrc, junk, Tjunk, st, Tst, col, g, Tg, dst_f32=None, Tdst=None,
                   hnb=None, Thnb=None):
        eps_ap = self.cst[:, 1:2]
        ss = st[:, col:col + 1]
        rs = st[:, col + 1:col + 2]
        self.act(junk, src, AF.Square, [Tsrc], [Tjunk, Tst], accum=ss)
        self.act(rs, ss, AF.Ln, [Tst, self.T_const], [Tst], bias=eps_ap, scale=1.0 / D)
        self.act(rs, rs, AF.Exp, [Tst], [Tst], scale=-0.5)
        if hnb is not None:
            self.stt(hnb, src, rs, g, ALU.mult, ALU.mult, [Tsrc, Tst, Tg], [Thnb])
        if dst_f32 is not None:
            self.stt(dst_f32, src, rs, g, ALU.mult, ALU.mult, [Tsrc, Tst, Tg], [Tdst])

    def transpose_block(self, hnb, Thnb, blk, psi):
        psb = self.ps[psi][:].bitcast(BF16)
        for c in range(8):
            self.tr(psb[:, c * 128:(c + 1) * 128], hnb[:, c * 128:(c + 1) * 128],
                    [Thnb, self.T_const], [self.T_ps[psi]])
        if blk < NB_OWN:
            dst = self.hnT[:, :, blk * 128:(blk + 1) * 128]
        else:
            b = blk - NB_OWN
            dst = self.hnT_ctx[:, :, b * 128:(b + 1) * 128]
        src = psb.rearrange("p (c t) -> p c t", c=8)
        eng = "dve" if blk % 2 == 0 else "act"
        self.cp(eng, dst, src, [self.T_ps[psi]], [self.T_hn[blk]])

    def _setup(self):
        p = self.p
        Tc = self.T_const
        tmpf = self.carve(0, [128, 128], F32)
        Ttmp = T()
        for src, dst in ((self.c_ident, self.identb), (self.c_rmat, self.rmatb), (self.c_tri, self.trib)):
            self.dma(tmpf[:, :], src, [], [Ttmp])
            self.cp("dve", dst[:], tmpf[:, :], [Ttmp], [Tc])
        self.memset("dve", self.cst[:, :], 0.0, [Tc])
        self.memset("dve", self.cst[:, 1:2], EPS, [Tc])
        self.memset("dve", self.cst[:, 4:8], -0.5, [Tc])
        self.dma(self.cst[:, 0:1], self.c_invf, [], [Tc])
        self.dma(self.cst[:, 8:30], self.c_bias, [], [Tc])
        lamt = self.carve(1024, [128, 256], F32)
        lt2 = self.carve(2048, [128, 128], F32)
        Tl = T()
        self.dma(lamt[:, :], self.bc(self.v_lam), [], [Tl])
        self.tt("dve", lt2[:, 0:64], lamt[:, 0:64], lamt[:, 64:128], ALU.mult, [Tl], [Tl])
        self.tt("dve", lt2[:, 64:128], lamt[:, 128:192], lamt[:, 192:256], ALU.mult, [Tl], [Tl])
        st0 = self.carve(3072, [128, 16], F32)
        Tst0 = T()
        self.act(lamt[:, 0:64], lt2[:, 0:64], AF.Copy, [Tl], [Tl, Tst0], accum=st0[:, 0:1])
        self.act(lamt[:, 64:128], lt2[:, 64:128], AF.Copy, [Tl], [Tl, Tst0], accum=st0[:, 1:2])
        self.act(st0[:, 2:4], st0[:, 0:2], AF.Exp, [Tst0], [Tst0])
        self.tt("dve", st0[:, 4:5], st0[:, 3:4], st0[:, 2:3], ALU.subtract, [Tst0], [Tst0])
        self.ts("dve", self.cst[:, 2:3], st0[:, 4:5], -self.lambda_init0, ALU.add, [Tst0], [Tc])
        self.dma(self.subg[:], self.bc(self.v_subg), [], [Tc])
        self.ts("dve", self.subg[:], self.subg[:], 1.0 - self.lambda_init0, ALU.mult, [Tc], [Tc])

        xst = [self.carve(4096 + i * 4096, [128, D], F32) for i in range(2)]
        Txst = [T(), T()]
        hnb = [self.carve(12288 + i * 2048, [128, D], BF16) for i in range(2)]
        Thnb = [T(), T()]
        junk = self.carve(16384, [128, D], F32)
        Tjunk = T()
        st = self.carve(20480, [128, 80], F32)
        Tst = T()
        memT = self.carve(20992, [128, 8, 256], BF16)
        Tmem = T()
        wkv = self.carve(25088, [128, 8, 1024], BF16)
        Twkv = T()
        self.dma(self.gbuf[:], self.bc(self.v_memg), [], [self.T_g])
        for mb in range(2):
            i = mb % 2
            self.dma(xst[i][:, :], self.meml[mb * 128:(mb + 1) * 128, :], [], [Txst[i]])
            self.norm_block(xst[i][:, :], Txst[i], junk[:, :], Tjunk, st, Tst, 2 * mb, self.gbuf[:], self.T_g,
                            hnb=hnb[i][:, :], Thnb=Thnb[i])
            psb = self.ps[mb][:].bitcast(BF16)
            for c in range(8):
                self.tr(psb[:, c * 128:(c + 1) * 128], hnb[i][:, c * 128:(c + 1) * 128], [Thnb[i], Tc],
                        [self.T_ps[mb]])
            self.cp("dve", memT[:, :, mb * 128:(mb + 1) * 128], psb.rearrange("p (c t) -> p c t", c=8),
                    [self.T_ps[mb]], [Tmem])
        for L in range(2):
            if (L == 0 and not self.do_l0) or (L == 1 and not self.do_l1):
                continue
            self.dma(wkv, self.w_kv[L].rearrange("p (c n) -> p c n", c=8), [], [Twkv], q="pool")
            for hd in range(4):
                pi = 2 + (hd % 2)
                for c in range(8):
                    self.mm(self.ps[pi][:, 0:256], wkv[:, c, hd * 128:(hd + 1) * 128], memT[:, c, :],
                            c == 0, c == 7, [Twkv, Tmem], [self.T_ps[pi]])
                self.cp("act", self.kmT[L][:, hd, :], self.ps[pi][:, 0:256], [self.T_ps[pi]], [self.T_km[L]])
            self.memset("pool", self.vm[L][:, :, :, 128:129], 1.0, [self.T_vm[L]])
            for mc in range(2):
                pi = 4 + mc
                for c in range(8):
                    self.mm(self.ps[pi][:, :], memT[:, c, mc * 128:(mc + 1) * 128], wkv[:, c, 512:1024],
                            c == 0, c == 7, [Twkv, Tmem], [self.T_ps[pi]])
                self.cp("dve", self.vm[L][:, mc, :, 0:128], self.ps[pi][:, :].rearrange("p (h e) -> p h e", h=4),
                        [self.T_ps[pi]], [self.T_vm[L]])

        if not self.do_l0:
            self.dma(self.gbuf[:], self.bc(self.v_lng[1:2, :]), [], [self.T_g])
            for blk in range(NB_OWN):
                i = blk % 2
                self.dma(xst[i][:, :], self.xl[blk * 128:(blk + 1) * 128, :], [], [Txst[i]])
                self.norm_block(xst[i][:, :], Txst[i], junk[:, :], Tjunk, st, Tst, 4 + 2 * (blk % 8),
                                self.gbuf[:], self.T_g, hnb=hnb[i][:, :], Thnb=Thnb[i])
                self.transpose_block(hnb[i], Thnb[i], blk, 6 + i)
            return

        posi = self.carve(41472, [128, 512], I32)
        Tpi = T()
        posf = self.carve(43520, [128, 512], F32)
        ang = self.carve(45568, [128, 512], F32)
        kf = self.carve(47616, [128, 512], F32)
        ki = self.carve(49664, [128, 384], I32)
        Trope = T()
        invf = self.cst[:, 0:1]
        self.memset("dve", self.cst[:, 3:4], math.pi / 2, [Tc])
        for t0 in range(0, TOK_ALL, 384):
            n = min(384, TOK_ALL - t0)
            self.dma(posi[:, 0:n], self.bc(self.posl[:, t0:t0 + n]), [], [Tpi])
            self.cp("dve", posf[:, 0:n], posi[:, 0:n], [Tpi], [Trope])
            for which, dst in ((0, self.sin), (1, self.cos)):
                if which == 0:
                    self.ts("dve", ang[:, 0:n], posf[:, 0:n], invf, ALU.mult, [Trope, Tc], [Trope])
                else:
                    self.ts("dve", ang[:, 0:n], posf[:, 0:n], invf, ALU.mult, [Trope, Tc], [Trope],
                            s2=self.cst[:, 3:4], op1=ALU.add)
                self.ts("dve", ki[:, 0:n], ang[:, 0:n], 1.0 / TWO_PI, ALU.mult, [Trope], [Trope])
                self.cp("dve", kf[:, 0:n], ki[:, 0:n], [Trope], [Trope])
                self.stt(ang[:, 0:n], kf[:, 0:n], -CW1, ang[:, 0:n], ALU.mult, ALU.add, [Trope], [Trope])
                self.stt(ang[:, 0:n], kf[:, 0:n], -CW2, ang[:, 0:n], ALU.mult, ALU.add, [Trope], [Trope])
                self.ts("dve", ang[:, 0:n], ang[:, 0:n], -3.1415925, ALU.max, [Trope], [Trope],
                        s2=3.1415925, op1=ALU.min)
                self.act(dst[:, t0:t0 + n], ang[:, 0:n], AF.Sin, [Trope], [self.T_cs])

        self.dbg("cst", self.cst[:, :], [Tc])
        self.dbg("cos", self.cos[:, 0:512], [self.T_cs])
        self.dbg("sin", self.sin[:, 0:512], [self.T_cs])
        self.dma(self.gbuf[:], self.bc(self.v_lng[0:1, :]), [Tmem], [self.T_g])
        for blk in range(32):
            i = blk % 2
            self.dma(xst[i][:, :], self.xl[blk * 128:(blk + 1) * 128, :], [], [Txst[i]])
            self.norm_block(xst[i][:, :], Txst[i], junk[:, :], Tjunk, st, Tst, 4 + 2 * (blk % 8),
                            self.gbuf[:], self.T_g, hnb=hnb[i][:, :], Thnb=Thnb[i])
            self.transpose_block(hnb[i], Thnb[i], blk, 6 + i)

    def _l0_mixer(self):
        Tc = self.T_const
        kTz = self.carve(0, [128, 2, TOK_ALL], BF16)
        qT = self.carve(16384, [128, TOK_OWN], BF16)
        vh = self.carve(20736, [128, 32, 132], BF16)
        xb = [self.carve(29184 + i * 512, [128, 256], BF16) for i in range(4)]
        t1 = [self.carve(31232 + i * 1024, [128, 256], F32) for i in range(4)]
        t2 = [self.carve(35328 + i * 1024, [128, 256], F32) for i in range(4)]
        NPT = 6
        pt = [self.carve(39424 + i * 1024, [128, 512], BF16) for i in range(NPT)]
        osb = self.carve(45568, [128, 3, 396], F32)
        yst = [self.carve(50320 + i * 1024, [128, 4, 128], BF16) for i in range(2)]
        stt_ = self.carve(52368, [128, 64], F32)
        yq = self.carve(52624, [128, 128], F32)
        T_kT = [T() for _ in range(9)]
        T_qT = [T() for _ in range(5)]
        T_vh = [T() for _ in range(8)]
        T_xb = [T() for _ in range(4)]
        T_t1 = [T() for _ in range(4)]
        T_t2 = [T() for _ in range(4)]
        T_pt = [T() for _ in range(NPT)]
        T_osb = T()
        T_yst = [T(), T()]
        T_st = T()
        T_yq = T()
        T_vone = T()

        self.memset("pool", kTz[64:128, 0, :], 0.0, [T_vone])
        self.memset("pool", kTz[0:64, 1, :], 0.0, [T_vone])
        self.memset("pool", vh[:, :, 128:129], 1.0, [T_vone])

        def load_w(h):
            s = h % 2
            self.dma(self.wslot[s][:], self.w_h0[h].rearrange("p (c n) -> p c n", c=8), [], [self.T_ws[s]], q="pool")

        load_w(0)
        allg = OWN_GROUPS + CTX_GROUPS
        LA = 4
        NS = 5
        ptr = 0
        obank = lambda a: 5 + a // 3
        oacc = lambda a: self.ps[obank(a)][:, (a % 3) * 132:(a % 3) * 132 + 129]

        def kblock_T(kb):
            t0 = kb * 128
            for gi_, (g0, gn) in enumerate(allg):
                if g0 <= t0 < g0 + gn:
                    return [T_kT[gi_], T_vh[kb // 4]]
            raise AssertionError

        for h in range(NH):
            s = h % 2
            W = self.wslot[s]
            Tw = self.T_ws[s]
            if h + 1 < NH:
                load_w(h + 1)
            jobs = []
            for kind, glist in (("q", list(range(len(OWN_GROUPS)))), ("k", list(range(9)))):
                for g in glist:
                    g0, gn = allg[g]
                    for off in range(0, gn, 256):
                        jobs.append((kind, g, g0 + off, min(256, gn - off)))

            def front(j):
                kind, g, t0, n = jobs[j]
                col0 = 0 if kind == "q" else 128
                i = j % 4
                pa = i
                for c in range(8):
                    self.mm(self.ps[pa][:, 0:n], W[:, c, col0:col0 + 128], self.hn_ap(c, t0, n),
                            c == 0, c == 7, [Tw] + self.hn_T(t0, n), [self.T_ps[pa]])
                self.cp("act", xb[i][:, 0:n], self.ps[pa][:, 0:n], [self.T_ps[pa]], [T_xb[i]])

            def back(j):
                kind, g, t0, n = jobs[j]
                i = j % 4
                pa, pb = i, 4 + i
                self.mm(self.ps[pb][:, 0:n], self.rmatb[:], xb[i][:, 0:n], True, True, [Tc, T_xb[i]],
                        [self.T_ps[pb]])
                self.tt("dve", t1[i][:, 0:n], self.ps[pa][:, 0:n], self.cos[:, t0:t0 + n], ALU.mult,
                        [self.T_ps[pa], self.T_cs], [T_t1[i]])
                self.tt("dve", t2[i][:, 0:n], self.ps[pb][:, 0:n], self.sin[:, t0:t0 + n], ALU.mult,
                        [self.T_ps[pb], self.T_cs], [T_t2[i]])
                if kind == "q":
                    self.tt("pool", qT[:, t0:t0 + n], t1[i][:, 0:n], t2[i][:, 0:n], ALU.add,
                            [T_t1[i], T_t2[i]], [T_qT[g]])
                else:
                    self.tt("pool", kTz[0:64, 0, t0:t0 + n], t1[i][0:64, 0:n], t2[i][0:64, 0:n], ALU.add,
                            [T_t1[i], T_t2[i], T_vone], [T_kT[g]])
                    self.tt("pool", kTz[64:128, 1, t0:t0 + n], t1[i][64:128, 0:n], t2[i][64:128, 0:n], ALU.add,
                            [T_t1[i], T_t2[i], T_vone], [T_kT[g]])

            front(0)
            for j in range(len(jobs)):
                if j + 1 < len(jobs):
                    front(j + 1)
                back(j)
            for vb in range(8):
                pi = 4 + (vb % 2)
                for j in range(4):
                    blk = vb * 4 + j
                    for c in range(8):
                        self.mm(self.ps[pi][:, j * 128:(j + 1) * 128], self.hn_ap(c, blk * 128, 128),
                                W[:, c, 256:384], c == 0, c == 7, [Tw, self.T_hn[blk]], [self.T_ps[pi]], skip=True)
                eng = "act" if vb % 2 == 0 else "dve"
                self.cp(eng, vh[:, vb * 4:(vb + 1) * 4, 0:128],
                        self.ps[pi][:, :].rearrange("p (j e) -> p j e", j=4), [self.T_ps[pi], T_vone], [T_vh[vb]])
            if h == NH - 1:
                self.load_epi_weights(0, ("gt", "wo"))
            if h == 0:
                self.dbg("qT", qT[:, :], T_qT)
                self.dbg("kTz", kTz[:, :, 0:512], T_kT)
                self.dbg("vh", vh[:, 0:4, :], T_vh)

            steps = []
            for gi, (q0, nq, nctx, btab) in enumerate(QGROUPS):
                klist = []
                for sl in range(nctx):
                    bcol = 8 + (sl if btab == 0 else 7 + sl)
                    klist.append((NB_OWN + sl, self.cst[:, bcol:bcol + 1], None, 0))
                for m in range(q0 + nq):
                    if m < q0:
                        klist.append((m, None, None, 0))
                    else:
                        klist.append((m, None, (m - q0) * 128, (m - q0) * 128))
                nk = len(klist)
                for ki_, (kb, bias_ap, dcol, c0) in enumerate(klist):
                    for mp in range(2):
                        steps.append(dict(gi=gi, ki=ki_, nk=nk, kb=kb, bias=bias_ap, dcol=dcol, c0=c0, mp=mp,
                                          last=(ki_ == nk - 1 and mp == 1)))
            started = {}

            def s_front(idx):
                nonlocal ptr
                st_ = steps[idx]
                q0, nq, nctx, btab = QGROUPS[st_["gi"]]
                ncols = nq * 128
                c0, kb, mp = st_["c0"], st_["kb"], st_["mp"]
                qg_T = []
                for (g0, gn), tq in zip(OWN_GROUPS, T_qT):
                    if g0 < (q0 + nq) * 128 and q0 * 128 < g0 + gn:
                        qg_T.append(tq)
                kT_T = kblock_T(kb)
                si = idx % NS
                pti = ptr % NPT
                ptr += 1
                st_["pti"] = pti
                self.mm(self.ps[si][:, c0:ncols], kTz[:, mp, kb * 128:(kb + 1) * 128],
                        qT[:, q0 * 128 + c0:q0 * 128 + ncols], True, True,
                        [kT_T[0], T_vone] + qg_T, [self.T_ps[si]])
                if st_["bias"] is not None:
                    self.act(pt[pti][:, c0:ncols], self.ps[si][:, c0:ncols], AF.Exp,
                             [self.T_ps[si], Tc], [T_pt[pti]], bias=st_["bias"], scale=0.125)
                else:
                    self.act(pt[pti][:, c0:ncols], self.ps[si][:, c0:ncols], AF.Exp,
                             [self.T_ps[si]], [T_pt[pti]], scale=0.125)
                dcol = st_["dcol"]
                if dcol is not None:
                    self.tt("pool", pt[pti][:, dcol:dcol + 128], pt[pti][:, dcol:dcol + 128],
                            self.trib[:], ALU.mult, [T_pt[pti], Tc], [T_pt[pti]])

            def s_back(idx):
                st_ = steps[idx]
                gi = st_["gi"]
                q0, nq, nctx, btab = QGROUPS[gi]
                c0, kb, mp, pti = st_["c0"], st_["kb"], st_["mp"], st_["pti"]
                kT_T = kblock_T(kb)
                stt_set = started.setdefault(gi, set())
                for qb in range(c0 // 128, nq):
                    a = mp * 4 + qb
                    bk = obank(a)
                    first = bk not in stt_set
                    stt_set.add(bk)
                    self.mm(oacc(a), pt[pti][:, qb * 128:(qb + 1) * 128], vh[:, kb, 0:129],
                            first, st_["ki"] == st_["nk"] - 1, [T_pt[pti], kT_T[1], T_vone], [self.T_ps[bk]],
                            skip=True)
                if st_["last"]:
                    finalize(gi)

            def finalize(gi):
                q0, nq, nctx, btab = QGROUPS[gi]
                for bk in range(3):
                    used = 396 if (bk + 1) * 3 <= 8 else 264
                    self.cp("dve", osb[:, bk, 0:used], self.ps[5 + bk][:, 0:used], [self.T_ps[5 + bk]], [T_osb])
                osf = osb.rearrange("p b n -> p (b n)")
                acc3 = osf[:, 0:8 * 132].rearrange("p (a n) -> p a n", n=132)
                rl = stt_[:, 0:8]
                self.recip(rl.rearrange("p (a o) -> p a o", o=1), acc3[:, 0:8, 128:129], [T_osb], [T_st])
                self.ts("dve", stt_[:, 4:8], stt_[:, 4:8], self.cst[:, 2:3], ALU.mult, [T_st, Tc], [T_st])
                ysl = yst[gi % 2]
                Tys = T_yst[gi % 2]
                for qb in range(nq):
                    o0 = osf[:, qb * 132:qb * 132 + 128]
                    o1 = osf[:, (4 + qb) * 132:(4 + qb) * 132 + 128]
                    self.ts("dve", o0, o0, stt_[:, qb:qb + 1], ALU.mult, [T_osb, T_st], [T_osb])
                    self.stt(o0, o1, stt_[:, 4 + qb:5 + qb], o0, ALU.mult, ALU.add, [T_osb, T_st], [T_osb])
                yv = acc3[:, 0:nq, 0:128]
                sqv = acc3[:, 4:4 + nq, 0:128]
                self.tt("dve", sqv, yv, yv, ALU.mult, [T_osb], [T_osb])
                self.p.add("dve", lambda e, o=stt_[:, 16:16 + nq], i_=sqv: e.tensor_reduce(
                    out=o, in_=i_, axis=mybir.AxisListType.X, op=ALU.add), [T_osb], [T_st])
                self.ts("dve", stt_[:, 16:16 + nq], stt_[:, 16:16 + nq], 1.0 / 128, ALU.mult, [T_st], [T_st],
                        s2=EPS, op1=ALU.add)
                self.tt("pool", stt_[:, 24:24 + nq], stt_[:, 16:16 + nq], self.cst[:, 4:4 + nq], ALU.pow,
                        [T_st, Tc], [T_st])
                for qb in range(nq):
                    o0 = osf[:, qb * 132:qb * 132 + 128]
                    self.stt(ysl[:, qb, :], o0, stt_[:, 24 + qb:25 + qb], self.subg[:], ALU.mult, ALU.mult,
                             [T_osb, T_st, Tc], [Tys])
                dst = self.yscr[q0 * 128:(q0 + nq) * 128, h * 128:(h + 1) * 128].rearrange(
                    "(q p) e -> p q e", p=128)
                self.dma(dst, ysl[:, 0:nq, :], [Tys], [self.T_y[b] for b in range(q0, q0 + nq)])

            nst = len(steps)
            for idx in range(nst + LA):
                if idx < nst:
                    s_front(idx)
                if idx - LA >= 0:
                    s_back(idx - LA)

    def _l1_mixer(self):
        Tc = self.T_const
        PADW = 16
        CH = 1152
        U = [self.carve(i * (PADW + CH) * 4, [128, PADW + CH], F32) for i in range(2)]
        A = self.carve(9344, [128, PADW + CH], F32)
        Bf = self.carve(14016, [128, PADW + CH], F32)
        pooledT = self.carve(18688, [128, 3, TOK_OWN], BF16)
        wgp = [self.carve(31744 + i * 2304, [128, 3, 384], BF16) for i in range(2)]
        pscale = self.carve(36352, [128, MIX], F32)
        ysb = [self.carve(42496 + i * 768, [128, 384], BF16) for i in range(2)]
        invc = self.carve(44032, [128, 4, 16], F32)
        T_U = [T(), T()]
        T_A = T()
        T_B = T()
        T_pl = [[T() for _ in range(2)] for _ in range(3)]
        T_wgp = [T(), T()]
        T_psc = T()
        T_ysb = [T(), T()]
        T_inv = T()

        self.dma(pscale[:, :], self.bc(self.v_pscale), [], [T_psc])
        for g, w in enumerate(POOL_WINDOWS):
            for t in range(16):
                self.memset("pool", invc[:, g, t:t + 1], 1.0 / min(t + 1, w), [T_inv])
        for i in range(2):
            self.memset("pool", U[i][:, 0:PADW], 0.0, [T_U[i]])
        self.memset("pool", A[:, 0:PADW], 0.0, [T_A])
        self.memset("pool", Bf[:, 0:PADW], 0.0, [T_B])

        def load_wu(ft):
            s = ft % 2
            self.dma(self.wslot[s][:, :, 0:128], self.w_u[ft].rearrange("p (c n) -> p c n", c=8), [],
                     [self.T_ws[s]], q="pool")

        load_wu(0)
        un = 0
        for g, w in enumerate(POOL_WINDOWS):
            self.dma(wgp[g % 2], self.w_gp[g].rearrange("p (c n) -> p c n", c=3), [], [T_wgp[g % 2]], q="pool")
            for cc in range(3):
                ft = g * 3 + cc
                s = ft % 2
                W = self.wslot[s]
                if ft + 1 < 12:
                    load_wu(ft + 1)
                for ci, (c0, cn) in enumerate(POOL_CHUNKS):
                    u = U[un % 2]
                    Tu = T_U[un % 2]
                    un += 1
                    halo = 0 if c0 == 0 else PADW
                    t = c0 - halo
                    pieces = []
                    while t < c0 + cn:
                        n = min(512, c0 + cn - t)
                        if t < c0:
                            n = halo
                        pieces.append((t, n))
                        t += n
                    for pi_, (t0, n) in enumerate(pieces):
                        pb = pi_ % 2
                        for c in range(8):
                            self.mm(self.ps[pb][:, 0:n], W[:, c, 0:128], self.hn_ap(c, t0, n), c == 0, c == 7,
                                    [self.T_ws[s]] + self.hn_T(t0 - (t0 % 128), 128 * ((t0 % 128 + n + 127) // 128)),
                                    [self.T_ps[pb]])
                        dcol = PADW + (t0 - c0)
                        eng = "act" if pi_ % 2 == 0 else "dve"
                        self.cp(eng, u[:, dcol:dcol + n], self.ps[pb][:, 0:n], [self.T_ps[pb]], [Tu])
                    src, Tsrc = u, Tu
                    bufs = [(A, T_A), (Bf, T_B)]
                    nsteps = g + 1
                    for stp in range(nsteps):
                        sh = 1 << stp
                        dst, Tdst = bufs[stp % 2]
                        eng = "dve" if stp % 2 == 0 else "pool"
                        lo = PADW - (PADW - sh) if False else sh
                        self.tt(eng, dst[:, sh:PADW + cn], src[:, sh:PADW + cn], src[:, 0:PADW + cn - sh], ALU.add,
                                [Tsrc], [Tdst])
                        src, Tsrc = dst, Tdst
                    dstp = pooledT[:, cc, c0:c0 + cn]
                    self.stt(dstp, src[:, PADW:PADW + cn], 1.0 / w, u[:, PADW:PADW + cn], ALU.mult, ALU.subtract,
                             [Tsrc, Tu], [T_pl[cc][ci]])
                    if c0 == 0:
                        tmp = src[:, 0:16]
                        self.tt("dve", tmp, src[:, PADW:PADW + 16], invc[:, g, :], ALU.mult, [Tsrc, T_inv], [Tsrc])
                        self.tt("dve", pooledT[:, cc, 0:16], tmp, u[:, PADW:PADW + 16], ALU.subtract,
                                [Tsrc, Tu], [T_pl[cc][ci]])
            for blk in range(NB_OWN):
                pb = 2 + blk % 2
                ci = 0 if blk < 8 else 1
                for cc in range(3):
                    self.mm(self.ps[pb][:, 0:384], pooledT[:, cc, blk * 128:(blk + 1) * 128], wgp[g % 2][:, cc, :],
                            cc == 0, cc == 2, [T_pl[cc][ci], T_wgp[g % 2]], [self.T_ps[pb]])
                yb = ysb[blk % 2]
                self.tt("dve", yb[:, :], self.ps[pb][:, 0:384], pscale[:, g * 384:(g + 1) * 384], ALU.mult,
                        [self.T_ps[pb], T_psc], [T_ysb[blk % 2]])
                self.dma(self.yscr[blk * 128:(blk + 1) * 128, g * 384:(g + 1) * 384], yb[:, :], [T_ysb[blk % 2]],
                         [self.T_y[blk]])

    def _epilogue(self, L):
        Tc = self.T_const
        last = (L == 1)
        xst = [self.carve(i * 4096, [128, D], F32) for i in range(2)]
        hsb = [self.carve(8192 + i * 4096, [128, D], F32) for i in range(2)]
        hnb = [self.carve(16384, [128, D], BF16), self.carve(51456, [128, D], BF16)]
        sg = [self.carve(18432 + i * 4096, [128, 2048], BF16) for i in range(2)]
        z = [self.carve(26624 + i * 4096, [128, 2048], BF16) for i in range(2)]
        zT = self.carve(34816, [128, 16, 128], BF16)
        ysb = [self.carve(38912 + i * 3072, [128, MIX], BF16) for i in range(2)]
        qmT = [self.carve(45056 + i * 1024, [128, 4, 128], BF16) for i in range(2)]
        pm = self.carve(47104, [128, 8, 128], BF16)
        om = self.carve(49152, [128, 4, 132], F32)
        stB = self.carve(51264, [128, 16], F32)
        stC = self.carve(51328, [128, 32], F32)
        junk = zT.rearrange("p e t -> p (e t)")[:, 0:D]
        T_xst = [T(), T()]
        T_hsb = [T(), T()]
        T_hnb = [T(), T()]
        T_sg = [T(), T()]
        T_z = [T(), T()]
        T_zT = T()
        T_ysb = [T(), T()]
        T_qm = [T(), T()]
        T_pm = T()
        T_stB = T()
        T_stC = T()
        T_om = T()

        self.load_epi_weights(L, ("gt", "qm", "wo"))
        gsrc = self.v_fing[0:1, :] if last else self.v_lng[1:2, :]
        self.dma(self.gbuf[:], self.bc(gsrc), [], [self.T_g])

        def stage_a(blk):
            i = blk % 2
            Thn = self.T_hn[blk]
            rows = slice(blk * 128, (blk + 1) * 128)
            self.dma(ysb[i][:, :], self.yscr[rows, :], [self.T_y[blk]], [T_ysb[i]])
            for hd in range(4):
                for c in range(8):
                    self.mm(self.ps[0][:, hd * 128:(hd + 1) * 128], self.wqm[:, c, hd * 128:(hd + 1) * 128],
                            self.hnT[:, c, rows], c == 0, c == 7, [self.T_wqm, Thn], [self.T_ps[0]], skip=True)
            self.cp("dve", qmT[i][:, :, :], self.ps[0][:, :].rearrange("p (h t) -> p h t", h=4), [self.T_ps[0]],
                    [T_qm[i]])
            for n4 in range(4):
                pi = 1 + n4 % 2
                for c in range(8):
                    self.mm(self.ps[pi][:, :], self.hnT[:, c, rows], self.wgt[:, c, n4 * 512:(n4 + 1) * 512],
                            c == 0, c == 7, [self.T_regB_w, Thn], [self.T_ps[pi]])
                self.act(sg[i][:, n4 * 512:(n4 + 1) * 512], self.ps[pi][:, :], AF.Silu, [self.T_ps[pi]], [T_sg[i]])

        def stage_b(blk):
            i = blk % 2
            rows = slice(blk * 128, (blk + 1) * 128)
            if L == 0 or not self.do_l0:
                self.dma(xst[i][:, :], self.xl[rows, :], [], [T_xst[i]])
            else:
                self.dma(xst[i][:, :], self.h1scr[rows, :], [self.T_h1[blk]], [T_xst[i]])
            for hd in range(4):
                for mc in range(2):
                    pi = 3 + hd // 2
                    col = ((hd % 2) * 2 + mc) * 128
                    self.mm(self.ps[pi][:, col:col + 128], self.kmT[L][:, hd, mc * 128:(mc + 1) * 128],
                            qmT[i][:, hd, :], True, True, [self.T_km[L], T_qm[i]], [self.T_ps[pi]], skip=True)
            for half in range(2):
                pi = 3 + half
                self.act(pm[:, half * 4:(half + 1) * 4, :], self.ps[pi][:, :].rearrange("p (a t) -> p a t", a=4),
                         AF.Exp, [self.T_ps[pi]], [T_pm], scale=128 ** -0.5)
            for hd in range(4):
                pi = 5 + hd // 2
                col = (hd % 2) * 132
                for mc in range(2):
                    self.mm(self.ps[pi][:, col:col + 129], pm[:, hd * 2 + mc, :], self.vm[L][:, mc, hd, 0:129],
                            mc == 0 and hd % 2 == 0, mc == 1, [T_pm, self.T_vm[L]], [self.T_ps[pi]], skip=True)
            for half in range(2):
                self.cp("dve", om[:, half * 2:(half + 1) * 2, :],
                        self.ps[5 + half][:, 0:264].rearrange("p (a n) -> p a n", a=2), [self.T_ps[5 + half]], [T_om])
            self.recip(stB[:, 0:4].rearrange("p (a o) -> p a o", o=1), om[:, :, 128:129], [T_om], [T_stB])
            self.tt("pool", z[i][:, 0:MIX], ysb[i][:, :], sg[i][:, 0:MIX], ALU.mult, [T_ysb[i], T_sg[i]], [T_z[i]])
            for hd in range(4):
                self.stt(z[i][:, MIX + hd * 128:MIX + (hd + 1) * 128], om[:, hd, 0:128], stB[:, hd:hd + 1],
                         sg[i][:, MIX + hd * 128:MIX + (hd + 1) * 128], ALU.mult, ALU.mult,
                         [T_om, T_stB, T_sg[i]], [T_z[i]])

        def stage_c(blk):
            i = blk % 2
            rows = slice(blk * 128, (blk + 1) * 128)
            for half in range(2):
                pi = 7
                psb = self.ps[pi][:].bitcast(BF16)
                for e in range(8):
                    ee = half * 8 + e
                    self.tr(psb[:, e * 128:(e + 1) * 128], z[i][:, ee * 128:(ee + 1) * 128], [T_z[i], Tc],
                            [self.T_ps[pi]])
                eng = "act" if half == 0 else "dve"
                self.cp(eng, zT[:, half * 8:(half + 1) * 8, :], psb.rearrange("p (e t) -> p e t", e=8),
                        [self.T_ps[pi]], [T_zT])
            for n2 in range(2):
                pi = 5 + n2
                for e in range(16):
                    self.mm(self.ps[pi][:, :], zT[:, e, :], self.wo[:, e, n2 * 512:(n2 + 1) * 512], e == 0, e == 15,
                            [T_zT, self.T_cs], [self.T_ps[pi]])
                self.tt("dve", hsb[i][:, n2 * 512:(n2 + 1) * 512], self.ps[pi][:, :],
                        xst[i][:, n2 * 512:(n2 + 1) * 512], ALU.add, [self.T_ps[pi], T_xst[i]], [T_hsb[i]])
            if last:
                self.norm_block(hsb[i][:, :], T_hsb[i], junk, T_zT, stC, T_stC, 2 * (blk % 8), self.gbuf[:],
                                self.T_g, dst_f32=xst[i][:, :], Tdst=T_xst[i])
                self.dma(self.out[rows, :], xst[i][:, :], [T_xst[i]], [self.T_out])
            else:
                if self.h1scr is not None:
                    self.dma(self.h1scr[rows, :], hsb[i][:, :], [T_hsb[i]], [self.T_h1[blk]])
                else:
                    self.dma(self.out[rows, :], hsb[i][:, :], [T_hsb[i]], [self.T_out])
                    return
                self.norm_block(hsb[i][:, :], T_hsb[i], junk, T_zT, stC, T_stC, 2 * (blk % 8), self.gbuf[:],
                                self.T_g, hnb=hnb[i][:, :], Thnb=T_hnb[i])

        def stage_d(blk):
            if last or self.h1scr is None:
                return
            self.transpose_block(hnb[blk % 2], T_hnb[blk % 2], blk, 0)

        for it in range(NB_OWN + 3):
            if it < NB_OWN:
                stage_a(it)
            if 0 <= it - 1 < NB_OWN:
                stage_b(it - 1)
            if 0 <= it - 2 < NB_OWN:
                stage_c(it - 2)
            if 0 <= it - 3 < NB_OWN:
                stage_d(it - 3)


def _own_ctx_blocks(j):
    if j == 0:
        own = list(range(0, 8)) + list(range(23, 32))
        ctx = list(range(8, 23))
    else:
        own = list(range(7, 24))
        ctx = list(range(0, 7)) + list(range(24, 32))
    return own, ctx


def _blocks_to_rows(blks):
    return np.concatenate([np.arange(b * 128, (b + 1) * 128) for b in blks])


def _consts(j):
    ident = np.eye(128, dtype=np.float32)
    rmat = np.zeros((128, 128), np.float32)
    for f in range(128):
        if (f % 64) < 32:
            rmat[f + 32, f] = -1.0
        else:
            rmat[f - 32, f] = 1.0
    pp, cc = np.meshgrid(np.arange(128), np.arange(128), indexing="ij")
    tri = (pp <= cc).astype(np.float32)
    inv = (np.float32(10000.0) ** (-np.arange(0, 64, 2, dtype=np.float32) / np.float32(64))).astype(np.float32)
    invf = inv[np.arange(128) % 32].reshape(128, 1).astype(np.float32)
    bias = np.zeros((128, 22), np.float32)
    if j == 0:
        bias[:, 0:7] = NEG
    else:
        bias[:, 7 + 7:22] = NEG
    return ident, rmat, tri, invf, bias


def _prep_weights(inp):
    f = np.float32
    aw = np.asarray(inp["attn_w_in"], f)[0]
    wq, wk, wv = aw[:, 0:1536], aw[:, 1536:3072], aw[:, 3072:4608]
    w_h0 = np.empty((NH, 128, 8, 384), f)
    for h in range(NH):
        cat = np.concatenate([wq[:, h * 128:(h + 1) * 128], wk[:, h * 128:(h + 1) * 128],
                              wv[:, h * 128:(h + 1) * 128]], axis=1)
        w_h0[h] = cat.reshape(8, 128, 384).transpose(1, 0, 2)
    pw = np.asarray(inp["pool_w_in"], f)[0]

    def pk(w, k):
        n = w.shape[1]
        return np.ascontiguousarray(w.reshape(k, 128, n).transpose(1, 0, 2)).reshape(128, k * n)

    w_qm = np.stack([pk(aw[:, 4608:5120], 8), pk(pw[:, 1536:2048], 8)])
    w_gt = np.stack([pk(aw[:, 5120:7168], 8), pk(pw[:, 2048:4096], 8)])
    wo = np.asarray(inp["w_out"], f)
    w_o = np.stack([pk(wo[0], 16), pk(wo[1], 16)])
    kv = np.asarray(inp["mem_w_kv"], f)
    w_kv = np.stack([pk(kv[0], 8), pk(kv[1], 8)])
    w_u = np.stack([pk(pw[:, ft * 128:(ft + 1) * 128], 8) for ft in range(12)])
    gp = np.asarray(inp["pool_w_group"], f)[0]
    w_gp = np.stack([pk(gp[g], 3) for g in range(4)])
    return dict(w_h0=np.ascontiguousarray(w_h0).reshape(NH, 128, 8 * 384), w_qm=w_qm, w_gt=w_gt, w_o=w_o,
                w_kv=w_kv, w_u=w_u, w_gp=w_gp)


_NC_CACHE = {}


def _get_nc(do_l0, do_l1):
    key = (do_l0, do_l1)
    if key not in _NC_CACHE:
        lam0 = 0.8 - 0.6 * math.exp(-0.3 * 0)
        _NC_CACHE[key] = Builder(do_l0, do_l1, lam0).build()
    return _NC_CACHE[key]


def _in_maps(inp, xsrc, full_tokens):
    f = np.float32
    wts = _prep_weights(inp)
    x = np.asarray(xsrc, f)
    mem = np.asarray(inp["mem"], f)
    pos = np.asarray(inp["positions"], np.int32)
    maps = []
    for core in range(8):
        b, j = core // 2, core % 2
        own, ctx = _own_ctx_blocks(j)
        rows_own = _blocks_to_rows(own)
        rows_all = np.concatenate([rows_own, _blocks_to_rows(ctx)])
        ident, rmat, tri, invf, bias = _consts(j)
        m = dict(wts)
        if full_tokens:
            m["xl"] = np.ascontiguousarray(x[b][rows_all])
        else:
            m["xl"] = np.ascontiguousarray(x[core])
        m["posl"] = np.ascontiguousarray(pos[b][rows_all]).reshape(1, TOK_ALL)
        m["meml"] = np.ascontiguousarray(mem[b])
        m["c_ident"] = ident
        m["c_rmat"] = rmat
        m["c_tri"] = tri
        m["c_invf"] = invf
        m["c_bias"] = bias
        m["v_lng"] = np.asarray(inp["ln_g"], f)
        m["v_memg"] = np.asarray(inp["mem_norm_g"], f).reshape(1, D)
        m["v_fing"] = np.asarray(inp["final_g"], f).reshape(1, D)
        m["v_subg"] = np.asarray(inp["attn_subln_g"], f).reshape(1, 128)
        m["v_pscale"] = np.asarray(inp["pool_scale"], f).reshape(1, MIX)
        m["v_lam"] = np.asarray(inp["attn_lambda"], f).reshape(1, 256)
        maps.append(m)
    return maps


def _assemble(res):
    out = np.empty((4, TOK_ALL, D), np.float32)
    for core in range(8):
        b, j = core // 2, core % 2
        own, _ = _own_ctx_blocks(j)
        o = res[core]["out"]
        for li, gb in enumerate(own):
            if j == 0 and li == 8:
                continue
            if j == 1 and li == 0:
                continue
            out[b, gb * 128:(gb + 1) * 128] = o[li * 128:(li + 1) * 128]
    return out


FUSED = True


def kernel(**inputs):
    if FUSED:
        nc = _get_nc(True, True)
        res = run_bass_kernel_spmd(nc, _in_maps(inputs, inputs["x"], True), core_ids=list(range(8)))
        return _assemble(res.results)
    nc0 = _get_nc(True, False)
    r0 = run_bass_kernel_spmd(nc0, _in_maps(inputs, inputs["x"], True), core_ids=list(range(8)))
    h1 = [r0.results[c]["out"] for c in range(8)]
    nc1 = _get_nc(False, True)
    r1 = run_bass_kernel_spmd(nc1, _in_maps(inputs, h1, False), core_ids=list(range(8)))
    return _assemble(r1.results)
```

```python
import math
from contextlib import ExitStack

import numpy as np
import concourse.bass as bass
import concourse.mybir as mybir
from concourse.bass_utils import run_bass_kernel_spmd

F32 = mybir.dt.float32
BF16 = mybir.dt.bfloat16
I32 = mybir.dt.int32
AF = mybir.ActivationFunctionType
ALU = mybir.AluOpType

NDMASEM = 14
DMA_SLOTS = {"sp": list(range(0, 8)), "pool": list(range(8, 14)), "act": list(range(0, 8))}

D = 1024
NB_OWN = 17
NB_CTX = 15
TOK_OWN = NB_OWN * 128
TOK_CTX = NB_CTX * 128
TOK_ALL = 4096
NH = 12
EPS = 1e-6
MIX = 1536
NEG = -30000.0
TWO_PI = 2.0 * math.pi
CW1 = 6.28125
CW2 = TWO_PI - CW1
OWN_GROUPS = [(0, 512), (512, 512), (1024, 512), (1536, 512), (2048, 128)]
CTX_GROUPS = [(2176, 512), (2688, 512), (3200, 512), (3712, 384)]
QGROUPS = [(0, 4, 7, 0), (4, 4, 7, 0), (8, 4, 15, 1), (12, 4, 15, 1), (16, 1, 15, 1)]
POOL_WINDOWS = (2, 4, 8, 16)
POOL_CHUNKS = [(0, 1024), (1024, 1152)]


class T:
    __slots__ = ("name", "w", "r", "excl")

    def __init__(self, name="", excl=False):
        self.name = name
        self.w = None
        self.r = {}
        self.excl = excl


class Op:
    __slots__ = ("eng", "fn", "deps", "veng", "idx", "signal", "count", "is_dma", "dval")


class Prog:
    ENGS = ("pe", "act", "dve", "pool", "sp")

    def __init__(self):
        self.streams = {e: [] for e in self.ENGS}
        self.ndma = {"sp": 0, "pool": 0, "act": 0}
        self.dma_cnt = [0] * NDMASEM
        self.guard = None

    def add(self, eng, fn, reads=(), writes=(), dma=False):
        o = Op()
        o.eng = eng
        o.fn = fn
        o.is_dma = dma
        o.signal = False
        o.count = 0
        o.dval = 0
        reads = list(reads)
        if self.guard is not None:
            reads.append(self.guard)
        deps = {}
        for t in reads:
            if t.w is not None:
                deps[id(t.w)] = t.w
            if t.excl:
                for p in t.r.values():
                    if p.eng != eng:
                        deps[id(p)] = p
        for t in writes:
            if t.w is not None:
                deps[id(t.w)] = t.w
            for p in t.r.values():
                deps[id(p)] = p
        o.deps = list(deps.values())
        if dma:
            slots = DMA_SLOTS[eng]
            slot = slots[self.ndma[eng] % len(slots)]
            self.ndma[eng] += 1
            self.dma_cnt[slot] += 1
            o.veng = ("dma", slot)
            o.dval = 16 * self.dma_cnt[slot]
        else:
            o.veng = eng
        for t in reads:
            t.r[o.veng] = o
        for t in writes:
            t.w = o
            t.r = {}
        o.idx = len(self.streams[eng])
        self.streams[eng].append(o)
        return o

    def emit(self, nc):
        for st in self.streams.values():
            for o in st:
                for d in o.deps:
                    d.signal = True
        for e, st in self.streams.items():
            c = 0
            for o in st:
                if o.is_dma:
                    continue
                if o.signal:
                    c += 1
                o.count = c
        with ExitStack() as es:
            sems = {e: es.enter_context(nc.semaphore("s_" + e)) for e in self.ENGS}
            dsems = [es.enter_context(nc.semaphore("d%d" % i)) for i in range(NDMASEM)]
            block = es.enter_context(nc.Block())
            streams = self.streams
            dma_cnt = self.dma_cnt

            def run(e, eng):
                waited = {}
                for o in streams[e]:
                    need = {}
                    for d in o.deps:
                        if d.is_dma:
                            key = d.veng
                            val = d.dval
                        else:
                            key = d.eng
                            val = d.count
                            if d.eng == e:
                                if e == "pe":
                                    continue
                                if o.idx - d.idx > 3:
                                    continue
                        if need.get(key, 0) < val:
                            need[key] = val
                    if o.is_dma and o.dval > 16:
                        if need.get(o.veng, 0) < o.dval - 16:
                            need[o.veng] = o.dval - 16
                    for key, val in need.items():
                        if waited.get(key, 0) >= val:
                            continue
                        sem = dsems[key[1]] if isinstance(key, tuple) else sems[key]
                        eng.wait_ge(sem, val)
                        waited[key] = val
                    ins = o.fn(eng)
                    if o.is_dma:
                        ins.then_inc(dsems[o.veng[1]], 16)
                    elif o.signal:
                        ins.then_inc(sems[e], 1)
                if e == "sp":
                    for slot in range(NDMASEM):
                        if dma_cnt[slot] > 0:
                            eng.wait_ge(dsems[slot], 16 * dma_cnt[slot])

            @block.tensor
            def _(eng):
                run("pe", eng)

            @block.scalar
            def _(eng):
                run("act", eng)

            @block.vector
            def _(eng):
                run("dve", eng)

            @block.gpsimd
            def _(eng):
                run("pool", eng)

            @block.sync
            def _(eng):
                run("sp", eng)


class Builder:
    def __init__(self, do_l0=True, do_l1=True, lambda_init0=0.2):
        self.do_l0 = do_l0
        self.do_l1 = do_l1
        self.lambda_init0 = lambda_init0
        self.p = Prog()
        self.epi_loaded = set()
        self.nc = bass.Bass("TRN2", target_bir_lowering=False)

    def mm(self, out, lhsT, rhs, start, stop, reads, writes, skip=False):
        self.p.add("pe", lambda e: e.matmul(out, lhsT=lhsT, rhs=rhs, start=start, stop=stop,
                                            skip_group_check=skip), reads, writes)

    def tr(self, out, in_, reads, writes):
        ident = self.identb
        self.p.add("pe", lambda e: e.transpose(out=out, in_=in_, identity=ident[:]), reads, writes)

    def act(self, out, in_, func, reads, writes, bias=None, scale=None, accum=None):
        kw = {}
        if bias is not None:
            kw["bias"] = bias
        if scale is not None:
            kw["scale"] = scale
        if accum is not None:
            kw["accum_out"] = accum
        self.p.add("act", lambda e: e.activation(out=out, in_=in_, func=func, **kw), reads, writes)

    def ts(self, eng, out, in0, s1, op0, reads, writes, s2=None, op1=None):
        if op1 is None:
            self.p.add(eng, lambda e: e.tensor_scalar(out=out, in0=in0, scalar1=s1, scalar2=None, op0=op0),
                       reads, writes)
        else:
            self.p.add(eng, lambda e: e.tensor_scalar(out=out, in0=in0, scalar1=s1, scalar2=s2, op0=op0, op1=op1),
                       reads, writes)

    def tt(self, eng, out, in0, in1, op, reads, writes):
        self.p.add(eng, lambda e: e.tensor_tensor(out=out, in0=in0, in1=in1, op=op), reads, writes)

    def stt(self, out, in0, scalar, in1, op0, op1, reads, writes):
        self.p.add("dve", lambda e: e.scalar_tensor_tensor(out=out, in0=in0, scalar=scalar, in1=in1,
                                                           op0=op0, op1=op1), reads, writes)

    def cp(self, eng, out, in_, reads, writes):
        if eng == "act":
            self.p.add("act", lambda e: e.copy(out=out, in_=in_), reads, writes)
        else:
            self.p.add(eng, lambda e: e.tensor_copy(out=out, in_=in_), reads, writes)

    def memset(self, eng, ap, val, writes):
        self.p.add(eng, lambda e: e.memset(ap, val), (), writes)

    def recip(self, out, in_, reads, writes):
        self.p.add("dve", lambda e: e.reciprocal(out=out, in_=in_), reads, writes)

    def dma(self, out, in_, reads, writes, q="sp"):
        self.p.add(q, lambda e: e.dma_start(out=out, in_=in_), reads, writes, dma=True)

    def bc(self, ap):
        b = ap.partition_broadcast(128)
        return b.rearrange("p o n -> p (o n)")

    def dbg(self, name, ap, reads):
        import os
        if not os.environ.get("KDBG"):
            return
        shape = [int(x) for x in ap.shape]
        d = self.nc.dram_tensor("dbg_" + name, shape, ap.dtype, kind="ExternalOutput").ap()
        self.dma(d, ap, reads, [T()])

    def load_epi_weights(self, L, which):
        g = self.p.guard
        self.p.guard = None
        for w in which:
            if (L, w) in self.epi_loaded:
                continue
            self.epi_loaded.add((L, w))
            if w == "gt":
                ctxT = [self.T_hn[b] for b in range(17, 32)]
                self.dma(self.regB[:, :].rearrange("p (c n) -> p c n", c=16),
                         self.w_gt[L].rearrange("p (c n) -> p c n", c=16), ctxT, [self.T_regB_w] + ctxT, q="pool")
            elif w == "qm":
                self.dma(self.wqm[:], self.w_qm[L].rearrange("p (c n) -> p c n", c=8), [], [self.T_wqm], q="pool")
            else:
                self.dma(self.wo, self.w_o[L].rearrange("p (e d) -> p e d", e=16), [], [self.T_cs], q="pool")
        self.p.guard = g

    def fence(self):
        g = self.p.guard
        self.p.guard = None
        fz = self.fz
        self.p.add("pool", lambda e: e.memset(fz[:], 0.0), (), [g])
        self.p.guard = g

    def build(self):
        nc = self.nc
        with ExitStack() as es:
            self.es = es
            self._declare(es)
            self._setup()
            self.fence()
            import os
            ph = int(os.environ.get("KPHASE", "9"))
            if self.do_l0 and ph >= 1:
                self.load_epi_weights(0, ("qm",))
                self._l0_mixer()
                self.fence()
                if ph >= 2:
                    self._epilogue(0)
                    if self.do_l1:
                        self.load_epi_weights(1, ("gt", "qm", "wo"))
                    self.fence()
            if self.do_l1:
                self._l1_mixer()
                self.fence()
                self._epilogue(1)
            self.p.emit(nc)
        return nc

    def _declare(self, es):
        nc = self.nc
        di = lambda n, s, d: nc.dram_tensor(n, s, d, kind="ExternalInput").ap()
        self.xl = di("xl", [TOK_ALL if self.do_l0 else TOK_OWN, D], F32)
        self.posl = di("posl", [1, TOK_ALL], I32)
        self.meml = di("meml", [256, D], F32)
        self.c_ident = di("c_ident", [128, 128], F32)
        self.c_rmat = di("c_rmat", [128, 128], F32)
        self.c_tri = di("c_tri", [128, 128], F32)
        self.c_invf = di("c_invf", [128, 1], F32)
        self.c_bias = di("c_bias", [128, 22], F32)
        self.v_lng = di("v_lng", [2, D], F32)
        self.v_memg = di("v_memg", [1, D], F32)
        self.v_fing = di("v_fing", [1, D], F32)
        self.v_subg = di("v_subg", [1, 128], F32)
        self.v_pscale = di("v_pscale", [1, MIX], F32)
        self.v_lam = di("v_lam", [1, 256], F32)
        self.w_h0 = di("w_h0", [NH, 128, 8 * 384], F32)
        self.w_qm = di("w_qm", [2, 128, 8 * 512], F32)
        self.w_gt = di("w_gt", [2, 128, 8 * 2048], F32)
        self.w_o = di("w_o", [2, 128, 16 * 1024], F32)
        self.w_kv = di("w_kv", [2, 128, 8 * 1024], F32)
        self.w_u = di("w_u", [12, 128, 8 * 128], F32)
        self.w_gp = di("w_gp", [4, 128, 3 * 384], F32)
        self.out = nc.dram_tensor("out", [TOK_OWN, D], F32, kind="ExternalOutput").ap()
        import os
        if os.environ.get("KDUMP"):
            self.yscr = nc.dram_tensor("yscr", [TOK_OWN, MIX], BF16, kind="ExternalOutput").ap()
        else:
            self.yscr = nc.dram_tensor("yscr", [TOK_OWN, MIX], BF16).ap()
        if self.do_l0 and self.do_l1:
            self.h1scr = nc.dram_tensor("h1scr", [TOK_OWN, D], F32).ap()
        else:
            self.h1scr = None

        sb = lambda n, s, d: es.enter_context(nc.sbuf_tensor(n, s, d))
        self.hnT = sb("hnT", [128, 8, TOK_OWN], BF16)
        self.regB = sb("regB", [128, 16384], BF16)
        self.regC = sb("regC", [128, 8192], F32)
        self.regD = sb("regD", [128, 13440], F32)
        self.wslot = [sb("wslot%d" % i, [128, 8, 384], BF16) for i in range(2)]
        self.wqm = sb("wqm", [128, 8, 512], BF16)
        self.kmT = [sb("kmT%d" % i, [128, 4, 256], BF16) for i in range(2)]
        self.vm = [sb("vm%d" % i, [128, 2, 4, 132], BF16) for i in range(2)]
        self.gbuf = sb("gbuf", [128, D], F32)
        self.subg = sb("subg", [128, 128], F32)
        self.identb = sb("identb", [128, 128], BF16)
        self.rmatb = sb("rmatb", [128, 128], BF16)
        self.trib = sb("trib", [128, 128], BF16)
        self.cst = sb("cst", [128, 32], F32)
        self.fz = sb("fz", [128, 2], F32)
        self.ps = [es.enter_context(nc.psum_tensor("ps%d" % i, [128, 512], F32)) for i in range(8)]

        self.hnT_ctx = self.regB[:, 0:8 * TOK_CTX].rearrange("p (c t) -> p c t", c=8)
        self.wgt = self.regB[:, :].rearrange("p (c n) -> p c n", c=8)
        self.cos = self.regC[:, 0:4096]
        self.sin = self.regC[:, 4096:8192]
        self.wo = self.regC[:, :].bitcast(BF16).rearrange("p (e d) -> p e d", e=16)

        self.T_hn = [T("hn%d" % i) for i in range(32)]
        self.T_regB_w = T("wgt")
        self.T_cs = T("cossin/wo")
        self.T_ws = [T("ws0"), T("ws1")]
        self.T_wqm = T("wqm")
        self.T_km = [T(), T()]
        self.T_vm = [T(), T()]
        self.T_g = T("gbuf")
        self.T_const = T("const")
        self.T_ps = [T("ps%d" % i, excl=True) for i in range(8)]
        self.T_y = [T("yscr%d" % i) for i in range(NB_OWN)]
        self.T_h1 = [T("h1scr%d" % i) for i in range(NB_OWN)]
        self.T_out = T("out")
        self.p.guard = T("guard")

    def carve(self, off_bytes, shape, dtype):
        free = list(shape[1:])
        n = int(np.prod(free))
        nbytes = n * (4 if dtype in (F32, I32) else 2)
        assert off_bytes % 4 == 0 and off_bytes + nbytes <= 13440 * 4, (off_bytes, nbytes)
        a = self.regD[:, off_bytes // 4:(off_bytes + nbytes + 3) // 4]
        if dtype != F32:
            a = a.bitcast(dtype)
        a = a[:, 0:n]
        if len(free) == 2:
            return a.rearrange("p (a b) -> p a b", a=free[0])
        if len(free) == 3:
            return a.rearrange("p (a b c) -> p a b c", a=free[0], b=free[1])
        return a

    def hn_ap(self, c, t0, n):
        if t0 < TOK_OWN:
            assert t0 + n <= TOK_OWN
            return self.hnT[:, c, t0:t0 + n]
        return self.hnT_ctx[:, c, t0 - TOK_OWN:t0 - TOK_OWN + n]

    def hn_T(self, t0, n):
        return [self.T_hn[b] for b in range(t0 // 128, (t0 + n) // 128)]

    def norm_block(self, src, Tsrc, junk, Tjunk, st, Tst, col, g, Tg, dst_f32=None, Tdst=None,
                   hnb=None, Thnb=None):
        eps_ap = self.cst[:, 1:2]
        ss = st[:, col:col + 1]
        rs = st[:, col + 1:col + 2]
        self.act(junk, src, AF.Square, [Tsrc], [Tjunk, Tst], accum=ss)
        self.act(rs, ss, AF.Ln, [Tst, self.T_const], [Tst], bias=eps_ap, scale=1.0 / D)
        self.act(rs, rs, AF.Exp, [Tst], [Tst], scale=-0.5)
        if hnb is not None:
            self.stt(hnb, src, rs, g, ALU.mult, ALU.mult, [Tsrc, Tst, Tg], [Thnb])
        if dst_f32 is not None:
            self.stt(dst_f32, src, rs, g, ALU.mult, ALU.mult, [Tsrc, Tst, Tg], [Tdst])

    def transpose_block(self, hnb, Thnb, blk, psi):
        psb = self.ps[psi][:].bitcast(BF16)
        for c in range(8):
            self.tr(psb[:, c * 128:(c + 1) * 128], hnb[:, c * 128:(c + 1) * 128],
                    [Thnb, self.T_const], [self.T_ps[psi]])
        if blk < NB_OWN:
            dst = self.hnT[:, :, blk * 128:(blk + 1) * 128]
        else:
            b = blk - NB_OWN
            dst = self.hnT_ctx[:, :, b * 128:(b + 1) * 128]
        src = psb.rearrange("p (c t) -> p c t", c=8)
        eng = "dve" if blk % 2 == 0 else "act"
        self.cp(eng, dst, src, [self.T_ps[psi]], [self.T_hn[blk]])

    def _setup(self):
        p = self.p
        Tc = self.T_const
        tmpf = self.carve(0, [128, 128], F32)
        Ttmp = T()
        for src, dst in ((self.c_ident, self.identb), (self.c_rmat, self.rmatb), (self.c_tri, self.trib)):
            self.dma(tmpf[:, :], src, [], [Ttmp])
            self.cp("dve", dst[:], tmpf[:, :], [Ttmp], [Tc])
        self.memset("dve", self.cst[:, :], 0.0, [Tc])
        self.memset("dve", self.cst[:, 1:2], EPS, [Tc])
        self.memset("dve", self.cst[:, 4:8], -0.5, [Tc])
        self.dma(self.cst[:, 0:1], self.c_invf, [], [Tc])
        self.dma(self.cst[:, 8:30], self.c_bias, [], [Tc])
        lamt = self.carve(1024, [128, 256], F32)
        lt2 = self.carve(2048, [128, 128], F32)
        Tl = T()
        self.dma(lamt[:, :], self.bc(self.v_lam), [], [Tl])
        self.tt("dve", lt2[:, 0:64], lamt[:, 0:64], lamt[:, 64:128], ALU.mult, [Tl], [Tl])
        self.tt("dve", lt2[:, 64:128], lamt[:, 128:192], lamt[:, 192:256], ALU.mult, [Tl], [Tl])
        st0 = self.carve(3072, [128, 16], F32)
        Tst0 = T()
        self.act(lamt[:, 0:64], lt2[:, 0:64], AF.Copy, [Tl], [Tl, Tst0], accum=st0[:, 0:1])
        self.act(lamt[:, 64:128], lt2[:, 64:128], AF.Copy, [Tl], [Tl, Tst0], accum=st0[:, 1:2])
        self.act(st0[:, 2:4], st0[:, 0:2], AF.Exp, [Tst0], [Tst0])
        self.tt("dve", st0[:, 4:5], st0[:, 3:4], st0[:, 2:3], ALU.subtract, [Tst0], [Tst0])
        self.ts("dve", self.cst[:, 2:3], st0[:, 4:5], -self.lambda_init0, ALU.add, [Tst0], [Tc])
        self.dma(self.subg[:], self.bc(self.v_subg), [], [Tc])
        self.ts("dve", self.subg[:], self.subg[:], 1.0 - self.lambda_init0, ALU.mult, [Tc], [Tc])

        xst = [self.carve(4096 + i * 4096, [128, D], F32) for i in range(2)]
        Txst = [T(), T()]
        hnb = [self.carve(12288 + i * 2048, [128, D], BF16) for i in range(2)]
        Thnb = [T(), T()]
        junk = self.carve(16384, [128, D], F32)
        Tjunk = T()
        st = self.carve(20480, [128, 80], F32)
        Tst = T()
        memT = self.carve(20992, [128, 8, 256], BF16)
        Tmem = T()
        wkv = self.carve(25088, [128, 8, 1024], BF16)
        Twkv = T()
        self.dma(self.gbuf[:], self.bc(self.v_memg), [], [self.T_g])
        for mb in range(2):
            i = mb % 2
            self.dma(xst[i][:, :], self.meml[mb * 128:(mb + 1) * 128, :], [], [Txst[i]])
            self.norm_block(xst[i][:, :], Txst[i], junk[:, :], Tjunk, st, Tst, 2 * mb, self.gbuf[:], self.T_g,
                            hnb=hnb[i][:, :], Thnb=Thnb[i])
            psb = self.ps[mb][:].bitcast(BF16)
            for c in range(8):
                self.tr(psb[:, c * 128:(c + 1) * 128], hnb[i][:, c * 128:(c + 1) * 128], [Thnb[i], Tc],
                        [self.T_ps[mb]])
            self.cp("dve", memT[:, :, mb * 128:(mb + 1) * 128], psb.rearrange("p (c t) -> p c t", c=8),
                    [self.T_ps[mb]], [Tmem])
        for L in range(2):
            if (L == 0 and not self.do_l0) or (L == 1 and not self.do_l1):
                continue
            self.dma(wkv, self.w_kv[L].rearrange("p (c n) -> p c n", c=8), [], [Twkv], q="pool")
            for hd in range(4):
                pi = 2 + (hd % 2)
                for c in range(8):
                    self.mm(self.ps[pi][:, 0:256], wkv[:, c, hd * 128:(hd + 1) * 128], memT[:, c, :],
                            c == 0, c == 7, [Twkv, Tmem], [self.T_ps[pi]])
                self.cp("act", self.kmT[L][:, hd, :], self.ps[pi][:, 0:256], [self.T_ps[pi]], [self.T_km[L]])
            self.memset("pool", self.vm[L][:, :, :, 128:129], 1.0, [self.T_vm[L]])
            for mc in range(2):
                pi = 4 + mc
                for c in range(8):
                    self.mm(self.ps[pi][:, :], memT[:, c, mc * 128:(mc + 1) * 128], wkv[:, c, 512:1024],
                            c == 0, c == 7, [Twkv, Tmem], [self.T_ps[pi]])
                self.cp("dve", self.vm[L][:, mc, :, 0:128], self.ps[pi][:, :].rearrange("p (h e) -> p h e", h=4),
                        [self.T_ps[pi]], [self.T_vm[L]])

        if not self.do_l0:
            self.dma(self.gbuf[:], self.bc(self.v_lng[1:2, :]), [], [self.T_g])
            for blk in range(NB_OWN):
                i = blk % 2
                self.dma(xst[i][:, :], self.xl[blk * 128:(blk + 1) * 128, :], [], [Txst[i]])
                self.norm_block(xst[i][:, :], Txst[i], junk[:, :], Tjunk, st, Tst, 4 + 2 * (blk % 8),
                                self.gbuf[:], self.T_g, hnb=hnb[i][:, :], Thnb=Thnb[i])
                self.transpose_block(hnb[i], Thnb[i], blk, 6 + i)
            return

        posi = self.carve(41472, [128, 512], I32)
        Tpi = T()
        posf = self.carve(43520, [128, 512], F32)
        ang = self.carve(45568, [128, 512], F32)
        kf = self.carve(47616, [128, 512], F32)
        ki = self.carve(49664, [128, 384], I32)
        Trope = T()
        invf = self.cst[:, 0:1]
        self.memset("dve", self.cst[:, 3:4], math.pi / 2, [Tc])
        for t0 in range(0, TOK_ALL, 384):
            n = min(384, TOK_ALL - t0)
            self.dma(posi[:, 0:n], self.bc(self.posl[:, t0:t0 + n]), [], [Tpi])
            self.cp("dve", posf[:, 0:n], posi[:, 0:n], [Tpi], [Trope])
            for which, dst in ((0, self.sin), (1, self.cos)):
                if which == 0:
                    self.ts("dve", ang[:, 0:n], posf[:, 0:n], invf, ALU.mult, [Trope, Tc], [Trope])
                else:
                    self.ts("dve", ang[:, 0:n], posf[:, 0:n], invf, ALU.mult, [Trope, Tc], [Trope],
                            s2=self.cst[:, 3:4], op1=ALU.add)
                self.ts("dve", ki[:, 0:n], ang[:, 0:n], 1.0 / TWO_PI, ALU.mult, [Trope], [Trope])
                self.cp("dve", kf[:, 0:n], ki[:, 0:n], [Trope], [Trope])
                self.stt(ang[:, 0:n], kf[:, 0:n], -CW1, ang[:, 0:n], ALU.mult, ALU.add, [Trope], [Trope])
                self.stt(ang[:, 0:n], kf[:, 0:n], -CW2, ang[:, 0:n], ALU.mult, ALU.add, [Trope], [Trope])
                self.ts("dve", ang[:, 0:n], ang[:, 0:n], -3.1415925, ALU.max, [Trope], [Trope],
                        s2=3.1415925, op1=ALU.min)
                self.act(dst[:, t0:t0 + n], ang[:, 0:n], AF.Sin, [Trope], [self.T_cs])

        self.dbg("cst", self.cst[:, :], [Tc])
        self.dbg("cos", self.cos[:, 0:512], [self.T_cs])
        self.dbg("sin", self.sin[:, 0:512], [self.T_cs])
        self.dma(self.gbuf[:], self.bc(self.v_lng[0:1, :]), [Tmem], [self.T_g])
        for blk in range(32):
            i = blk % 2
            self.dma(xst[i][:, :], self.xl[blk * 128:(blk + 1) * 128, :], [], [Txst[i]])
            self.norm_block(xst[i][:, :], Txst[i], junk[:, :], Tjunk, st, Tst, 4 + 2 * (blk % 8),
                            self.gbuf[:], self.T_g, hnb=hnb[i][:, :], Thnb=Thnb[i])
            self.transpose_block(hnb[i], Thnb[i], blk, 6 + i)

    def _l0_mixer(self):
        Tc = self.T_const
        kTz = self.carve(0, [128, 2, TOK_ALL], BF16)
        qT = self.carve(16384, [128, TOK_OWN], BF16)
        vh = self.carve(20736, [128, 32, 132], BF16)
        xb = [self.carve(29184 + i * 512, [128, 256], BF16) for i in range(4)]
        t1 = [self.carve(31232 + i * 1024, [128, 256], F32) for i in range(4)]
        t2 = [self.carve(35328 + i * 1024, [128, 256], F32) for i in range(4)]
        NPT = 6
        pt = [self.carve(39424 + i * 1024, [128, 512], BF16) for i in range(NPT)]
        osb = self.carve(45568, [128, 3, 396], F32)
        yst = [self.carve(50320 + i * 1024, [128, 4, 128], BF16) for i in range(2)]
        stt_ = self.carve(52368, [128, 64], F32)
        yq = self.carve(52624, [128, 128], F32)
        T_kT = [T() for _ in range(9)]
        T_qT = [T() for _ in range(5)]
        T_vh = [T() for _ in range(8)]
        T_xb = [T() for _ in range(4)]
        T_t1 = [T() for _ in range(4)]
        T_t2 = [T() for _ in range(4)]
        T_pt = [T() for _ in range(NPT)]
        T_osb = T()
        T_yst = [T(), T()]
        T_st = T()
        T_yq = T()
        T_vone = T()

        self.memset("pool", kTz[64:128, 0, :], 0.0, [T_vone])
        self.memset("pool", kTz[0:64, 1, :], 0.0, [T_vone])
        self.memset("pool", vh[:, :, 128:129], 1.0, [T_vone])

        def load_w(h):
            s = h % 2
            self.dma(self.wslot[s][:], self.w_h0[h].rearrange("p (c n) -> p c n", c=8), [], [self.T_ws[s]], q="pool")

        load_w(0)
        allg = OWN_GROUPS + CTX_GROUPS
        LA = 4
        NS = 5
        ptr = 0
        obank = lambda a: 5 + a // 3
        oacc = lambda a: self.ps[obank(a)][:, (a % 3) * 132:(a % 3) * 132 + 129]

        def kblock_T(kb):
            t0 = kb * 128
            for gi_, (g0, gn) in enumerate(allg):
                if g0 <= t0 < g0 + gn:
                    return [T_kT[gi_], T_vh[kb // 4]]
            raise AssertionError

        for h in range(NH):
            s = h % 2
            W = self.wslot[s]
            Tw = self.T_ws[s]
            if h + 1 < NH:
                load_w(h + 1)
            jobs = []
            for kind, glist in (("q", list(range(len(OWN_GROUPS)))), ("k", list(range(9)))):
                for g in glist:
                    g0, gn = allg[g]
                    for off in range(0, gn, 256):
                        jobs.append((kind, g, g0 + off, min(256, gn - off)))

            def front(j):
                kind, g, t0, n = jobs[j]
                col0 = 0 if kind == "q" else 128
                i = j % 4
                pa = i
                for c in range(8):
                    self.mm(self.ps[pa][:, 0:n], W[:, c, col0:col0 + 128], self.hn_ap(c, t0, n),
                            c == 0, c == 7, [Tw] + self.hn_T(t0, n), [self.T_ps[pa]])
                self.cp("act", xb[i][:, 0:n], self.ps[pa][:, 0:n], [self.T_ps[pa]], [T_xb[i]])

            def back(j):
                kind, g, t0, n = jobs[j]
                i = j % 4
                pa, pb = i, 4 + i
                self.mm(self.ps[pb][:, 0:n], self.rmatb[:], xb[i][:, 0:n], True, True, [Tc, T_xb[i]],
                        [self.T_ps[pb]])
                self.tt("dve", t1[i][:, 0:n], self.ps[pa][:, 0:n], self.cos[:, t0:t0 + n], ALU.mult,
                        [self.T_ps[pa], self.T_cs], [T_t1[i]])
                self.tt("dve", t2[i][:, 0:n], self.ps[pb][:, 0:n], self.sin[:, t0:t0 + n], ALU.mult,
                        [self.T_ps[pb], self.T_cs], [T_t2[i]])
                if kind == "q":
                    self.tt("pool", qT[:, t0:t0 + n], t1[i][:, 0:n], t2[i][:, 0:n], ALU.add,
                            [T_t1[i], T_t2[i]], [T_qT[g]])
                else:
                    self.tt("pool", kTz[0:64, 0, t0:t0 + n], t1[i][0:64, 0:n], t2[i][0:64, 0:n], ALU.add,
                            [T_t1[i], T_t2[i], T_vone], [T_kT[g]])
                    self.tt("pool", kTz[64:128, 1, t0:t0 + n], t1[i][64:128, 0:n], t2[i][64:128, 0:n], ALU.add,
                            [T_t1[i], T_t2[i], T_vone], [T_kT[g]])

            front(0)
            for j in range(len(jobs)):
                if j + 1 < len(jobs):
                    front(j + 1)
                back(j)
            for vb in range(8):
                pi = 4 + (vb % 2)
                for j in range(4):
                    blk = vb * 4 + j
                    for c in range(8):
                        self.mm(self.ps[pi][:, j * 128:(j + 1) * 128], self.hn_ap(c, blk * 128, 128),
                                W[:, c, 256:384], c == 0, c == 7, [Tw, self.T_hn[blk]], [self.T_ps[pi]], skip=True)
                eng = "act" if vb % 2 == 0 else "dve"
                self.cp(eng, vh[:, vb * 4:(vb + 1) * 4, 0:128],
                        self.ps[pi][:, :].rearrange("p (j e) -> p j e", j=4), [self.T_ps[pi], T_vone], [T_vh[vb]])
            if h == NH - 1:
                self.load_epi_weights(0, ("gt", "wo"))
            if h == 0:
                self.dbg("qT", qT[:, :], T_qT)
                self.dbg("kTz", kTz[:, :, 0:512], T_kT)
                self.dbg("vh", vh[:, 0:4, :], T_vh)

            steps = []
            for gi, (q0, nq, nctx, btab) in enumerate(QGROUPS):
                klist = []
                for sl in range(nctx):
                    bcol = 8 + (sl if btab == 0 else 7 + sl)
                    klist.append((NB_OWN + sl, self.cst[:, bcol:bcol + 1], None, 0))
                for m in range(q0 + nq):
                    if m < q0:
                        klist.append((m, None, None, 0))
                    else:
                        klist.append((m, None, (m - q0) * 128, (m - q0) * 128))
                nk = len(klist)
                for ki_, (kb, bias_ap, dcol, c0) in enumerate(klist):
                    for mp in range(2):
                        steps.append(dict(gi=gi, ki=ki_, nk=nk, kb=kb, bias=bias_ap, dcol=dcol, c0=c0, mp=mp,
                                          last=(ki_ == nk - 1 and mp == 1)))
            started = {}

            def s_front(idx):
                nonlocal ptr
                st_ = steps[idx]
                q0, nq, nctx, btab = QGROUPS[st_["gi"]]
                ncols = nq * 128
                c0, kb, mp = st_["c0"], st_["kb"], st_["mp"]
                qg_T = []
                for (g0, gn), tq in zip(OWN_GROUPS, T_qT):
                    if g0 < (q0 + nq) * 128 and q0 * 128 < g0 + gn:
                        qg_T.append(tq)
                kT_T = kblock_T(kb)
                si = idx % NS
                pti = ptr % NPT
                ptr += 1
                st_["pti"] = pti
                self.mm(self.ps[si][:, c0:ncols], kTz[:, mp, kb * 128:(kb + 1) * 128],
                        qT[:, q0 * 128 + c0:q0 * 128 + ncols], True, True,
                        [kT_T[0], T_vone] + qg_T, [self.T_ps[si]])
                if st_["bias"] is not None:
                    self.act(pt[pti][:, c0:ncols], self.ps[si][:, c0:ncols], AF.Exp,
                             [self.T_ps[si], Tc], [T_pt[pti]], bias=st_["bias"], scale=0.125)
                else:
                    self.act(pt[pti][:, c0:ncols], self.ps[si][:, c0:ncols], AF.Exp,
                             [self.T_ps[si]], [T_pt[pti]], scale=0.125)
                dcol = st_["dcol"]
                if dcol is not None:
                    self.tt("pool", pt[pti][:, dcol:dcol + 128], pt[pti][:, dcol:dcol + 128],
                            self.trib[:], ALU.mult, [T_pt[pti], Tc], [T_pt[pti]])

            def s_back(idx):
                st_ = steps[idx]
                gi = st_["gi"]
                q0, nq, nctx, btab = QGROUPS[gi]
                c0, kb, mp, pti = st_["c0"], st_["kb"], st_["mp"], st_["pti"]
                kT_T = kblock_T(kb)
                stt_set = started.setdefault(gi, set())
                for qb in range(c0 // 128, nq):
                    a = mp * 4 + qb
                    bk = obank(a)
                    first = bk not in stt_set
                    stt_set.add(bk)
                    self.mm(oacc(a), pt[pti][:, qb * 128:(qb + 1) * 128], vh[:, kb, 0:129],
                            first, st_["ki"] == st_["nk"] - 1, [T_pt[pti], kT_T[1], T_vone], [self.T_ps[bk]],
                            skip=True)
                if st_["last"]:
                    finalize(gi)

            def finalize(gi):
                q0, nq, nctx, btab = QGROUPS[gi]
                for bk in range(3):
                    used = 396 if (bk + 1) * 3 <= 8 else 264
                    self.cp("dve", osb[:, bk, 0:used], self.ps[5 + bk][:, 0:used], [self.T_ps[5 + bk]], [T_osb])
                osf = osb.rearrange("p b n -> p (b n)")
                acc3 = osf[:, 0:8 * 132].rearrange("p (a n) -> p a n", n=132)
                rl = stt_[:, 0:8]
                self.recip(rl.rearrange("p (a o) -> p a o", o=1), acc3[:, 0:8, 128:129], [T_osb], [T_st])
                self.ts("dve", stt_[:, 4:8], stt_[:, 4:8], self.cst[:, 2:3], ALU.mult, [T_st, Tc], [T_st])
                ysl = yst[gi % 2]
                Tys = T_yst[gi % 2]
                for qb in range(nq):
                    o0 = osf[:, qb * 132:qb * 132 + 128]
                    o1 = osf[:, (4 + qb) * 132:(4 + qb) * 132 + 128]
                    self.ts("dve", o0, o0, stt_[:, qb:qb + 1], ALU.mult, [T_osb, T_st], [T_osb])
                    self.stt(o0, o1, stt_[:, 4 + qb:5 + qb], o0, ALU.mult, ALU.add, [T_osb, T_st], [T_osb])
                yv = acc3[:, 0:nq, 0:128]
                sqv = acc3[:, 4:4 + nq, 0:128]
                self.tt("dve", sqv, yv, yv, ALU.mult, [T_osb], [T_osb])
                self.p.add("dve", lambda e, o=stt_[:, 16:16 + nq], i_=sqv: e.tensor_reduce(
                    out=o, in_=i_, axis=mybir.AxisListType.X, op=ALU.add), [T_osb], [T_st])
                self.ts("dve", stt_[:, 16:16 + nq], stt_[:, 16:16 + nq], 1.0 / 128, ALU.mult, [T_st], [T_st],
                        s2=EPS, op1=ALU.add)
                self.tt("pool", stt_[:, 24:24 + nq], stt_[:, 16:16 + nq], self.cst[:, 4:4 + nq], ALU.pow,
                        [T_st, Tc], [T_st])
                for qb in range(nq):
                    o0 = osf[:, qb * 132:qb * 132 + 128]
                    self.stt(ysl[:, qb, :], o0, stt_[:, 24 + qb:25 + qb], self.subg[:], ALU.mult, ALU.mult,
                             [T_osb, T_st, Tc], [Tys])
                dst = self.yscr[q0 * 128:(q0 + nq) * 128, h * 128:(h + 1) * 128].rearrange(
                    "(q p) e -> p q e", p=128)
                self.dma(dst, ysl[:, 0:nq, :], [Tys], [self.T_y[b] for b in range(q0, q0 + nq)])

            nst = len(steps)
            fi = 0
            for bi in range(nst):
                la_cur = LA + 1 if (steps[bi]["gi"] > 0 and steps[bi]["ki"] < 2) else LA
                while fi < nst and fi <= bi + la_cur:
                    s_front(fi)
                    fi += 1
                s_back(bi)

    def _l1_mixer(self):
        Tc = self.T_const
        PADW = 16
        CH = 1152
        U = [self.carve(i * (PADW + CH) * 4, [128, PADW + CH], F32) for i in range(2)]
        A = self.carve(9344, [128, PADW + CH], F32)
        Bf = self.carve(14016, [128, PADW + CH], F32)
        pooledT = self.carve(18688, [128, 3, TOK_OWN], BF16)
        wgp = [self.carve(31744 + i * 2304, [128, 3, 384], BF16) for i in range(2)]
        pscale = self.carve(36352, [128, MIX], F32)
        ysb = [self.carve(42496 + i * 768, [128, 384], BF16) for i in range(2)]
        invc = self.carve(44032, [128, 4, 16], F32)
        T_U = [T(), T()]
        T_A = T()
        T_B = T()
        T_pl = [[T() for _ in range(2)] for _ in range(3)]
        T_wgp = [T(), T()]
        T_psc = T()
        T_ysb = [T(), T()]
        T_inv = T()

        self.dma(pscale[:, :], self.bc(self.v_pscale), [], [T_psc])
        for g, w in enumerate(POOL_WINDOWS):
            for t in range(16):
                self.memset("pool", invc[:, g, t:t + 1], 1.0 / min(t + 1, w), [T_inv])
        for i in range(2):
            self.memset("pool", U[i][:, 0:PADW], 0.0, [T_U[i]])
        self.memset("pool", A[:, 0:PADW], 0.0, [T_A])
        self.memset("pool", Bf[:, 0:PADW], 0.0, [T_B])

        def load_wu(ft):
            s = ft % 2
            self.dma(self.wslot[s][:, :, 0:128], self.w_u[ft].rearrange("p (c n) -> p c n", c=8), [],
                     [self.T_ws[s]], q="pool")

        load_wu(0)
        un = 0
        for g, w in enumerate(POOL_WINDOWS):
            self.dma(wgp[g % 2], self.w_gp[g].rearrange("p (c n) -> p c n", c=3), [], [T_wgp[g % 2]], q="pool")
            for cc in range(3):
                ft = g * 3 + cc
                s = ft % 2
                W = self.wslot[s]
                if ft + 1 < 12:
                    load_wu(ft + 1)
                for ci, (c0, cn) in enumerate(POOL_CHUNKS):
                    u = U[un % 2]
                    Tu = T_U[un % 2]
                    un += 1
                    halo = 0 if c0 == 0 else PADW
                    t = c0 - halo
                    pieces = []
                    while t < c0 + cn:
                        n = min(512, c0 + cn - t)
                        if t < c0:
                            n = halo
                        pieces.append((t, n))
                        t += n
                    for pi_, (t0, n) in enumerate(pieces):
                        pb = pi_ % 2
                        for c in range(8):
                            self.mm(self.ps[pb][:, 0:n], W[:, c, 0:128], self.hn_ap(c, t0, n), c == 0, c == 7,
                                    [self.T_ws[s]] + self.hn_T(t0 - (t0 % 128), 128 * ((t0 % 128 + n + 127) // 128)),
                                    [self.T_ps[pb]])
                        dcol = PADW + (t0 - c0)
                        eng = "act" if pi_ % 2 == 0 else "dve"
                        self.cp(eng, u[:, dcol:dcol + n], self.ps[pb][:, 0:n], [self.T_ps[pb]], [Tu])
                    src, Tsrc = u, Tu
                    bufs = [(A, T_A), (Bf, T_B)]
                    nsteps = g + 1
                    for stp in range(nsteps):
                        sh = 1 << stp
                        dst, Tdst = bufs[stp % 2]
                        eng = "dve" if stp % 2 == 0 else "pool"
                        lo = PADW - (PADW - sh) if False else sh
                        self.tt(eng, dst[:, sh:PADW + cn], src[:, sh:PADW + cn], src[:, 0:PADW + cn - sh], ALU.add,
                                [Tsrc], [Tdst])
                        src, Tsrc = dst, Tdst
                    dstp = pooledT[:, cc, c0:c0 + cn]
                    self.stt(dstp, src[:, PADW:PADW + cn], 1.0 / w, u[:, PADW:PADW + cn], ALU.mult, ALU.subtract,
                             [Tsrc, Tu], [T_pl[cc][ci]])
                    if c0 == 0:
                        tmp = src[:, 0:16]
                        self.tt("dve", tmp, src[:, PADW:PADW + 16], invc[:, g, :], ALU.mult, [Tsrc, T_inv], [Tsrc])
                        self.tt("dve", pooledT[:, cc, 0:16], tmp, u[:, PADW:PADW + 16], ALU.subtract,
                                [Tsrc, Tu], [T_pl[cc][ci]])
            for blk in range(NB_OWN):
                pb = 2 + blk % 2
                ci = 0 if blk < 8 else 1
                for cc in range(3):
                    self.mm(self.ps[pb][:, 0:384], pooledT[:, cc, blk * 128:(blk + 1) * 128], wgp[g % 2][:, cc, :],
                            cc == 0, cc == 2, [T_pl[cc][ci], T_wgp[g % 2]], [self.T_ps[pb]])
                yb = ysb[blk % 2]
                self.tt("dve", yb[:, :], self.ps[pb][:, 0:384], pscale[:, g * 384:(g + 1) * 384], ALU.mult,
                        [self.T_ps[pb], T_psc], [T_ysb[blk % 2]])
                self.dma(self.yscr[blk * 128:(blk + 1) * 128, g * 384:(g + 1) * 384], yb[:, :], [T_ysb[blk % 2]],
                         [self.T_y[blk]])

    def _epilogue(self, L):
        Tc = self.T_const
        last = (L == 1)
        xst = [self.carve(i * 4096, [128, D], F32) for i in range(2)]
        hsb = [self.carve(8192 + i * 4096, [128, D], F32) for i in range(2)]
        hnb = [self.carve(16384, [128, D], BF16), self.carve(51456, [128, D], BF16)]
        sg = [self.carve(18432 + i * 4096, [128, 2048], BF16) for i in range(2)]
        z = [self.carve(26624 + i * 4096, [128, 2048], BF16) for i in range(2)]
        zT = self.carve(34816, [128, 16, 128], BF16)
        ysb = [self.carve(38912 + i * 3072, [128, MIX], BF16) for i in range(2)]
        qmT = [self.carve(45056 + i * 1024, [128, 4, 128], BF16) for i in range(2)]
        pm = self.carve(47104, [128, 8, 128], BF16)
        om = self.carve(49152, [128, 4, 132], F32)
        stB = self.carve(51264, [128, 16], F32)
        stC = self.carve(51328, [128, 32], F32)
        junk = zT.rearrange("p e t -> p (e t)")[:, 0:D]
        T_xst = [T(), T()]
        T_hsb = [T(), T()]
        T_hnb = [T(), T()]
        T_sg = [T(), T()]
        T_z = [T(), T()]
        T_zT = T()
        T_ysb = [T(), T()]
        T_qm = [T(), T()]
        T_pm = T()
        T_stB = T()
        T_stC = T()
        T_om = T()

        self.load_epi_weights(L, ("gt", "qm", "wo"))
        gsrc = self.v_fing[0:1, :] if last else self.v_lng[1:2, :]
        self.dma(self.gbuf[:], self.bc(gsrc), [], [self.T_g])

        def stage_a(blk):
            i = blk % 2
            Thn = self.T_hn[blk]
            rows = slice(blk * 128, (blk + 1) * 128)
            self.dma(ysb[i][:, :], self.yscr[rows, :], [self.T_y[blk]], [T_ysb[i]])
            for hd in range(4):
                for c in range(8):
                    self.mm(self.ps[0][:, hd * 128:(hd + 1) * 128], self.wqm[:, c, hd * 128:(hd + 1) * 128],
                            self.hnT[:, c, rows], c == 0, c == 7, [self.T_wqm, Thn], [self.T_ps[0]], skip=True)
            self.cp("dve", qmT[i][:, :, :], self.ps[0][:, :].rearrange("p (h t) -> p h t", h=4), [self.T_ps[0]],
                    [T_qm[i]])
            for n4 in range(4):
                pi = 1 + n4 % 2
                for c in range(8):
                    self.mm(self.ps[pi][:, :], self.hnT[:, c, rows], self.wgt[:, c, n4 * 512:(n4 + 1) * 512],
                            c == 0, c == 7, [self.T_regB_w, Thn], [self.T_ps[pi]])
                self.act(sg[i][:, n4 * 512:(n4 + 1) * 512], self.ps[pi][:, :], AF.Silu, [self.T_ps[pi]], [T_sg[i]])

        def stage_b(blk):
            i = blk % 2
            rows = slice(blk * 128, (blk + 1) * 128)
            if L == 0 or not self.do_l0:
                self.dma(xst[i][:, :], self.xl[rows, :], [], [T_xst[i]])
            else:
                self.dma(xst[i][:, :], self.h1scr[rows, :], [self.T_h1[blk]], [T_xst[i]])
            for hd in range(4):
                for mc in range(2):
                    pi = 3 + hd // 2
                    col = ((hd % 2) * 2 + mc) * 128
                    self.mm(self.ps[pi][:, col:col + 128], self.kmT[L][:, hd, mc * 128:(mc + 1) * 128],
                            qmT[i][:, hd, :], True, True, [self.T_km[L], T_qm[i]], [self.T_ps[pi]], skip=True)
            for half in range(2):
                pi = 3 + half
                self.act(pm[:, half * 4:(half + 1) * 4, :], self.ps[pi][:, :].rearrange("p (a t) -> p a t", a=4),
                         AF.Exp, [self.T_ps[pi]], [T_pm], scale=128 ** -0.5)
            for hd in range(4):
                pi = 5 + hd // 2
                col = (hd % 2) * 132
                for mc in range(2):
                    self.mm(self.ps[pi][:, col:col + 129], pm[:, hd * 2 + mc, :], self.vm[L][:, mc, hd, 0:129],
                            mc == 0 and hd % 2 == 0, mc == 1, [T_pm, self.T_vm[L]], [self.T_ps[pi]], skip=True)
            for half in range(2):
                self.cp("dve", om[:, half * 2:(half + 1) * 2, :],
                        self.ps[5 + half][:, 0:264].rearrange("p (a n) -> p a n", a=2), [self.T_ps[5 + half]], [T_om])
            self.recip(stB[:, 0:4].rearrange("p (a o) -> p a o", o=1), om[:, :, 128:129], [T_om], [T_stB])
            self.tt("pool", z[i][:, 0:MIX], ysb[i][:, :], sg[i][:, 0:MIX], ALU.mult, [T_ysb[i], T_sg[i]], [T_z[i]])
            for hd in range(4):
                self.stt(z[i][:, MIX + hd * 128:MIX + (hd + 1) * 128], om[:, hd, 0:128], stB[:, hd:hd + 1],
                         sg[i][:, MIX + hd * 128:MIX + (hd + 1) * 128], ALU.mult, ALU.mult,
                         [T_om, T_stB, T_sg[i]], [T_z[i]])

        def stage_c(blk):
            i = blk % 2
            rows = slice(blk * 128, (blk + 1) * 128)
            for half in range(2):
                pi = 7
                psb = self.ps[pi][:].bitcast(BF16)
                for e in range(8):
                    ee = half * 8 + e
                    self.tr(psb[:, e * 128:(e + 1) * 128], z[i][:, ee * 128:(ee + 1) * 128], [T_z[i], Tc],
                            [self.T_ps[pi]])
                eng = "act" if half == 0 else "dve"
                self.cp(eng, zT[:, half * 8:(half + 1) * 8, :], psb.rearrange("p (e t) -> p e t", e=8),
                        [self.T_ps[pi]], [T_zT])
            for n2 in range(2):
                pi = 5 + n2
                for e in range(16):
                    self.mm(self.ps[pi][:, :], zT[:, e, :], self.wo[:, e, n2 * 512:(n2 + 1) * 512], e == 0, e == 15,
                            [T_zT, self.T_cs], [self.T_ps[pi]])
                self.tt("dve", hsb[i][:, n2 * 512:(n2 + 1) * 512], self.ps[pi][:, :],
                        xst[i][:, n2 * 512:(n2 + 1) * 512], ALU.add, [self.T_ps[pi], T_xst[i]], [T_hsb[i]])
            if last:
                self.norm_block(hsb[i][:, :], T_hsb[i], junk, T_zT, stC, T_stC, 2 * (blk % 8), self.gbuf[:],
                                self.T_g, dst_f32=xst[i][:, :], Tdst=T_xst[i])
                self.dma(self.out[rows, :], xst[i][:, :], [T_xst[i]], [self.T_out])
            else:
                if self.h1scr is not None:
                    self.dma(self.h1scr[rows, :], hsb[i][:, :], [T_hsb[i]], [self.T_h1[blk]])
                else:
                    self.dma(self.out[rows, :], hsb[i][:, :], [T_hsb[i]], [self.T_out])
                    return
                self.norm_block(hsb[i][:, :], T_hsb[i], junk, T_zT, stC, T_stC, 2 * (blk % 8), self.gbuf[:],
                                self.T_g, hnb=hnb[i][:, :], Thnb=T_hnb[i])

        def stage_d(blk):
            if last or self.h1scr is None:
                return
            self.transpose_block(hnb[blk % 2], T_hnb[blk % 2], blk, 0)

        for it in range(NB_OWN + 3):
            if it < NB_OWN:
                stage_a(it)
            if 0 <= it - 1 < NB_OWN:
                stage_b(it - 1)
            if 0 <= it - 2 < NB_OWN:
                stage_c(it - 2)
            if 0 <= it - 3 < NB_OWN:
                stage_d(it - 3)


def _own_ctx_blocks(j):
    if j == 0:
        own = list(range(0, 8)) + list(range(23, 32))
        ctx = list(range(8, 23))
    else:
        own = list(range(7, 24))
        ctx = list(range(0, 7)) + list(range(24, 32))
    return own, ctx


def _blocks_to_rows(blks):
    return np.concatenate([np.arange(b * 128, (b + 1) * 128) for b in blks])


def _consts(j):
    ident = np.eye(128, dtype=np.float32)
    rmat = np.zeros((128, 128), np.float32)
    for f in range(128):
        if (f % 64) < 32:
            rmat[f + 32, f] = -1.0
        else:
            rmat[f - 32, f] = 1.0
    pp, cc = np.meshgrid(np.arange(128), np.arange(128), indexing="ij")
    tri = (pp <= cc).astype(np.float32)
    inv = (np.float32(10000.0) ** (-np.arange(0, 64, 2, dtype=np.float32) / np.float32(64))).astype(np.float32)
    invf = inv[np.arange(128) % 32].reshape(128, 1).astype(np.float32)
    bias = np.zeros((128, 22), np.float32)
    if j == 0:
        bias[:, 0:7] = NEG
    else:
        bias[:, 7 + 7:22] = NEG
    return ident, rmat, tri, invf, bias


def _prep_weights(inp):
    f = np.float32
    aw = np.asarray(inp["attn_w_in"], f)[0]
    wq, wk, wv = aw[:, 0:1536], aw[:, 1536:3072], aw[:, 3072:4608]
    w_h0 = np.empty((NH, 128, 8, 384), f)
    for h in range(NH):
        cat = np.concatenate([wq[:, h * 128:(h + 1) * 128], wk[:, h * 128:(h + 1) * 128],
                              wv[:, h * 128:(h + 1) * 128]], axis=1)
        w_h0[h] = cat.reshape(8, 128, 384).transpose(1, 0, 2)
    pw = np.asarray(inp["pool_w_in"], f)[0]

    def pk(w, k):
        n = w.shape[1]
        return np.ascontiguousarray(w.reshape(k, 128, n).transpose(1, 0, 2)).reshape(128, k * n)

    w_qm = np.stack([pk(aw[:, 4608:5120], 8), pk(pw[:, 1536:2048], 8)])
    w_gt = np.stack([pk(aw[:, 5120:7168], 8), pk(pw[:, 2048:4096], 8)])
    wo = np.asarray(inp["w_out"], f)
    w_o = np.stack([pk(wo[0], 16), pk(wo[1], 16)])
    kv = np.asarray(inp["mem_w_kv"], f)
    w_kv = np.stack([pk(kv[0], 8), pk(kv[1], 8)])
    w_u = np.stack([pk(pw[:, ft * 128:(ft + 1) * 128], 8) for ft in range(12)])
    gp = np.asarray(inp["pool_w_group"], f)[0]
    w_gp = np.stack([pk(gp[g], 3) for g in range(4)])
    return dict(w_h0=np.ascontiguousarray(w_h0).reshape(NH, 128, 8 * 384), w_qm=w_qm, w_gt=w_gt, w_o=w_o,
                w_kv=w_kv, w_u=w_u, w_gp=w_gp)


_NC_CACHE = {}


def _get_nc(do_l0, do_l1):
    key = (do_l0, do_l1)
    if key not in _NC_CACHE:
        lam0 = 0.8 - 0.6 * math.exp(-0.3 * 0)
        _NC_CACHE[key] = Builder(do_l0, do_l1, lam0).build()
    return _NC_CACHE[key]


def _in_maps(inp, xsrc, full_tokens):
    f = np.float32
    wts = _prep_weights(inp)
    x = np.asarray(xsrc, f)
    mem = np.asarray(inp["mem"], f)
    pos = np.asarray(inp["positions"], np.int32)
    maps = []
    for core in range(8):
        b, j = core // 2, core % 2
        own, ctx = _own_ctx_blocks(j)
        rows_own = _blocks_to_rows(own)
        rows_all = np.concatenate([rows_own, _blocks_to_rows(ctx)])
        ident, rmat, tri, invf, bias = _consts(j)
        m = dict(wts)
        if full_tokens:
            m["xl"] = np.ascontiguousarray(x[b][rows_all])
        else:
            m["xl"] = np.ascontiguousarray(x[core])
        m["posl"] = np.ascontiguousarray(pos[b][rows_all]).reshape(1, TOK_ALL)
        m["meml"] = np.ascontiguousarray(mem[b])
        m["c_ident"] = ident
        m["c_rmat"] = rmat
        m["c_tri"] = tri
        m["c_invf"] = invf
        m["c_bias"] = bias
        m["v_lng"] = np.asarray(inp["ln_g"], f)
        m["v_memg"] = np.asarray(inp["mem_norm_g"], f).reshape(1, D)
        m["v_fing"] = np.asarray(inp["final_g"], f).reshape(1, D)
        m["v_subg"] = np.asarray(inp["attn_subln_g"], f).reshape(1, 128)
        m["v_pscale"] = np.asarray(inp["pool_scale"], f).reshape(1, MIX)
        m["v_lam"] = np.asarray(inp["attn_lambda"], f).reshape(1, 256)
        maps.append(m)
    return maps


def _assemble(res):
    out = np.empty((4, TOK_ALL, D), np.float32)
    for core in range(8):
        b, j = core // 2, core % 2
        own, _ = _own_ctx_blocks(j)
        o = res[core]["out"]
        for li, gb in enumerate(own):
            if j == 0 and li == 8:
                continue
            if j == 1 and li == 0:
                continue
            out[b, gb * 128:(gb + 1) * 128] = o[li * 128:(li + 1) * 128]
    return out


FUSED = True


def kernel(**inputs):
    if FUSED:
        nc = _get_nc(True, True)
        res = run_bass_kernel_spmd(nc, _in_maps(inputs, inputs["x"], True), core_ids=list(range(8)))
        return _assemble(res.results)
    nc0 = _get_nc(True, False)
    r0 = run_bass_kernel_spmd(nc0, _in_maps(inputs, inputs["x"], True), core_ids=list(range(8)))
    h1 = [r0.results[c]["out"] for c in range(8)]
    nc1 = _get_nc(False, True)
    r1 = run_bass_kernel_spmd(nc1, _in_maps(inputs, h1, False), core_ids=list(range(8)))
    return _assemble(r1.results)
```

```python
import math
from contextlib import ExitStack

import numpy as np
import concourse.bass as bass
import concourse.mybir as mybir
from concourse.bass_utils import run_bass_kernel_spmd

F32 = mybir.dt.float32
BF16 = mybir.dt.bfloat16
I32 = mybir.dt.int32
AF = mybir.ActivationFunctionType
ALU = mybir.AluOpType

NDMASEM = 14
DMA_SLOTS = {"sp": list(range(0, 8)), "pool": list(range(8, 14)), "act": list(range(0, 8))}

D = 1024
NB_OWN = 17
NB_CTX = 15
TOK_OWN = NB_OWN * 128
TOK_CTX = NB_CTX * 128
TOK_ALL = 4096
NH = 12
EPS = 1e-6
MIX = 1536
NEG = -30000.0
TWO_PI = 2.0 * math.pi
CW1 = 6.28125
CW2 = TWO_PI - CW1
OWN_GROUPS = [(0, 512), (512, 512), (1024, 512), (1536, 512), (2048, 128)]
CTX_GROUPS = [(2176, 512), (2688, 512), (3200, 512), (3712, 384)]
QGROUPS = [(0, 4, 7, 0), (4, 4, 7, 0), (8, 4, 15, 1), (12, 4, 15, 1), (16, 1, 15, 1)]
POOL_WINDOWS = (2, 4, 8, 16)
POOL_CHUNKS = [(0, 1024), (1024, 1152)]


class T:
    __slots__ = ("name", "w", "r", "excl")

    def __init__(self, name="", excl=False):
        self.name = name
        self.w = None
        self.r = {}
        self.excl = excl


class Op:
    __slots__ = ("eng", "fn", "deps", "veng", "idx", "signal", "count", "is_dma", "dval")


class Prog:
    ENGS = ("pe", "act", "dve", "pool", "sp")

    def __init__(self):
        self.streams = {e: [] for e in self.ENGS}
        self.ndma = {"sp": 0, "pool": 0, "act": 0}
        self.dma_cnt = [0] * NDMASEM
        self.guard = None

    def add(self, eng, fn, reads=(), writes=(), dma=False):
        o = Op()
        o.eng = eng
        o.fn = fn
        o.is_dma = dma
        o.signal = False
        o.count = 0
        o.dval = 0
        reads = list(reads)
        if self.guard is not None:
            reads.append(self.guard)
        deps = {}
        for t in reads:
            if t.w is not None:
                deps[id(t.w)] = t.w
            if t.excl:
                for p in t.r.values():
                    if p.eng != eng:
                        deps[id(p)] = p
        for t in writes:
            if t.w is not None:
                deps[id(t.w)] = t.w
            for p in t.r.values():
                deps[id(p)] = p
        o.deps = list(deps.values())
        if dma:
            slots = DMA_SLOTS[eng]
            slot = slots[self.ndma[eng] % len(slots)]
            self.ndma[eng] += 1
            self.dma_cnt[slot] += 1
            o.veng = ("dma", slot)
            o.dval = 16 * self.dma_cnt[slot]
        else:
            o.veng = eng
        for t in reads:
            t.r[o.veng] = o
        for t in writes:
            t.w = o
            t.r = {}
        o.idx = len(self.streams[eng])
        self.streams[eng].append(o)
        return o

    def emit(self, nc):
        for st in self.streams.values():
            for o in st:
                for d in o.deps:
                    d.signal = True
        for e, st in self.streams.items():
            c = 0
            for o in st:
                if o.is_dma:
                    continue
                if o.signal:
                    c += 1
                o.count = c
        with ExitStack() as es:
            sems = {e: es.enter_context(nc.semaphore("s_" + e)) for e in self.ENGS}
            dsems = [es.enter_context(nc.semaphore("d%d" % i)) for i in range(NDMASEM)]
            block = es.enter_context(nc.Block())
            streams = self.streams
            dma_cnt = self.dma_cnt

            def run(e, eng):
                waited = {}
                for o in streams[e]:
                    need = {}
                    for d in o.deps:
                        if d.is_dma:
                            key = d.veng
                            val = d.dval
                        else:
                            key = d.eng
                            val = d.count
                            if d.eng == e:
                                if e == "pe":
                                    continue
                                if o.idx - d.idx > 3:
                                    continue
                        if need.get(key, 0) < val:
                            need[key] = val
                    if o.is_dma and o.dval > 16:
                        if need.get(o.veng, 0) < o.dval - 16:
                            need[o.veng] = o.dval - 16
                    for key, val in need.items():
                        if waited.get(key, 0) >= val:
                            continue
                        sem = dsems[key[1]] if isinstance(key, tuple) else sems[key]
                        eng.wait_ge(sem, val)
                        waited[key] = val
                    ins = o.fn(eng)
                    if o.is_dma:
                        ins.then_inc(dsems[o.veng[1]], 16)
                    elif o.signal:
                        ins.then_inc(sems[e], 1)
                if e == "sp":
                    for slot in range(NDMASEM):
                        if dma_cnt[slot] > 0:
                            eng.wait_ge(dsems[slot], 16 * dma_cnt[slot])

            @block.tensor
            def _(eng):
                run("pe", eng)

            @block.scalar
            def _(eng):
                run("act", eng)

            @block.vector
            def _(eng):
                run("dve", eng)

            @block.gpsimd
            def _(eng):
                run("pool", eng)

            @block.sync
            def _(eng):
                run("sp", eng)


class Builder:
    def __init__(self, do_l0=True, do_l1=True, lambda_init0=0.2):
        self.do_l0 = do_l0
        self.do_l1 = do_l1
        self.lambda_init0 = lambda_init0
        self.p = Prog()
        self.epi_loaded = set()
        self.nc = bass.Bass("TRN2", target_bir_lowering=False)

    def mm(self, out, lhsT, rhs, start, stop, reads, writes, skip=False):
        self.p.add("pe", lambda e: e.matmul(out, lhsT=lhsT, rhs=rhs, start=start, stop=stop,
                                            skip_group_check=skip), reads, writes)

    def tr(self, out, in_, reads, writes):
        ident = self.identb
        self.p.add("pe", lambda e: e.transpose(out=out, in_=in_, identity=ident[:]), reads, writes)

    def act(self, out, in_, func, reads, writes, bias=None, scale=None, accum=None):
        kw = {}
        if bias is not None:
            kw["bias"] = bias
        if scale is not None:
            kw["scale"] = scale
        if accum is not None:
            kw["accum_out"] = accum
        self.p.add("act", lambda e: e.activation(out=out, in_=in_, func=func, **kw), reads, writes)

    def ts(self, eng, out, in0, s1, op0, reads, writes, s2=None, op1=None):
        if op1 is None:
            self.p.add(eng, lambda e: e.tensor_scalar(out=out, in0=in0, scalar1=s1, scalar2=None, op0=op0),
                       reads, writes)
        else:
            self.p.add(eng, lambda e: e.tensor_scalar(out=out, in0=in0, scalar1=s1, scalar2=s2, op0=op0, op1=op1),
                       reads, writes)

    def tt(self, eng, out, in0, in1, op, reads, writes):
        self.p.add(eng, lambda e: e.tensor_tensor(out=out, in0=in0, in1=in1, op=op), reads, writes)

    def stt(self, out, in0, scalar, in1, op0, op1, reads, writes):
        self.p.add("dve", lambda e: e.scalar_tensor_tensor(out=out, in0=in0, scalar=scalar, in1=in1,
                                                           op0=op0, op1=op1), reads, writes)

    def cp(self, eng, out, in_, reads, writes):
        if eng == "act":
            self.p.add("act", lambda e: e.copy(out=out, in_=in_), reads, writes)
        else:
            self.p.add(eng, lambda e: e.tensor_copy(out=out, in_=in_), reads, writes)

    def memset(self, eng, ap, val, writes):
        self.p.add(eng, lambda e: e.memset(ap, val), (), writes)

    def recip(self, out, in_, reads, writes):
        self.p.add("dve", lambda e: e.reciprocal(out=out, in_=in_), reads, writes)

    def dma(self, out, in_, reads, writes, q="sp"):
        self.p.add(q, lambda e: e.dma_start(out=out, in_=in_), reads, writes, dma=True)

    def bc(self, ap):
        b = ap.partition_broadcast(128)
        return b.rearrange("p o n -> p (o n)")

    def dbg(self, name, ap, reads):
        import os
        if not os.environ.get("KDBG"):
            return
        shape = [int(x) for x in ap.shape]
        d = self.nc.dram_tensor("dbg_" + name, shape, ap.dtype, kind="ExternalOutput").ap()
        self.dma(d, ap, reads, [T()])

    def load_epi_weights(self, L, which):
        g = self.p.guard
        self.p.guard = None
        for w in which:
            if (L, w) in self.epi_loaded:
                continue
            self.epi_loaded.add((L, w))
            if w == "gt":
                ctxT = [self.T_hn[b] for b in range(17, 32)]
                self.dma(self.regB[:, :].rearrange("p (c n) -> p c n", c=16),
                         self.w_gt[L].rearrange("p (c n) -> p c n", c=16), ctxT, [self.T_regB_w] + ctxT, q="pool")
            elif w == "qm":
                self.dma(self.wqm[:], self.w_qm[L].rearrange("p (c n) -> p c n", c=8), [], [self.T_wqm], q="pool")
            else:
                self.dma(self.wo, self.w_o[L].rearrange("p (e d) -> p e d", e=16), [], [self.T_cs], q="pool")
        self.p.guard = g

    def fence(self):
        g = self.p.guard
        self.p.guard = None
        fz = self.fz
        self.p.add("pool", lambda e: e.memset(fz[:], 0.0), (), [g])
        self.p.guard = g

    def build(self):
        nc = self.nc
        with ExitStack() as es:
            self.es = es
            self._declare(es)
            self._setup()
            self.fence()
            import os
            ph = int(os.environ.get("KPHASE", "9"))
            if self.do_l0 and ph >= 1:
                self.load_epi_weights(0, ("qm",))
                self._l0_mixer()
                self.fence()
                if ph >= 2:
                    self._epilogue(0)
                    if self.do_l1:
                        self.load_epi_weights(1, ("gt", "qm", "wo"))
                    self.fence()
            if self.do_l1:
                self._l1_mixer()
                self.fence()
                self._epilogue(1)
            self.p.emit(nc)
        return nc

    def _declare(self, es):
        nc = self.nc
        di = lambda n, s, d: nc.dram_tensor(n, s, d, kind="ExternalInput").ap()
        self.xl = di("xl", [TOK_ALL if self.do_l0 else TOK_OWN, D], F32)
        self.posl = di("posl", [1, TOK_ALL], I32)
        self.meml = di("meml", [256, D], F32)
        self.c_ident = di("c_ident", [128, 128], F32)
        self.c_rmat = di("c_rmat", [128, 128], F32)
        self.c_tri = di("c_tri", [128, 128], F32)
        self.c_invf = di("c_invf", [128, 1], F32)
        self.c_bias = di("c_bias", [128, 22], F32)
        self.v_lng = di("v_lng", [2, D], F32)
        self.v_memg = di("v_memg", [1, D], F32)
        self.v_fing = di("v_fing", [1, D], F32)
        self.v_subg = di("v_subg", [1, 128], F32)
        self.v_pscale = di("v_pscale", [1, MIX], F32)
        self.v_lam = di("v_lam", [1, 256], F32)
        self.w_h0 = di("w_h0", [NH, 128, 8 * 384], F32)
        self.w_qm = di("w_qm", [2, 128, 8 * 512], F32)
        self.w_gt = di("w_gt", [2, 128, 8 * 2048], F32)
        self.w_o = di("w_o", [2, 128, 16 * 1024], F32)
        self.w_kv = di("w_kv", [2, 128, 8 * 1024], F32)
        self.w_u = di("w_u", [12, 128, 8 * 128], F32)
        self.w_gp = di("w_gp", [4, 128, 3 * 384], F32)
        self.out = nc.dram_tensor("out", [TOK_OWN, D], F32, kind="ExternalOutput").ap()
        import os
        if os.environ.get("KDUMP"):
            self.yscr = nc.dram_tensor("yscr", [TOK_OWN, MIX], BF16, kind="ExternalOutput").ap()
        else:
            self.yscr = nc.dram_tensor("yscr", [TOK_OWN, MIX], BF16).ap()
        if self.do_l0 and self.do_l1:
            self.h1scr = nc.dram_tensor("h1scr", [TOK_OWN, D], F32).ap()
        else:
            self.h1scr = None

        sb = lambda n, s, d: es.enter_context(nc.sbuf_tensor(n, s, d))
        self.hnT = sb("hnT", [128, 8, TOK_OWN], BF16)
        self.regB = sb("regB", [128, 16384], BF16)
        self.regC = sb("regC", [128, 8192], F32)
        self.regD = sb("regD", [128, 13440], F32)
        self.wslot = [sb("wslot%d" % i, [128, 8, 384], BF16) for i in range(2)]
        self.wqm = sb("wqm", [128, 8, 512], BF16)
        self.kmT = [sb("kmT%d" % i, [128, 4, 256], BF16) for i in range(2)]
        self.vm = [sb("vm%d" % i, [128, 2, 4, 132], BF16) for i in range(2)]
        self.gbuf = sb("gbuf", [128, D], F32)
        self.subg = sb("subg", [128, 128], F32)
        self.identb = sb("identb", [128, 128], BF16)
        self.rmatb = sb("rmatb", [128, 128], BF16)
        self.trib = sb("trib", [128, 128], BF16)
        self.cst = sb("cst", [128, 32], F32)
        self.fz = sb("fz", [128, 2], F32)
        self.ps = [es.enter_context(nc.psum_tensor("ps%d" % i, [128, 512], F32)) for i in range(8)]

        self.hnT_ctx = self.regB[:, 0:8 * TOK_CTX].rearrange("p (c t) -> p c t", c=8)
        self.wgt = self.regB[:, :].rearrange("p (c n) -> p c n", c=8)
        self.cos = self.regC[:, 0:4096]
        self.sin = self.regC[:, 4096:8192]
        self.wo = self.regC[:, :].bitcast(BF16).rearrange("p (e d) -> p e d", e=16)

        self.T_hn = [T("hn%d" % i) for i in range(32)]
        self.T_regB_w = T("wgt")
        self.T_cs = T("cossin/wo")
        self.T_ws = [T("ws0"), T("ws1")]
        self.T_wqm = T("wqm")
        self.T_km = [T(), T()]
        self.T_vm = [T(), T()]
        self.T_g = T("gbuf")
        self.T_const = T("const")
        self.T_ps = [T("ps%d" % i, excl=True) for i in range(8)]
        self.T_y = [T("yscr%d" % i) for i in range(NB_OWN)]
        self.T_h1 = [T("h1scr%d" % i) for i in range(NB_OWN)]
        self.T_out = T("out")
        self.p.guard = T("guard")

    def carve(self, off_bytes, shape, dtype):
        free = list(shape[1:])
        n = int(np.prod(free))
        nbytes = n * (4 if dtype in (F32, I32) else 2)
        assert off_bytes % 4 == 0 and off_bytes + nbytes <= 13440 * 4, (off_bytes, nbytes)
        a = self.regD[:, off_bytes // 4:(off_bytes + nbytes + 3) // 4]
        if dtype != F32:
            a = a.bitcast(dtype)
        a = a[:, 0:n]
        if len(free) == 2:
            return a.rearrange("p (a b) -> p a b", a=free[0])
        if len(free) == 3:
            return a.rearrange("p (a b c) -> p a b c", a=free[0], b=free[1])
        return a

    def hn_ap(self, c, t0, n):
        if t0 < TOK_OWN:
            assert t0 + n <= TOK_OWN
            return self.hnT[:, c, t0:t0 + n]
        return self.hnT_ctx[:, c, t0 - TOK_OWN:t0 - TOK_OWN + n]

    def hn_T(self, t0, n):
        return [self.T_hn[b] for b in range(t0 // 128, (t0 + n) // 128)]

    def norm_block(self, src, Tsrc, junk, Tjunk, st, Tst, col, g, Tg, dst_f32=None, Tdst=None,
                   hnb=None, Thnb=None):
        eps_ap = self.cst[:, 1:2]
        ss = st[:, col:col + 1]
        rs = st[:, col + 1:col + 2]
        self.act(junk, src, AF.Square, [Tsrc], [Tjunk, Tst], accum=ss)
        self.act(rs, ss, AF.Ln, [Tst, self.T_const], [Tst], bias=eps_ap, scale=1.0 / D)
        self.act(rs, rs, AF.Exp, [Tst], [Tst], scale=-0.5)
        if hnb is not None:
            self.stt(hnb, src, rs, g, ALU.mult, ALU.mult, [Tsrc, Tst, Tg], [Thnb])
        if dst_f32 is not None:
            self.stt(dst_f32, src, rs, g, ALU.mult, ALU.mult, [Tsrc, Tst, Tg], [Tdst])

    def transpose_block(self, hnb, Thnb, blk, psi):
        psb = self.ps[psi][:].bitcast(BF16)
        for c in range(8):
            self.tr(psb[:, c * 128:(c + 1) * 128], hnb[:, c * 128:(c + 1) * 128],
                    [Thnb, self.T_const], [self.T_ps[psi]])
        if blk < NB_OWN:
            dst = self.hnT[:, :, blk * 128:(blk + 1) * 128]
        else:
            b = blk - NB_OWN
            dst = self.hnT_ctx[:, :, b * 128:(b + 1) * 128]
        src = psb.rearrange("p (c t) -> p c t", c=8)
        eng = "dve" if blk % 2 == 0 else "act"
        self.cp(eng, dst, src, [self.T_ps[psi]], [self.T_hn[blk]])

    def _setup(self):
        p = self.p
        Tc = self.T_const
        tmpf = self.carve(0, [128, 128], F32)
        Ttmp = T()
        for src, dst in ((self.c_ident, self.identb), (self.c_rmat, self.rmatb), (self.c_tri, self.trib)):
            self.dma(tmpf[:, :], src, [], [Ttmp])
            self.cp("dve", dst[:], tmpf[:, :], [Ttmp], [Tc])
        self.memset("dve", self.cst[:, :], 0.0, [Tc])
        self.memset("dve", self.cst[:, 1:2], EPS, [Tc])
        self.memset("dve", self.cst[:, 4:8], -0.5, [Tc])
        self.dma(self.cst[:, 0:1], self.c_invf, [], [Tc])
        self.dma(self.cst[:, 8:30], self.c_bias, [], [Tc])
        lamt = self.carve(1024, [128, 256], F32)
        lt2 = self.carve(2048, [128, 128], F32)
        Tl = T()
        self.dma(lamt[:, :], self.bc(self.v_lam), [], [Tl])
        self.tt("dve", lt2[:, 0:64], lamt[:, 0:64], lamt[:, 64:128], ALU.mult, [Tl], [Tl])
        self.tt("dve", lt2[:, 64:128], lamt[:, 128:192], lamt[:, 192:256], ALU.mult, [Tl], [Tl])
        st0 = self.carve(3072, [128, 16], F32)
        Tst0 = T()
        self.act(lamt[:, 0:64], lt2[:, 0:64], AF.Copy, [Tl], [Tl, Tst0], accum=st0[:, 0:1])
        self.act(lamt[:, 64:128], lt2[:, 64:128], AF.Copy, [Tl], [Tl, Tst0], accum=st0[:, 1:2])
        self.act(st0[:, 2:4], st0[:, 0:2], AF.Exp, [Tst0], [Tst0])
        self.tt("dve", st0[:, 4:5], st0[:, 3:4], st0[:, 2:3], ALU.subtract, [Tst0], [Tst0])
        self.ts("dve", self.cst[:, 2:3], st0[:, 4:5], -self.lambda_init0, ALU.add, [Tst0], [Tc])
        self.dma(self.subg[:], self.bc(self.v_subg), [], [Tc])
        self.ts("dve", self.subg[:], self.subg[:], 1.0 - self.lambda_init0, ALU.mult, [Tc], [Tc])

        xst = [self.carve(4096 + i * 4096, [128, D], F32) for i in range(2)]
        Txst = [T(), T()]
        hnb = [self.carve(12288 + i * 2048, [128, D], BF16) for i in range(2)]
        Thnb = [T(), T()]
        junk = self.carve(16384, [128, D], F32)
        Tjunk = T()
        st = self.carve(20480, [128, 80], F32)
        Tst = T()
        memT = self.carve(20992, [128, 8, 256], BF16)
        Tmem = T()
        wkv = self.carve(25088, [128, 8, 1024], BF16)
        Twkv = T()
        self.dma(self.gbuf[:], self.bc(self.v_memg), [], [self.T_g])
        for mb in range(2):
            i = mb % 2
            self.dma(xst[i][:, :], self.meml[mb * 128:(mb + 1) * 128, :], [], [Txst[i]])
            self.norm_block(xst[i][:, :], Txst[i], junk[:, :], Tjunk, st, Tst, 2 * mb, self.gbuf[:], self.T_g,
                            hnb=hnb[i][:, :], Thnb=Thnb[i])
            psb = self.ps[mb][:].bitcast(BF16)
            for c in range(8):
                self.tr(psb[:, c * 128:(c + 1) * 128], hnb[i][:, c * 128:(c + 1) * 128], [Thnb[i], Tc],
                        [self.T_ps[mb]])
            self.cp("dve", memT[:, :, mb * 128:(mb + 1) * 128], psb.rearrange("p (c t) -> p c t", c=8),
                    [self.T_ps[mb]], [Tmem])
        for L in range(2):
            if (L == 0 and not self.do_l0) or (L == 1 and not self.do_l1):
                continue
            self.dma(wkv, self.w_kv[L].rearrange("p (c n) -> p c n", c=8), [], [Twkv], q="pool")
            for hd in range(4):
                pi = 2 + (hd % 2)
                for c in range(8):
                    self.mm(self.ps[pi][:, 0:256], wkv[:, c, hd * 128:(hd + 1) * 128], memT[:, c, :],
                            c == 0, c == 7, [Twkv, Tmem], [self.T_ps[pi]])
                self.cp("act", self.kmT[L][:, hd, :], self.ps[pi][:, 0:256], [self.T_ps[pi]], [self.T_km[L]])
            self.memset("pool", self.vm[L][:, :, :, 128:129], 1.0, [self.T_vm[L]])
            for mc in range(2):
                pi = 4 + mc
                for c in range(8):
                    self.mm(self.ps[pi][:, :], memT[:, c, mc * 128:(mc + 1) * 128], wkv[:, c, 512:1024],
                            c == 0, c == 7, [Twkv, Tmem], [self.T_ps[pi]])
                self.cp("dve", self.vm[L][:, mc, :, 0:128], self.ps[pi][:, :].rearrange("p (h e) -> p h e", h=4),
                        [self.T_ps[pi]], [self.T_vm[L]])

        if not self.do_l0:
            self.dma(self.gbuf[:], self.bc(self.v_lng[1:2, :]), [], [self.T_g])
            for blk in range(NB_OWN):
                i = blk % 2
                self.dma(xst[i][:, :], self.xl[blk * 128:(blk + 1) * 128, :], [], [Txst[i]])
                self.norm_block(xst[i][:, :], Txst[i], junk[:, :], Tjunk, st, Tst, 4 + 2 * (blk % 8),
                                self.gbuf[:], self.T_g, hnb=hnb[i][:, :], Thnb=Thnb[i])
                self.transpose_block(hnb[i], Thnb[i], blk, 6 + i)
            return

        posi = self.carve(41472, [128, 512], I32)
        Tpi = T()
        posf = self.carve(43520, [128, 512], F32)
        ang = self.carve(45568, [128, 512], F32)
        kf = self.carve(47616, [128, 512], F32)
        ki = self.carve(49664, [128, 384], I32)
        Trope = T()
        invf = self.cst[:, 0:1]
        self.memset("dve", self.cst[:, 3:4], math.pi / 2, [Tc])
        for t0 in range(0, TOK_ALL, 384):
            n = min(384, TOK_ALL - t0)
            self.dma(posi[:, 0:n], self.bc(self.posl[:, t0:t0 + n]), [], [Tpi])
            self.cp("dve", posf[:, 0:n], posi[:, 0:n], [Tpi], [Trope])
            for which, dst in ((0, self.sin), (1, self.cos)):
                if which == 0:
                    self.ts("dve", ang[:, 0:n], posf[:, 0:n], invf, ALU.mult, [Trope, Tc], [Trope])
                else:
                    self.ts("dve", ang[:, 0:n], posf[:, 0:n], invf, ALU.mult, [Trope, Tc], [Trope],
                            s2=self.cst[:, 3:4], op1=ALU.add)
                self.ts("dve", ki[:, 0:n], ang[:, 0:n], 1.0 / TWO_PI, ALU.mult, [Trope], [Trope])
                self.cp("dve", kf[:, 0:n], ki[:, 0:n], [Trope], [Trope])
                self.stt(ang[:, 0:n], kf[:, 0:n], -CW1, ang[:, 0:n], ALU.mult, ALU.add, [Trope], [Trope])
                self.stt(ang[:, 0:n], kf[:, 0:n], -CW2, ang[:, 0:n], ALU.mult, ALU.add, [Trope], [Trope])
                self.ts("dve", ang[:, 0:n], ang[:, 0:n], -3.1415925, ALU.max, [Trope], [Trope],
                        s2=3.1415925, op1=ALU.min)
                self.act(dst[:, t0:t0 + n], ang[:, 0:n], AF.Sin, [Trope], [self.T_cs])

        self.dbg("cst", self.cst[:, :], [Tc])
        self.dbg("cos", self.cos[:, 0:512], [self.T_cs])
        self.dbg("sin", self.sin[:, 0:512], [self.T_cs])
        self.dma(self.gbuf[:], self.bc(self.v_lng[0:1, :]), [Tmem], [self.T_g])
        for blk in range(32):
            i = blk % 2
            self.dma(xst[i][:, :], self.xl[blk * 128:(blk + 1) * 128, :], [], [Txst[i]])
            self.norm_block(xst[i][:, :], Txst[i], junk[:, :], Tjunk, st, Tst, 4 + 2 * (blk % 8),
                            self.gbuf[:], self.T_g, hnb=hnb[i][:, :], Thnb=Thnb[i])
            self.transpose_block(hnb[i], Thnb[i], blk, 6 + i)

    def _l0_mixer(self):
        Tc = self.T_const
        kTz = self.carve(0, [128, 2, TOK_ALL], BF16)
        qT = self.carve(16384, [128, TOK_OWN], BF16)
        vh = self.carve(20736, [128, 32, 132], BF16)
        xb = [self.carve(29184 + i * 1024, [128, 512], BF16) for i in range(2)]
        t1 = [self.carve(31232 + i * 2048, [128, 512], F32) for i in range(2)]
        t2 = [self.carve(35328 + i * 2048, [128, 512], F32) for i in range(2)]
        NPT = 6
        pt = [self.carve(39424 + i * 1024, [128, 512], BF16) for i in range(NPT)]
        osb = self.carve(45568, [128, 3, 396], F32)
        yst = [self.carve(50320 + i * 1024, [128, 4, 128], BF16) for i in range(2)]
        stt_ = self.carve(52368, [128, 64], F32)
        yq = self.carve(52624, [128, 128], F32)
        T_kT = [T() for _ in range(9)]
        T_qT = [T() for _ in range(5)]
        T_vh = [T() for _ in range(8)]
        T_xb = [T() for _ in range(4)]
        T_t1 = [T() for _ in range(4)]
        T_t2 = [T() for _ in range(4)]
        T_pt = [T() for _ in range(NPT)]
        T_osb = T()
        T_yst = [T(), T()]
        T_st = T()
        T_yq = T()
        T_vone = T()

        self.memset("pool", kTz[64:128, 0, :], 0.0, [T_vone])
        self.memset("pool", kTz[0:64, 1, :], 0.0, [T_vone])
        self.memset("pool", vh[:, :, 128:129], 1.0, [T_vone])

        def load_w(h):
            s = h % 2
            self.dma(self.wslot[s][:], self.w_h0[h].rearrange("p (c n) -> p c n", c=8), [], [self.T_ws[s]], q="pool")

        load_w(0)
        allg = OWN_GROUPS + CTX_GROUPS
        LA = 4
        NS = 5
        ptr = 0
        obank = lambda a: 5 + a // 3
        oacc = lambda a: self.ps[obank(a)][:, (a % 3) * 132:(a % 3) * 132 + 129]

        def kblock_T(kb):
            t0 = kb * 128
            for gi_, (g0, gn) in enumerate(allg):
                if g0 <= t0 < g0 + gn:
                    return [T_kT[gi_], T_vh[kb // 4]]
            raise AssertionError

        for h in range(NH):
            s = h % 2
            W = self.wslot[s]
            Tw = self.T_ws[s]
            if h + 1 < NH:
                load_w(h + 1)
            jobs = []
            for kind, glist in (("q", list(range(len(OWN_GROUPS)))), ("k", list(range(9)))):
                for g in glist:
                    g0, gn = allg[g]
                    for off in range(0, gn, 512):
                        jobs.append((kind, g, g0 + off, min(512, gn - off)))

            def front(j):
                kind, g, t0, n = jobs[j]
                col0 = 0 if kind == "q" else 128
                i = j % 2
                pa = j % 4
                for c in range(8):
                    self.mm(self.ps[pa][:, 0:n], W[:, c, col0:col0 + 128], self.hn_ap(c, t0, n),
                            c == 0, c == 7, [Tw] + self.hn_T(t0, n), [self.T_ps[pa]])
                self.cp("act", xb[i][:, 0:n], self.ps[pa][:, 0:n], [self.T_ps[pa]], [T_xb[i]])

            def back(j):
                kind, g, t0, n = jobs[j]
                i = j % 2
                pa, pb = j % 4, 4 + j % 4
                self.mm(self.ps[pb][:, 0:n], self.rmatb[:], xb[i][:, 0:n], True, True, [Tc, T_xb[i]],
                        [self.T_ps[pb]])
                self.tt("dve", t1[i][:, 0:n], self.ps[pa][:, 0:n], self.cos[:, t0:t0 + n], ALU.mult,
                        [self.T_ps[pa], self.T_cs], [T_t1[i]])
                self.tt("dve", t2[i][:, 0:n], self.ps[pb][:, 0:n], self.sin[:, t0:t0 + n], ALU.mult,
                        [self.T_ps[pb], self.T_cs], [T_t2[i]])
                if kind == "q":
                    self.tt("pool", qT[:, t0:t0 + n], t1[i][:, 0:n], t2[i][:, 0:n], ALU.add,
                            [T_t1[i], T_t2[i]], [T_qT[g]])
                else:
                    self.tt("pool", kTz[0:64, 0, t0:t0 + n], t1[i][0:64, 0:n], t2[i][0:64, 0:n], ALU.add,
                            [T_t1[i], T_t2[i], T_vone], [T_kT[g]])
                    self.tt("pool", kTz[64:128, 1, t0:t0 + n], t1[i][64:128, 0:n], t2[i][64:128, 0:n], ALU.add,
                            [T_t1[i], T_t2[i], T_vone], [T_kT[g]])

            front(0)
            for j in range(len(jobs)):
                if j + 1 < len(jobs):
                    front(j + 1)
                back(j)
            for vb in range(8):
                pi = 4 + (vb % 2)
                for j in range(4):
                    blk = vb * 4 + j
                    for c in range(8):
                        self.mm(self.ps[pi][:, j * 128:(j + 1) * 128], self.hn_ap(c, blk * 128, 128),
                                W[:, c, 256:384], c == 0, c == 7, [Tw, self.T_hn[blk]], [self.T_ps[pi]], skip=True)
                eng = "act" if vb % 2 == 0 else "dve"
                self.cp(eng, vh[:, vb * 4:(vb + 1) * 4, 0:128],
                        self.ps[pi][:, :].rearrange("p (j e) -> p j e", j=4), [self.T_ps[pi], T_vone], [T_vh[vb]])
            if h == NH - 1:
                self.load_epi_weights(0, ("gt", "wo"))
            if h == 0:
                self.dbg("qT", qT[:, :], T_qT)
                self.dbg("kTz", kTz[:, :, 0:512], T_kT)
                self.dbg("vh", vh[:, 0:4, :], T_vh)

            steps = []
            for gi, (q0, nq, nctx, btab) in enumerate(QGROUPS):
                klist = []
                for sl in range(nctx):
                    bcol = 8 + (sl if btab == 0 else 7 + sl)
                    klist.append((NB_OWN + sl, self.cst[:, bcol:bcol + 1], None, 0))
                for m in range(q0 + nq):
                    if m < q0:
                        klist.append((m, None, None, 0))
                    else:
                        klist.append((m, None, (m - q0) * 128, (m - q0) * 128))
                nk = len(klist)
                for ki_, (kb, bias_ap, dcol, c0) in enumerate(klist):
                    for mp in range(2):
                        steps.append(dict(gi=gi, ki=ki_, nk=nk, kb=kb, bias=bias_ap, dcol=dcol, c0=c0, mp=mp,
                                          last=(ki_ == nk - 1 and mp == 1)))
            started = {}

            def s_front(idx):
                nonlocal ptr
                st_ = steps[idx]
                q0, nq, nctx, btab = QGROUPS[st_["gi"]]
                ncols = nq * 128
                c0, kb, mp = st_["c0"], st_["kb"], st_["mp"]
                qg_T = []
                for (g0, gn), tq in zip(OWN_GROUPS, T_qT):
                    if g0 < (q0 + nq) * 128 and q0 * 128 < g0 + gn:
                        qg_T.append(tq)
                kT_T = kblock_T(kb)
                si = idx % NS
                pti = ptr % NPT
                ptr += 1
                st_["pti"] = pti
                self.mm(self.ps[si][:, c0:ncols], kTz[:, mp, kb * 128:(kb + 1) * 128],
                        qT[:, q0 * 128 + c0:q0 * 128 + ncols], True, True,
                        [kT_T[0], T_vone] + qg_T, [self.T_ps[si]])
                if st_["bias"] is not None:
                    self.act(pt[pti][:, c0:ncols], self.ps[si][:, c0:ncols], AF.Exp,
                             [self.T_ps[si], Tc], [T_pt[pti]], bias=st_["bias"], scale=0.125)
                else:
                    self.act(pt[pti][:, c0:ncols], self.ps[si][:, c0:ncols], AF.Exp,
                             [self.T_ps[si]], [T_pt[pti]], scale=0.125)
                dcol = st_["dcol"]
                if dcol is not None:
                    self.tt("pool", pt[pti][:, dcol:dcol + 128], pt[pti][:, dcol:dcol + 128],
                            self.trib[:], ALU.mult, [T_pt[pti], Tc], [T_pt[pti]])

            def s_back(idx):
                st_ = steps[idx]
                gi = st_["gi"]
                q0, nq, nctx, btab = QGROUPS[gi]
                c0, kb, mp, pti = st_["c0"], st_["kb"], st_["mp"], st_["pti"]
                kT_T = kblock_T(kb)
                stt_set = started.setdefault(gi, set())
                for qb in range(c0 // 128, nq):
                    a = mp * 4 + qb
                    bk = obank(a)
                    first = bk not in stt_set
                    stt_set.add(bk)
                    self.mm(oacc(a), pt[pti][:, qb * 128:(qb + 1) * 128], vh[:, kb, 0:129],
                            first, st_["ki"] == st_["nk"] - 1, [T_pt[pti], kT_T[1], T_vone], [self.T_ps[bk]],
                            skip=True)
                if st_["last"]:
                    finalize(gi)

            def finalize(gi):
                q0, nq, nctx, btab = QGROUPS[gi]
                for bk in range(3):
                    used = 396 if (bk + 1) * 3 <= 8 else 264
                    self.cp("dve", osb[:, bk, 0:used], self.ps[5 + bk][:, 0:used], [self.T_ps[5 + bk]], [T_osb])
                osf = osb.rearrange("p b n -> p (b n)")
                acc3 = osf[:, 0:8 * 132].rearrange("p (a n) -> p a n", n=132)
                rl = stt_[:, 0:8]
                self.recip(rl.rearrange("p (a o) -> p a o", o=1), acc3[:, 0:8, 128:129], [T_osb], [T_st])
                self.ts("dve", stt_[:, 4:8], stt_[:, 4:8], self.cst[:, 2:3], ALU.mult, [T_st, Tc], [T_st])
                ysl = yst[gi % 2]
                Tys = T_yst[gi % 2]
                for qb in range(nq):
                    o0 = osf[:, qb * 132:qb * 132 + 128]
                    o1 = osf[:, (4 + qb) * 132:(4 + qb) * 132 + 128]
                    self.ts("dve", o0, o0, stt_[:, qb:qb + 1], ALU.mult, [T_osb, T_st], [T_osb])
                    self.stt(o0, o1, stt_[:, 4 + qb:5 + qb], o0, ALU.mult, ALU.add, [T_osb, T_st], [T_osb])
                yv = acc3[:, 0:nq, 0:128]
                sqv = acc3[:, 4:4 + nq, 0:128]
                self.tt("dve", sqv, yv, yv, ALU.mult, [T_osb], [T_osb])
                self.p.add("dve", lambda e, o=stt_[:, 16:16 + nq], i_=sqv: e.tensor_reduce(
                    out=o, in_=i_, axis=mybir.AxisListType.X, op=ALU.add), [T_osb], [T_st])
                self.ts("dve", stt_[:, 16:16 + nq], stt_[:, 16:16 + nq], 1.0 / 128, ALU.mult, [T_st], [T_st],
                        s2=EPS, op1=ALU.add)
                self.tt("pool", stt_[:, 24:24 + nq], stt_[:, 16:16 + nq], self.cst[:, 4:4 + nq], ALU.pow,
                        [T_st, Tc], [T_st])
                for qb in range(nq):
                    o0 = osf[:, qb * 132:qb * 132 + 128]
                    self.stt(ysl[:, qb, :], o0, stt_[:, 24 + qb:25 + qb], self.subg[:], ALU.mult, ALU.mult,
                             [T_osb, T_st, Tc], [Tys])
                dst = self.yscr[q0 * 128:(q0 + nq) * 128, h * 128:(h + 1) * 128].rearrange(
                    "(q p) e -> p q e", p=128)
                self.dma(dst, ysl[:, 0:nq, :], [Tys], [self.T_y[b] for b in range(q0, q0 + nq)])

            nst = len(steps)
            fi = 0
            for bi in range(nst):
                la_cur = LA + 1 if (steps[bi]["gi"] > 0 and steps[bi]["ki"] < 2) else LA
                while fi < nst and fi <= bi + la_cur:
                    s_front(fi)
                    fi += 1
                s_back(bi)

    def _l1_mixer(self):
        Tc = self.T_const
        PADW = 16
        CH = 1152
        U = [self.carve(i * (PADW + CH) * 4, [128, PADW + CH], F32) for i in range(2)]
        A = self.carve(9344, [128, PADW + CH], F32)
        Bf = self.carve(14016, [128, PADW + CH], F32)
        pooledT = self.carve(18688, [128, 3, TOK_OWN], BF16)
        wgp = [self.carve(31744 + i * 2304, [128, 3, 384], BF16) for i in range(2)]
        pscale = self.carve(36352, [128, MIX], F32)
        ysb = [self.carve(42496 + i * 768, [128, 384], BF16) for i in range(2)]
        invc = self.carve(44032, [128, 4, 16], F32)
        T_U = [T(), T()]
        T_A = T()
        T_B = T()
        T_pl = [[T() for _ in range(2)] for _ in range(3)]
        T_wgp = [T(), T()]
        T_psc = T()
        T_ysb = [T(), T()]
        T_inv = T()

        self.dma(pscale[:, :], self.bc(self.v_pscale), [], [T_psc])
        for g, w in enumerate(POOL_WINDOWS):
            for t in range(16):
                self.memset("pool", invc[:, g, t:t + 1], 1.0 / min(t + 1, w), [T_inv])
        for i in range(2):
            self.memset("pool", U[i][:, 0:PADW], 0.0, [T_U[i]])
        self.memset("pool", A[:, 0:PADW], 0.0, [T_A])
        self.memset("pool", Bf[:, 0:PADW], 0.0, [T_B])

        def load_wu(ft):
            s = ft % 2
            self.dma(self.wslot[s][:, :, 0:128], self.w_u[ft].rearrange("p (c n) -> p c n", c=8), [],
                     [self.T_ws[s]], q="pool")

        load_wu(0)
        un = 0
        for g, w in enumerate(POOL_WINDOWS):
            self.dma(wgp[g % 2], self.w_gp[g].rearrange("p (c n) -> p c n", c=3), [], [T_wgp[g % 2]], q="pool")
            for cc in range(3):
                ft = g * 3 + cc
                s = ft % 2
                W = self.wslot[s]
                if ft + 1 < 12:
                    load_wu(ft + 1)
                for ci, (c0, cn) in enumerate(POOL_CHUNKS):
                    u = U[un % 2]
                    Tu = T_U[un % 2]
                    un += 1
                    halo = 0 if c0 == 0 else PADW
                    t = c0 - halo
                    pieces = []
                    while t < c0 + cn:
                        n = min(512, c0 + cn - t)
                        if t < c0:
                            n = halo
                        pieces.append((t, n))
                        t += n
                    for pi_, (t0, n) in enumerate(pieces):
                        pb = pi_ % 2
                        for c in range(8):
                            self.mm(self.ps[pb][:, 0:n], W[:, c, 0:128], self.hn_ap(c, t0, n), c == 0, c == 7,
                                    [self.T_ws[s]] + self.hn_T(t0 - (t0 % 128), 128 * ((t0 % 128 + n + 127) // 128)),
                                    [self.T_ps[pb]])
                        dcol = PADW + (t0 - c0)
                        eng = "act" if pi_ % 2 == 0 else "dve"
                        self.cp(eng, u[:, dcol:dcol + n], self.ps[pb][:, 0:n], [self.T_ps[pb]], [Tu])
                    src, Tsrc = u, Tu
                    bufs = [(A, T_A), (Bf, T_B)]
                    nsteps = g + 1
                    for stp in range(nsteps):
                        sh = 1 << stp
                        dst, Tdst = bufs[stp % 2]
                        eng = "dve" if stp % 2 == 0 else "pool"
                        lo = PADW - (PADW - sh) if False else sh
                        self.tt(eng, dst[:, sh:PADW + cn], src[:, sh:PADW + cn], src[:, 0:PADW + cn - sh], ALU.add,
                                [Tsrc], [Tdst])
                        src, Tsrc = dst, Tdst
                    dstp = pooledT[:, cc, c0:c0 + cn]
                    self.stt(dstp, src[:, PADW:PADW + cn], 1.0 / w, u[:, PADW:PADW + cn], ALU.mult, ALU.subtract,
                             [Tsrc, Tu], [T_pl[cc][ci]])
                    if c0 == 0:
                        tmp = src[:, 0:16]
                        self.tt("dve", tmp, src[:, PADW:PADW + 16], invc[:, g, :], ALU.mult, [Tsrc, T_inv], [Tsrc])
                        self.tt("dve", pooledT[:, cc, 0:16], tmp, u[:, PADW:PADW + 16], ALU.subtract,
                                [Tsrc, Tu], [T_pl[cc][ci]])
            for blk in range(NB_OWN):
                pb = 2 + blk % 2
                ci = 0 if blk < 8 else 1
                for cc in range(3):
                    self.mm(self.ps[pb][:, 0:384], pooledT[:, cc, blk * 128:(blk + 1) * 128], wgp[g % 2][:, cc, :],
                            cc == 0, cc == 2, [T_pl[cc][ci], T_wgp[g % 2]], [self.T_ps[pb]])
                yb = ysb[blk % 2]
                self.tt("dve", yb[:, :], self.ps[pb][:, 0:384], pscale[:, g * 384:(g + 1) * 384], ALU.mult,
                        [self.T_ps[pb], T_psc], [T_ysb[blk % 2]])
                self.dma(self.yscr[blk * 128:(blk + 1) * 128, g * 384:(g + 1) * 384], yb[:, :], [T_ysb[blk % 2]],
                         [self.T_y[blk]])

    def _epilogue(self, L):
        Tc = self.T_const
        last = (L == 1)
        xst = [self.carve(i * 4096, [128, D], F32) for i in range(2)]
        hsb = [self.carve(8192 + i * 4096, [128, D], F32) for i in range(2)]
        hnb = [self.carve(16384, [128, D], BF16), self.carve(51456, [128, D], BF16)]
        sg = [self.carve(18432 + i * 4096, [128, 2048], BF16) for i in range(2)]
        z = [self.carve(26624 + i * 4096, [128, 2048], BF16) for i in range(2)]
        zT = self.carve(34816, [128, 16, 128], BF16)
        ysb = [self.carve(38912 + i * 3072, [128, MIX], BF16) for i in range(2)]
        qmT = [self.carve(45056 + i * 1024, [128, 4, 128], BF16) for i in range(2)]
        pm = self.carve(47104, [128, 8, 128], BF16)
        om = self.carve(49152, [128, 4, 132], F32)
        stB = self.carve(51264, [128, 16], F32)
        stC = self.carve(51328, [128, 32], F32)
        junk = zT.rearrange("p e t -> p (e t)")[:, 0:D]
        T_xst = [T(), T()]
        T_hsb = [T(), T()]
        T_hnb = [T(), T()]
        T_sg = [T(), T()]
        T_z = [T(), T()]
        T_zT = T()
        T_ysb = [T(), T()]
        T_qm = [T(), T()]
        T_pm = T()
        T_stB = T()
        T_stC = T()
        T_om = T()

        self.load_epi_weights(L, ("gt", "qm", "wo"))
        gsrc = self.v_fing[0:1, :] if last else self.v_lng[1:2, :]
        self.dma(self.gbuf[:], self.bc(gsrc), [], [self.T_g])

        def stage_a(blk):
            i = blk % 2
            Thn = self.T_hn[blk]
            rows = slice(blk * 128, (blk + 1) * 128)
            self.dma(ysb[i][:, :], self.yscr[rows, :], [self.T_y[blk]], [T_ysb[i]])
            for hd in range(4):
                for c in range(8):
                    self.mm(self.ps[0][:, hd * 128:(hd + 1) * 128], self.wqm[:, c, hd * 128:(hd + 1) * 128],
                            self.hnT[:, c, rows], c == 0, c == 7, [self.T_wqm, Thn], [self.T_ps[0]], skip=True)
            self.cp("dve", qmT[i][:, :, :], self.ps[0][:, :].rearrange("p (h t) -> p h t", h=4), [self.T_ps[0]],
                    [T_qm[i]])
            for n4 in range(4):
                pi = 1 + n4 % 2
                for c in range(8):
                    self.mm(self.ps[pi][:, :], self.hnT[:, c, rows], self.wgt[:, c, n4 * 512:(n4 + 1) * 512],
                            c == 0, c == 7, [self.T_regB_w, Thn], [self.T_ps[pi]])
                self.act(sg[i][:, n4 * 512:(n4 + 1) * 512], self.ps[pi][:, :], AF.Silu, [self.T_ps[pi]], [T_sg[i]])

        def stage_b(blk):
            i = blk % 2
            rows = slice(blk * 128, (blk + 1) * 128)
            if L == 0 or not self.do_l0:
                self.dma(xst[i][:, :], self.xl[rows, :], [], [T_xst[i]])
            else:
                self.dma(xst[i][:, :], self.h1scr[rows, :], [self.T_h1[blk]], [T_xst[i]])
            for hd in range(4):
                for mc in range(2):
                    pi = 3 + hd // 2
                    col = ((hd % 2) * 2 + mc) * 128
                    self.mm(self.ps[pi][:, col:col + 128], self.kmT[L][:, hd, mc * 128:(mc + 1) * 128],
                            qmT[i][:, hd, :], True, True, [self.T_km[L], T_qm[i]], [self.T_ps[pi]], skip=True)
            for half in range(2):
                pi = 3 + half
                self.act(pm[:, half * 4:(half + 1) * 4, :], self.ps[pi][:, :].rearrange("p (a t) -> p a t", a=4),
                         AF.Exp, [self.T_ps[pi]], [T_pm], scale=128 ** -0.5)
            for hd in range(4):
                pi = 5 + hd // 2
                col = (hd % 2) * 132
                for mc in range(2):
                    self.mm(self.ps[pi][:, col:col + 129], pm[:, hd * 2 + mc, :], self.vm[L][:, mc, hd, 0:129],
                            mc == 0 and hd % 2 == 0, mc == 1, [T_pm, self.T_vm[L]], [self.T_ps[pi]], skip=True)
            for half in range(2):
                self.cp("dve", om[:, half * 2:(half + 1) * 2, :],
                        self.ps[5 + half][:, 0:264].rearrange("p (a n) -> p a n", a=2), [self.T_ps[5 + half]], [T_om])
            self.recip(stB[:, 0:4].rearrange("p (a o) -> p a o", o=1), om[:, :, 128:129], [T_om], [T_stB])
            self.tt("pool", z[i][:, 0:MIX], ysb[i][:, :], sg[i][:, 0:MIX], ALU.mult, [T_ysb[i], T_sg[i]], [T_z[i]])
            for hd in range(4):
                self.stt(z[i][:, MIX + hd * 128:MIX + (hd + 1) * 128], om[:, hd, 0:128], stB[:, hd:hd + 1],
                         sg[i][:, MIX + hd * 128:MIX + (hd + 1) * 128], ALU.mult, ALU.mult,
                         [T_om, T_stB, T_sg[i]], [T_z[i]])

        def stage_c(blk):
            i = blk % 2
            rows = slice(blk * 128, (blk + 1) * 128)
            for half in range(2):
                pi = 7
                psb = self.ps[pi][:].bitcast(BF16)
                for e in range(8):
                    ee = half * 8 + e
                    self.tr(psb[:, e * 128:(e + 1) * 128], z[i][:, ee * 128:(ee + 1) * 128], [T_z[i], Tc],
                            [self.T_ps[pi]])
                eng = "act" if half == 0 else "dve"
                self.cp(eng, zT[:, half * 8:(half + 1) * 8, :], psb.rearrange("p (e t) -> p e t", e=8),
                        [self.T_ps[pi]], [T_zT])
            for n2 in range(2):
                pi = 5 + n2
                for e in range(16):
                    self.mm(self.ps[pi][:, :], zT[:, e, :], self.wo[:, e, n2 * 512:(n2 + 1) * 512], e == 0, e == 15,
                            [T_zT, self.T_cs], [self.T_ps[pi]])
                self.tt("dve", hsb[i][:, n2 * 512:(n2 + 1) * 512], self.ps[pi][:, :],
                        xst[i][:, n2 * 512:(n2 + 1) * 512], ALU.add, [self.T_ps[pi], T_xst[i]], [T_hsb[i]])
            if last:
                self.norm_block(hsb[i][:, :], T_hsb[i], junk, T_zT, stC, T_stC, 2 * (blk % 8), self.gbuf[:],
                                self.T_g, dst_f32=xst[i][:, :], Tdst=T_xst[i])
                self.dma(self.out[rows, :], xst[i][:, :], [T_xst[i]], [self.T_out])
            else:
                if self.h1scr is not None:
                    self.dma(self.h1scr[rows, :], hsb[i][:, :], [T_hsb[i]], [self.T_h1[blk]])
                else:
                    self.dma(self.out[rows, :], hsb[i][:, :], [T_hsb[i]], [self.T_out])
                    return
                self.norm_block(hsb[i][:, :], T_hsb[i], junk, T_zT, stC, T_stC, 2 * (blk % 8), self.gbuf[:],
                                self.T_g, hnb=hnb[i][:, :], Thnb=T_hnb[i])

        def stage_d(blk):
            if last or self.h1scr is None:
                return
            self.transpose_block(hnb[blk % 2], T_hnb[blk % 2], blk, 0)

        for it in range(NB_OWN + 3):
            if it < NB_OWN:
                stage_a(it)
            if 0 <= it - 1 < NB_OWN:
                stage_b(it - 1)
            if 0 <= it - 2 < NB_OWN:
                stage_c(it - 2)
            if 0 <= it - 3 < NB_OWN:
                stage_d(it - 3)


def _own_ctx_blocks(j):
    if j == 0:
        own = list(range(0, 8)) + list(range(23, 32))
        ctx = list(range(8, 23))
    else:
        own = list(range(7, 24))
        ctx = list(range(0, 7)) + list(range(24, 32))
    return own, ctx


def _blocks_to_rows(blks):
    return np.concatenate([np.arange(b * 128, (b + 1) * 128) for b in blks])


def _consts(j):
    ident = np.eye(128, dtype=np.float32)
    rmat = np.zeros((128, 128), np.float32)
    for f in range(128):
        if (f % 64) < 32:
            rmat[f + 32, f] = -1.0
        else:
            rmat[f - 32, f] = 1.0
    pp, cc = np.meshgrid(np.arange(128), np.arange(128), indexing="ij")
    tri = (pp <= cc).astype(np.float32)
    inv = (np.float32(10000.0) ** (-np.arange(0, 64, 2, dtype=np.float32) / np.float32(64))).astype(np.float32)
    invf = inv[np.arange(128) % 32].reshape(128, 1).astype(np.float32)
    bias = np.zeros((128, 22), np.float32)
    if j == 0:
        bias[:, 0:7] = NEG
    else:
        bias[:, 7 + 7:22] = NEG
    return ident, rmat, tri, invf, bias


def _prep_weights(inp):
    f = np.float32
    aw = np.asarray(inp["attn_w_in"], f)[0]
    wq, wk, wv = aw[:, 0:1536], aw[:, 1536:3072], aw[:, 3072:4608]
    w_h0 = np.empty((NH, 128, 8, 384), f)
    for h in range(NH):
        cat = np.concatenate([wq[:, h * 128:(h + 1) * 128], wk[:, h * 128:(h + 1) * 128],
                              wv[:, h * 128:(h + 1) * 128]], axis=1)
        w_h0[h] = cat.reshape(8, 128, 384).transpose(1, 0, 2)
    pw = np.asarray(inp["pool_w_in"], f)[0]

    def pk(w, k):
        n = w.shape[1]
        return np.ascontiguousarray(w.reshape(k, 128, n).transpose(1, 0, 2)).reshape(128, k * n)

    w_qm = np.stack([pk(aw[:, 4608:5120], 8), pk(pw[:, 1536:2048], 8)])
    w_gt = np.stack([pk(aw[:, 5120:7168], 8), pk(pw[:, 2048:4096], 8)])
    wo = np.asarray(inp["w_out"], f)
    w_o = np.stack([pk(wo[0], 16), pk(wo[1], 16)])
    kv = np.asarray(inp["mem_w_kv"], f)
    w_kv = np.stack([pk(kv[0], 8), pk(kv[1], 8)])
    w_u = np.stack([pk(pw[:, ft * 128:(ft + 1) * 128], 8) for ft in range(12)])
    gp = np.asarray(inp["pool_w_group"], f)[0]
    w_gp = np.stack([pk(gp[g], 3) for g in range(4)])
    return dict(w_h0=np.ascontiguousarray(w_h0).reshape(NH, 128, 8 * 384), w_qm=w_qm, w_gt=w_gt, w_o=w_o,
                w_kv=w_kv, w_u=w_u, w_gp=w_gp)


_NC_CACHE = {}


def _get_nc(do_l0, do_l1):
    key = (do_l0, do_l1)
    if key not in _NC_CACHE:
        lam0 = 0.8 - 0.6 * math.exp(-0.3 * 0)
        _NC_CACHE[key] = Builder(do_l0, do_l1, lam0).build()
    return _NC_CACHE[key]


def _in_maps(inp, xsrc, full_tokens):
    f = np.float32
    wts = _prep_weights(inp)
    x = np.asarray(xsrc, f)
    mem = np.asarray(inp["mem"], f)
    pos = np.asarray(inp["positions"], np.int32)
    maps = []
    for core in range(8):
        b, j = core // 2, core % 2
        own, ctx = _own_ctx_blocks(j)
        rows_own = _blocks_to_rows(own)
        rows_all = np.concatenate([rows_own, _blocks_to_rows(ctx)])
        ident, rmat, tri, invf, bias = _consts(j)
        m = dict(wts)
        if full_tokens:
            m["xl"] = np.ascontiguousarray(x[b][rows_all])
        else:
            m["xl"] = np.ascontiguousarray(x[core])
        m["posl"] = np.ascontiguousarray(pos[b][rows_all]).reshape(1, TOK_ALL)
        m["meml"] = np.ascontiguousarray(mem[b])
        m["c_ident"] = ident
        m["c_rmat"] = rmat
        m["c_tri"] = tri
        m["c_invf"] = invf
        m["c_bias"] = bias
        m["v_lng"] = np.asarray(inp["ln_g"], f)
        m["v_memg"] = np.asarray(inp["mem_norm_g"], f).reshape(1, D)
        m["v_fing"] = np.asarray(inp["final_g"], f).reshape(1, D)
        m["v_subg"] = np.asarray(inp["attn_subln_g"], f).reshape(1, 128)
        m["v_pscale"] = np.asarray(inp["pool_scale"], f).reshape(1, MIX)
        m["v_lam"] = np.asarray(inp["attn_lambda"], f).reshape(1, 256)
        maps.append(m)
    return maps


def _assemble(res):
    out = np.empty((4, TOK_ALL, D), np.float32)
    for core in range(8):
        b, j = core // 2, core % 2
        own, _ = _own_ctx_blocks(j)
        o = res[core]["out"]
        for li, gb in enumerate(own):
            if j == 0 and li == 8:
                continue
            if j == 1 and li == 0:
                continue
            out[b, gb * 128:(gb + 1) * 128] = o[li * 128:(li + 1) * 128]
    return out


FUSED = True


def kernel(**inputs):
    if FUSED:
        nc = _get_nc(True, True)
        res = run_bass_kernel_spmd(nc, _in_maps(inputs, inputs["x"], True), core_ids=list(range(8)))
        return _assemble(res.results)
    nc0 = _get_nc(True, False)
    r0 = run_bass_kernel_spmd(nc0, _in_maps(inputs, inputs["x"], True), core_ids=list(range(8)))
    h1 = [r0.results[c]["out"] for c in range(8)]
    nc1 = _get_nc(False, True)
    r1 = run_bass_kernel_spmd(nc1, _in_maps(inputs, h1, False), core_ids=list(range(8)))
    return _assemble(r1.results)
```
